# Optimizing a Trainium2 kernel written in Bass

```python
import jax, jax.numpy as jnp
from jax import lax
import numpy as np

D_MODEL = 1024
BATCH = 8
SEQ = 4096
DEPTH = 1

CHUNK = 64
N_META = 16
NORM_EPS = 1e-5
D_FF = 2816

RWKV_HEAD = 64
RWKV_HEADS = 8
RWKV_WIDTH = RWKV_HEADS * RWKV_HEAD
LORA_W = 32
LORA_A = 32
LORA_G = 96
GN_EPS = 64e-5

ATTN_HEAD = 64
ATTN_HEADS = 8
ATTN_KV_HEADS = 2
ATTN_GROUP = ATTN_HEADS // ATTN_KV_HEADS
ATTN_WIDTH = ATTN_HEADS * ATTN_HEAD
WINDOW = 128
WIN_CHUNKS = WINDOW // CHUNK
NEG_INF = -1e30

MIX_WIDTH = RWKV_WIDTH + ATTN_WIDTH
RWKV_COLS = 3 * RWKV_WIDTH + LORA_W + LORA_A + LORA_G
ATTN_COLS = (ATTN_HEADS + 2 * ATTN_KV_HEADS) * ATTN_HEAD
IN_COLS = RWKV_COLS + ATTN_COLS

kernel_name = "hymba_rwkv7_swa_sink_macaron"


def rmsnorm(x, g):
    xf = x.astype(jnp.float32)
    y = xf * lax.rsqrt(jnp.mean(xf * xf, axis=-1, keepdims=True) + NORM_EPS)
    return (y * g.astype(jnp.float32)).astype(x.dtype)


def swiglu(x, w_gate, w_up, w_down):
    return (jax.nn.silu(x @ w_gate) * (x @ w_up)) @ w_down


def token_shift(z):
    return jnp.pad(z, ((0, 0), (1, 0), (0, 0)))[:, :-1]


def alibi_slopes(n_heads):
    return jnp.exp2(-8.0 * (jnp.arange(n_heads, dtype=jnp.float32) + 1.0) / n_heads)


def rwkv7_mix(z, mu, w0, w2, a0, a2, g2, k_k, k_a, r_k, ln_w, ln_b):
    f32 = jnp.float32
    B, L, _ = z.shape
    zf = z.astype(f32)
    zf = zf + (token_shift(zf) - zf) * mu.astype(f32)
    s1 = RWKV_WIDTH
    s2 = 2 * RWKV_WIDTH
    s3 = 3 * RWKV_WIDTH
    s4 = s3 + LORA_W
    s5 = s4 + LORA_A
    zr, zk, zv, zw, za, zg = jnp.split(zf, [s1, s2, s3, s4, s5], axis=-1)
    w_log = -jax.nn.softplus(-(w0.astype(f32) + jnp.tanh(zw) @ w2.astype(f32))) - 0.5
    decay = jnp.exp(-jnp.exp(w_log))
    a = jax.nn.sigmoid(a0.astype(f32) + za @ a2.astype(f32))
    g = jax.nn.sigmoid(zg) @ g2.astype(f32)
    hs = lambda t: t.reshape(B, L, RWKV_HEADS, RWKV_HEAD)
    kk = hs(zk * k_k.astype(f32))
    kk = kk / jnp.maximum(jnp.sqrt(jnp.sum(kk * kk, axis=-1, keepdims=True)), 1e-12)
    k = zk * (1.0 + (a - 1.0) * k_a.astype(f32))
    r_h, k_h, v_h, a_h, w_h = hs(zr), hs(k), hs(zv), hs(a), hs(decay)
    tm = lambda t: jnp.transpose(t, (1, 0, 2, 3))

    def step(S, inp):
        r_t, w_t, k_t, v_t, kk_t, a_t = inp
        sa = jnp.einsum('bhvk,bhk->bhv', S, -kk_t)
        S = (S * w_t[:, :, None, :]
             + sa[..., None] * (kk_t * a_t)[:, :, None, :]
             + v_t[..., None] * k_t[:, :, None, :])
        y_t = jnp.einsum('bhvk,bhk->bhv', S, r_t)
        return S, y_t

    S0 = jnp.zeros((B, RWKV_HEADS, RWKV_HEAD, RWKV_HEAD), f32)
    _, y = lax.scan(step, S0, (tm(r_h), tm(w_h), tm(k_h), tm(v_h), tm(kk), tm(a_h)))
    y = jnp.transpose(y, (1, 0, 2, 3))
    mean = jnp.mean(y, axis=-1, keepdims=True)
    var = jnp.mean(jnp.square(y - mean), axis=-1, keepdims=True)
    y = ((y - mean) * lax.rsqrt(var + GN_EPS)).reshape(B, L, RWKV_WIDTH)
    y = y * ln_w.astype(f32) + ln_b.astype(f32)
    bonus = jnp.sum(r_h * k_h * r_k.astype(f32), axis=-1, keepdims=True) * v_h
    out = (y + bonus.reshape(B, L, RWKV_WIDTH)) * g
    return out.astype(z.dtype)


def swa_sink_attention(q, k, v, sinks):
    f32 = jnp.float32
    B, L = q.shape[0], q.shape[1]
    n_real = L - N_META
    nc = n_real // CHUNK
    scale = ATTN_HEAD ** -0.5
    slopes = alibi_slopes(ATTN_HEADS).reshape(ATTN_KV_HEADS, ATTN_GROUP)
    sink = sinks.astype(f32).reshape(ATTN_KV_HEADS, ATTN_GROUP)
    q = q.reshape(B, L, ATTN_KV_HEADS, ATTN_GROUP, ATTN_HEAD)
    qm = q[:, :N_META]
    qr = q[:, N_META:].reshape(B, nc, CHUNK, ATTN_KV_HEADS, ATTN_GROUP, ATTN_HEAD)
    km, vm = k[:, :N_META], v[:, :N_META]
    kr = k[:, N_META:].reshape(B, nc, CHUNK, ATTN_KV_HEADS, ATTN_HEAD)
    vr = v[:, N_META:].reshape(B, nc, CHUNK, ATTN_KV_HEADS, ATTN_HEAD)
    pad = ((0, 0), (WIN_CHUNKS, 0), (0, 0), (0, 0), (0, 0))
    kp, vp = jnp.pad(kr, pad), jnp.pad(vr, pad)
    kw = jnp.concatenate([kp[:, i:i + nc] for i in range(WIN_CHUNKS + 1)], axis=2)
    vw = jnp.concatenate([vp[:, i:i + nc] for i in range(WIN_CHUNKS + 1)], axis=2)
    n_wk = (WIN_CHUNKS + 1) * CHUNK
    c = jnp.arange(nc)
    q_pos = N_META + c[:, None] * CHUNK + jnp.arange(CHUNK)[None, :]
    k_pos = N_META + (c[:, None] - WIN_CHUNKS) * CHUNK + jnp.arange(n_wk)[None, :]
    k_valid = k_pos >= N_META
    m_pos = jnp.arange(N_META)
    dist_w = jnp.abs(q_pos[:, :, None] - k_pos[:, None, :]).astype(f32)
    dist_m = jnp.abs(q_pos[:, :, None] - m_pos[None, None, :]).astype(f32)
    sl5 = slopes[:, :, None, None, None]
    s_w = jnp.einsum('bcqkgd,bcskd->bkgcqs', qr, kw).astype(f32) * scale - sl5 * dist_w
    s_w = jnp.where(k_valid[:, None, :], s_w, NEG_INF)
    s_m = jnp.einsum('bcqkgd,bmkd->bkgcqm', qr, km).astype(f32) * scale - sl5 * dist_m
    s_sink = jnp.broadcast_to(sink[:, :, None, None, None], s_m.shape[:-1] + (1,))
    p = jax.nn.softmax(jnp.concatenate([s_sink, s_m, s_w], axis=-1), axis=-1)
    p_m = p[..., 1:1 + N_META].astype(v.dtype)
    p_w = p[..., 1 + N_META:].astype(v.dtype)
    o_r = (jnp.einsum('bkgcqm,bmkd->bcqkgd', p_m, vm)
           + jnp.einsum('bkgcqs,bcskd->bcqkgd', p_w, vw)).reshape(B, n_real, ATTN_WIDTH)
    dist_mm = jnp.abs(m_pos[:, None] - m_pos[None, :]).astype(f32)
    s_mm = jnp.einsum('bqkgd,bmkd->bkgqm', qm, km).astype(f32) * scale - slopes[:, :, None, None] * dist_mm
    s_msink = jnp.broadcast_to(sink[:, :, None, None], s_mm.shape[:-1] + (1,))
    pm = jax.nn.softmax(jnp.concatenate([s_msink, s_mm], axis=-1), axis=-1)[..., 1:].astype(v.dtype)
    o_m = jnp.einsum('bkgqm,bmkd->bqkgd', pm, vm).reshape(B, N_META, ATTN_WIDTH)
    return jnp.concatenate([o_m, o_r], axis=1)


def setup_inputs(seed: int = 0) -> dict:
    key = jax.random.key(seed)
    ks = iter(jax.random.split(key, 40))
    f32 = jnp.float32

    def nrm(shape, scale):
        return jax.random.normal(next(ks), shape, f32) * scale

    def gain(shape):
        return 1.0 + 0.02 * jax.random.normal(next(ks), shape, f32)

    Ld = DEPTH
    return {
        "x": nrm((BATCH, SEQ, D_MODEL), 1.0),
        "meta_tokens": nrm((N_META, D_MODEL), 1.0),
        "ffn1_norm": gain((Ld, D_MODEL)),
        "ffn1_w_gate": nrm((Ld, D_MODEL, D_FF), D_MODEL ** -0.5),
        "ffn1_w_up": nrm((Ld, D_MODEL, D_FF), D_MODEL ** -0.5),
        "ffn1_w_down": nrm((Ld, D_FF, D_MODEL), D_FF ** -0.5),
        "mix_norm": gain((Ld, D_MODEL)),
        "w_in": nrm((Ld, D_MODEL, IN_COLS), D_MODEL ** -0.5),
        "b_attn": nrm((Ld, ATTN_COLS), 0.02),
        "rwkv_mu": jax.random.uniform(next(ks), (Ld, RWKV_COLS), f32),
        "rwkv_w0": jax.random.uniform(next(ks), (Ld, RWKV_WIDTH), f32, -6.0, 1.0),
        "rwkv_w2": nrm((Ld, LORA_W, RWKV_WIDTH), 0.1),
        "rwkv_a0": nrm((Ld, RWKV_WIDTH), 0.1),
        "rwkv_a2": nrm((Ld, LORA_A, RWKV_WIDTH), 0.1),
        "rwkv_g2": nrm((Ld, LORA_G, RWKV_WIDTH), LORA_G ** -0.5),
        "rwkv_k_k": 0.85 + 0.02 * jax.random.normal(next(ks), (Ld, RWKV_WIDTH), f32),
        "rwkv_k_a": gain((Ld, RWKV_WIDTH)),
        "rwkv_r_k": nrm((Ld, RWKV_HEADS, RWKV_HEAD), 0.1),
        "rwkv_ln_w": gain((Ld, RWKV_WIDTH)),
        "rwkv_ln_b": nrm((Ld, RWKV_WIDTH), 0.02),
        "attn_sinks": nrm((Ld, ATTN_HEADS), 0.5),
        "w_out": nrm((Ld, MIX_WIDTH, D_MODEL), MIX_WIDTH ** -0.5),
        "ffn2_norm": gain((Ld, D_MODEL)),
        "ffn2_w_gate": nrm((Ld, D_MODEL, D_FF), D_MODEL ** -0.5),
        "ffn2_w_up": nrm((Ld, D_MODEL, D_FF), D_MODEL ** -0.5),
        "ffn2_w_down": nrm((Ld, D_FF, D_MODEL), D_FF ** -0.5),
        "final_norm": gain((D_MODEL,)),
    }


def reference(x, meta_tokens, ffn1_norm, ffn1_w_gate, ffn1_w_up, ffn1_w_down, mix_norm, w_in, b_attn,
              rwkv_mu, rwkv_w0, rwkv_w2, rwkv_a0, rwkv_a2, rwkv_g2, rwkv_k_k, rwkv_k_a, rwkv_r_k,
              rwkv_ln_w, rwkv_ln_b, attn_sinks, w_out, ffn2_norm, ffn2_w_gate, ffn2_w_up, ffn2_w_down,
              final_norm):
    B = x.shape[0]
    meta = jnp.broadcast_to(meta_tokens[None].astype(x.dtype), (B, N_META, D_MODEL))
    h = jnp.concatenate([meta, x], axis=1)
    L = h.shape[1]
    for l in range(DEPTH):
        h = h + 0.5 * swiglu(rmsnorm(h, ffn1_norm[l]), ffn1_w_gate[l], ffn1_w_up[l], ffn1_w_down[l])
        z = rmsnorm(h, mix_norm[l]) @ w_in[l]
        z_rwkv = z[..., :RWKV_COLS]
        z_attn = z[..., RWKV_COLS:] + b_attn[l]
        y_rwkv = rwkv7_mix(z_rwkv, rwkv_mu[l], rwkv_w0[l], rwkv_w2[l], rwkv_a0[l], rwkv_a2[l], rwkv_g2[l],
                           rwkv_k_k[l], rwkv_k_a[l], rwkv_r_k[l], rwkv_ln_w[l], rwkv_ln_b[l])
        kv_w = ATTN_KV_HEADS * ATTN_HEAD
        q = z_attn[..., :ATTN_WIDTH].reshape(B, L, ATTN_HEADS, ATTN_HEAD)
        k = z_attn[..., ATTN_WIDTH:ATTN_WIDTH + kv_w].reshape(B, L, ATTN_KV_HEADS, ATTN_HEAD)
        v = z_attn[..., ATTN_WIDTH + kv_w:].reshape(B, L, ATTN_KV_HEADS, ATTN_HEAD)
        y_attn = swa_sink_attention(q, k, v, attn_sinks[l])
        h = h + jnp.concatenate([y_rwkv, y_attn], axis=-1) @ w_out[l]
        h = h + 0.5 * swiglu(rmsnorm(h, ffn2_norm[l]), ffn2_w_gate[l], ffn2_w_up[l], ffn2_w_down[l])
    return rmsnorm(h, final_norm)[:, N_META:]
```

```python
import numpy as np
from contextlib import ExitStack
import concourse.bass as bass
import concourse.mybir as mybir
from concourse.bass_utils import run_bass_kernel_spmd

F32 = mybir.dt.float32
BF16 = mybir.dt.bfloat16
AF = mybir.ActivationFunctionType
ALU = mybir.AluOpType

NM = 16
DM = 1024
FF = 2816
NFC = 22
DEC = 0.6065306597126334


import os
STRICT = os.environ.get("K_STRICT", "1") == "1"
DYN = os.environ.get("K_DYN", "0") == "1"
FRAC = os.environ.get("K_FRAC", "0") == "1"
NORM_GAP = int(os.environ.get("K_NGAP", "3"))
FW = [float(v) for v in os.environ.get("K_FW", "1,1,1,1").split(",")]
SLACK = float(os.environ.get("K_SLACK", "0.0"))
WGT = [int(v) for v in os.environ.get("K_WGT", "1,2,1,1,1").split(",")]


class Sched:
    STREAMS = ("sync", "scalar", "vector", "gpsimd", "tensor")

    def __init__(self, nc):
        self.nc = nc
        self.prog = {}
        self.groups = {}
        self.w = {}
        self.r = {}
        self.waited = {n: {} for n in self.STREAMS}
        self.nops = 0
        self.tfree = {n: 0.0 for n in self.STREAMS}
        self.fin = {}
        self.step_fin = 0.0
        self.cost = {"sync": 2.0, "scalar": 0.55, "vector": 0.5, "gpsimd": 2.0, "tensor": 0.08}
        self.lat = 1.2

    def add_prog(self, name, sem, inc, inorder):
        self.prog[name] = dict(sem=sem, count=0, inc=inc, inorder=inorder)

    def add_group(self, name, sems):
        subs = []
        for i, s in enumerate(sems):
            sub = f"{name}_{i}"
            self.add_prog(sub, s, 16, False)
            subs.append(sub)
        self.groups[name] = dict(subs=subs, i=0)

    def op(self, stream, prog, fn, reads=(), writes=(), cost=None):
        if getattr(self, "dry", False):
            return
        deps = {}
        if prog in self.groups:
            g = self.groups[prog]
            prog = g["subs"][g["i"] % len(g["subs"])]
            g["i"] += 1
            if self.prog[prog]["count"] > 0:
                deps[prog] = self.prog[prog]["count"]
        inorder = self.prog[prog]["inorder"] and not (STRICT and prog != "pe")
        for k in reads:
            w = self.w.get(k)
            if w is not None and not (w[0] == prog and prog == "pe"):
                deps[w[0]] = max(deps.get(w[0], 0), w[1])
        for k in writes:
            w = self.w.get(k)
            if w is not None and not (inorder and w[0] == prog):
                deps[w[0]] = max(deps.get(w[0], 0), w[1])
            for (p, n) in self.r.get(k, ()):
                if inorder and p == prog:
                    continue
                deps[p] = max(deps.get(p, 0), n)
        eng = getattr(self.nc, stream)
        wd = self.waited[stream]
        t0 = self.tfree[stream]
        for p, n in deps.items():
            f_ = self.fin.get((p, n), 0.0) + (0.0 if p == prog else self.lat)
            if f_ > t0:
                t0 = f_
            if wd.get(p, 0) < n:
                wd[p] = n
                eng.wait_ge(self.prog[p]["sem"], n)
        pr = self.prog[prog]
        pr["count"] += pr["inc"]
        cnt = pr["count"]
        t1 = t0 + (cost if cost is not None else self.cost[stream])
        self.tfree[stream] = t1 if pr["inorder"] else t0 + 0.1
        self.fin[(prog, cnt)] = t1
        if t1 > self.step_fin:
            self.step_fin = t1
        fn(eng).then_inc(pr["sem"], pr["inc"])
        for k in writes:
            self.w[k] = (prog, cnt)
            self.r[k] = []
        for k in reads:
            lst = [x for x in self.r.get(k, []) if x[0] != prog]
            lst.append((prog, cnt))
            self.r[k] = lst
        self.nops += 1

    def barrier(self, streams=None):
        waits_all = [(n, pr["sem"], pr["count"]) for n, pr in self.prog.items() if pr["count"] > 0]
        for st in (streams or self.STREAMS):
            wd = self.waited[st]
            eng = getattr(self.nc, st)
            for (n, s, c) in waits_all:
                if wd.get(n, 0) < c:
                    wd[n] = c
                    eng.wait_ge(s, c)


def make_consts():
    c = {}
    c["ident"] = np.eye(128, dtype=np.float32)
    s = np.arange(64)[:, None]
    t = np.arange(64)[None, :]
    rep8 = lambda m: np.ascontiguousarray(np.tile(m.astype(np.float32)[:, None, :], (1, 8, 1)).reshape(m.shape[0], -1))
    c["msu"] = (t > s).astype(np.float32)
    c["msl"] = (t < s).astype(np.float32)
    c["mui"] = (t >= s).astype(np.float32)
    c["idt"] = (t == s).astype(np.float32)
    slopes = 2.0 ** (-(np.arange(8) + 1.0))
    j = np.arange(64)[:, None].astype(np.float64)
    i = np.arange(64)[None, :].astype(np.float64)

    def alibi(dist):
        return np.ascontiguousarray(
            (-slopes[None, :, None] * dist[:, None, :]).astype(np.float32).reshape(dist.shape[0], -1))

    c["al2"] = alibi(128 + i - j)
    c["al1"] = alibi(64 + i - j)
    c["al0"] = alibi(np.abs(i - j))
    m = np.arange(16)[:, None].astype(np.float64)
    c["alm0"] = alibi(16 + i - m)
    c["almd"] = alibi(np.full((16, 64), 64.0))
    i16 = np.arange(16)[None, :].astype(np.float64)
    c["almm"] = alibi(np.abs(i16 - m))
    return c


CONST_SHAPES = dict(ident=[128, 128], msu=[64, 64], msl=[64, 64], mui=[64, 64], idt=[64, 64],
                    al2=[64, 512], al1=[64, 512], al0=[64, 512], alm0=[16, 512], almd=[16, 512],
                    almm=[16, 128])
NPP = 90


def build_nc(NR=4096, phases="ABOC", dbg=False):
    nc = bass.Bass("TRN2", target_bir_lowering=False)
    LT = NM + NR
    S = Sched(nc)

    def din(name, shape):
        return nc.dram_tensor(name, list(shape), F32, kind="ExternalInput").ap()

    x_d = din("x", [NR, DM])
    meta_d = din("meta", [NM, DM])
    gvec_d = din("gvec", [4, DM])
    wg_d = [din("wg1", [DM, FF]), din("wg2", [DM, FF])]
    wu_d = [din("wu1", [DM, FF]), din("wu2", [DM, FF])]
    wd_d = [din("wd1", [FF, DM]), din("wd2", [FF, DM])]
    win_d = din("w_in", [DM, 2464])
    wout_d = din("w_out", [DM, DM])
    w2_d = din("w2", [32, 512])
    a2_d = din("a2", [32, 512])
    g2_d = din("g2", [96, 512])
    pp_d = din("pp", [64, NPP])
    pl_d = din("pl", [96, 3])
    bv_d = din("bv", [1, 128])
    sinks_d = din("sinks", [1, 8])
    cst = {k: din("c_" + k, shp) for k, shp in CONST_SHAPES.items()}
    out_d = nc.dram_tensor("out", [NR, DM], F32, kind="ExternalOutput").ap()
    h1_d = nc.dram_tensor("h1s", [LT, DM], F32).ap()
    h2_d = nc.dram_tensor("h2s", [LT, DM], F32).ap()
    ym_d = nc.dram_tensor("yms", [DM, LT], BF16).ap()

    def OPf(stream, prog):
        def f(fn, r=(), w=()):
            S.op(stream, prog, fn, r, w)
        return f

    PE = OPf("tensor", "pe")
    ACT = OPf("scalar", "act")
    DVE = OPf("vector", "dve")
    POOL = OPf("gpsimd", "pool")
    LD = OPf("sync", "dq0")
    ST = OPf("sync", "dq2")
    LDC = OPf("gpsimd", "dq1")

    with ExitStack() as top:
        sems = {n: top.enter_context(nc.semaphore(n)) for n in ("pe", "act", "dve", "pool")}
        for n in ("pe", "act", "dve", "pool"):
            S.add_prog(n, sems[n], 1, True)
        for n, K_ in (("dq0", 8), ("dq1", 12), ("dq2", 6)):
            S.add_group(n, [top.enter_context(nc.semaphore(f"{n}_{i}")) for i in range(K_)])

        pp_t = [top.enter_context(nc.psum_tensor(f"pp{i}", [128, 1024], F32)) for i in range(3)]
        p_single = top.enter_context(nc.psum_tensor("psg", [128, 512], F32))
        PSB = top.enter_context(nc.psum_tensor("psb", [128, 1024], BF16))
        banks = []
        for i in range(3):
            banks.append((pp_t[i], 0, f"ps{2 * i}"))
            banks.append((pp_t[i], 512, f"ps{2 * i + 1}"))
        banks.append((p_single, 0, "ps6"))
        rot = {"s": 0, "p": 0}

        rot["m"] = 0
        rot["pipe"] = False

        def ps1(pool="m"):
            if not rot["pipe"]:
                t, off, k = banks[rot["s"] % 7]
                rot["s"] += 1
            elif pool == "s":
                t, off, k = banks[5 + rot["s"] % 2]
                rot["s"] += 1
            else:
                t, off, k = banks[rot["m"] % 5]
                rot["m"] += 1
            return t[:, off:off + 512], k

        def ps2():
            i = rot["p"] % 3
            rot["p"] += 1
            return pp_t[i][:, :], [f"ps{2 * i}", f"ps{2 * i + 1}"]

        def sbt(es, name, shape, dt=F32):
            return es.enter_context(nc.sbuf_tensor(name, list(shape), dt))

        IDF = sbt(top, "IDF", [128, 128])
        IDB = sbt(top, "IDB", [128, 128], BF16)
        LD(lambda e: e.dma_start(out=IDF[:], in_=cst["ident"]), w=["IDF"])
        DVE(lambda e: e.tensor_copy(out=IDB[:], in_=IDF[:]), r=["IDF"], w=["IDB"])
        SS = sbt(top, "SS", [128, 8])
        RS = sbt(top, "RS", [128, 8])
        JUNK = sbt(top, "JUNK", [128, 1024], BF16)

        def rms_rstd(src_fn, P, nsub, rkey, wtag):
            DVE(lambda e: e.memset(SS[:P, 0:nsub], 0.0), w=["SS"])
            for s in range(nsub):
                ACT(lambda e, s=s: e.activation(out=JUNK[:P, :], in_=src_fn(s), func=AF.Square,
                                                accum_out=SS[:P, s:s + 1]), r=[rkey, "SS"], w=["JUNK", "SS"])
            DVE(lambda e: e.tensor_scalar(out=RS[:P, 0:nsub], in0=SS[:P, 0:nsub], scalar1=1.0 / DM, scalar2=1e-5,
                                          op0=ALU.mult, op1=ALU.add), r=["SS"], w=["RS"])
            ACT(lambda e: e.activation(out=RS[:P, 0:nsub], in_=RS[:P, 0:nsub], func=AF.Sqrt), r=["RS"], w=["RS"])
            DVE(lambda e: e.reciprocal(out=RS[:P, 0:nsub], in_=RS[:P, 0:nsub]), r=["RS"], w=["RS"])

        def norm_ew(src_fn, P, nsub, rkey, GB, NTK):
            rms_rstd(src_fn, P, nsub, rkey, None)
            for s in range(nsub):
                nb = s % 2
                DVE(lambda e, s=s, nb=nb: e.scalar_tensor_tensor(out=NTK[nb][:P, :], in0=src_fn(s), scalar=RS[:P, s:s + 1],
                                                                 in1=GB[:P, :], op0=ALU.mult, op1=ALU.mult),
                    r=[rkey, "RS", "GB"], w=[("NTK", nb)])

        def norm_tr(P, nsub, NTK, nT, nTkey):
            for s in range(nsub):
                nb = s % 2
                for c in range(8):
                    PE(lambda e, c=c, nb=nb: e.transpose(PSB[:, c * 128:c * 128 + P], NTK[nb][:P, c * 128:(c + 1) * 128],
                                                         IDB[:P, :P]), r=[("NTK", nb), "IDB"], w=["psb"])
                ACT(lambda e, s=s: e.copy(out=nT[:, :, s * 128:s * 128 + P],
                                          in_=PSB[:].rearrange("p (c t) -> p c t", t=128)[:, :, 0:P]),
                    r=["psb"], w=[nTkey])

        def norm_transpose(src_fn, P, nsub, rkey, GB, NTK, nT, nTkey):
            rms_rstd(src_fn, P, nsub, rkey, None)
            for s in range(nsub):
                nb = s % 2
                DVE(lambda e, s=s, nb=nb: e.scalar_tensor_tensor(out=NTK[nb][:P, :], in0=src_fn(s), scalar=RS[:P, s:s + 1],
                                                                 in1=GB[:P, :], op0=ALU.mult, op1=ALU.mult),
                    r=[rkey, "RS", "GB"], w=[("NTK", nb)])
                for c in range(8):
                    PE(lambda e, c=c, nb=nb: e.transpose(PSB[:, c * 128:c * 128 + P], NTK[nb][:P, c * 128:(c + 1) * 128],
                                                         IDB[:P, :P]), r=[("NTK", nb), "IDB"], w=["psb"])
                ACT(lambda e, s=s: e.copy(out=nT[:, :, s * 128:s * 128 + P],
                                          in_=PSB[:].rearrange("p (c t) -> p c t", t=128)[:, :, 0:P]),
                    r=["psb"], w=[nTkey])

        def ffn_phase(tag, tiles, src_fn, dst_fn, wi, gidx, final):
            with ExitStack() as es:
                Wg = sbt(es, "Wg" + tag, [128, 8, FF], BF16)
                Wu = sbt(es, "Wu" + tag, [128, 8, FF], BF16)
                Wd = sbt(es, "Wd" + tag, [128, NFC, DM], BF16)
                GB = sbt(es, "GB" + tag, [128, DM])
                GF = sbt(es, "GF" + tag, [128, DM]) if final else None
                XT = [sbt(es, f"XT{b}" + tag, [128, 2, DM]) for b in range(2)]
                NTK = [sbt(es, f"NTK{b}" + tag, [128, DM], BF16) for b in range(2)]
                nTs = [sbt(es, f"nT{b}" + tag, [128, 8, 256], BF16) for b in range(2)]
                actT = sbt(es, "actT" + tag, [128, NFC, 256], BF16)
                SG = [sbt(es, f"SG{b}" + tag, [128, 256]) for b in range(2)]
                LD(lambda e: e.dma_start(out=GB[:], in_=gvec_d[gidx:gidx + 1, :].partition_broadcast(128)), w=["GB"])
                if final:
                    LD(lambda e: e.dma_start(out=GF[:], in_=gvec_d[3:4, :].partition_broadcast(128)), w=["GF"])

                def load_tile(ti):
                    r0, T = tiles[ti]
                    P = min(T, 128)
                    nsub = (T + 127) // 128
                    b = ti % 2
                    LD(lambda e: e.dma_start(out=XT[b][:P, 0:nsub, :],
                                             in_=src_fn(r0, T).rearrange("(s p) d -> p s d", p=P)), w=[("XT", b)])

                load_tile(0)
                for c in range(8):
                    LDC(lambda e, c=c: e.dma_start(out=Wg[:, c, :], in_=wg_d[wi][c * 128:(c + 1) * 128, :]), w=[("Wg", c)])
                    LDC(lambda e, c=c: e.dma_start(out=Wu[:, c, :], in_=wu_d[wi][c * 128:(c + 1) * 128, :]), w=[("Wu", c)])
                for fc in range(NFC):
                    LDC(lambda e, fc=fc: e.dma_start(out=Wd[:, fc, :], in_=wd_d[wi][fc * 128:(fc + 1) * 128, :]), w=[("Wd", fc)])

                def tile_geo(ti):
                    r0_, T_ = tiles[ti]
                    return min(T_, 128), (T_ + 127) // 128

                P0_, ns0_ = tile_geo(0)
                norm_ew(lambda s: XT[0][:P0_, s, :], P0_, ns0_, ("XT", 0), GB, NTK)
                norm_tr(P0_, ns0_, NTK, nTs[0], ("nT", 0))
                for ti, (r0, T) in enumerate(tiles):
                    P = min(T, 128)
                    nsub = (T + 127) // 128
                    b = ti % 2
                    if ti + 1 < len(tiles):
                        load_tile(ti + 1)
                    xt = XT[b]
                    nT = nTs[b]
                    nTk = ("nT", b)
                    if ti + 1 < len(tiles):
                        Pn_, nsn_ = tile_geo(ti + 1)
                        xn_ = XT[1 - b]
                        norm_ew(lambda s, xn_=xn_, Pn_=Pn_: xn_[:Pn_, s, :], Pn_, nsn_, ("XT", 1 - b), GB, NTK)
                    for fc in range(NFC):
                        pg, kg = ps1()
                        pu, ku = ps1()
                        for c in range(8):
                            PE(lambda e, c=c, fc=fc, pg=pg: e.matmul(pg[:, 0:T], lhsT=Wg[:, c, fc * 128:(fc + 1) * 128],
                                                                     rhs=nT[:, c, 0:T], start=(c == 0), stop=(c == 7)),
                               r=[("Wg", c), nTk], w=[kg])
                        for c in range(8):
                            PE(lambda e, c=c, fc=fc, pu=pu: e.matmul(pu[:, 0:T], lhsT=Wu[:, c, fc * 128:(fc + 1) * 128],
                                                                     rhs=nT[:, c, 0:T], start=(c == 0), stop=(c == 7)),
                               r=[("Wu", c), nTk], w=[ku])
                        sb_ = fc % 2
                        ACT(lambda e, pg=pg, sb_=sb_: e.activation(out=SG[sb_][:, 0:T], in_=pg[:, 0:T], func=AF.Silu),
                            r=[kg], w=[("SG", sb_)])
                        DVE(lambda e, pu=pu, sb_=sb_, fc=fc: e.tensor_tensor(out=actT[:, fc, 0:T], in0=SG[sb_][:, 0:T],
                                                                              in1=pu[:, 0:T], op=ALU.mult),
                            r=[("SG", sb_), ku], w=[("actT", fc)])
                    if ti + 1 < len(tiles):
                        norm_tr(Pn_, nsn_, NTK, nTs[1 - b], ("nT", 1 - b))
                    for s in range(nsub):
                        for hf in range(2):
                            po, ko = ps1()
                            for fc in range(NFC):
                                PE(lambda e, fc=fc, s=s, hf=hf, po=po: e.matmul(
                                    po[:P, :], lhsT=actT[:, fc, s * 128:s * 128 + P], rhs=Wd[:, fc, hf * 512:(hf + 1) * 512],
                                    start=(fc == 0), stop=(fc == NFC - 1)), r=[("actT", fc), ("Wd", fc)], w=[ko])
                            DVE(lambda e, s=s, hf=hf, po=po: e.scalar_tensor_tensor(
                                out=xt[:P, s, hf * 512:(hf + 1) * 512], in0=po[:P, :], scalar=0.5,
                                in1=xt[:P, s, hf * 512:(hf + 1) * 512], op0=ALU.mult, op1=ALU.add),
                                r=[ko, ("XT", b)], w=[("XT", b)])
                    if final:
                        rms_rstd(lambda s: xt[:P, s, :], P, nsub, ("XT", b), None)
                        for s in range(nsub):
                            DVE(lambda e, s=s: e.scalar_tensor_tensor(out=xt[:P, s, :], in0=xt[:P, s, :], scalar=RS[:P, s:s + 1],
                                                                      in1=GF[:P, :], op0=ALU.mult, op1=ALU.mult),
                                r=[("XT", b), "RS", "GF"], w=[("XT", b)])
                    ST(lambda e, r0=r0, T=T, P=P, nsub=nsub, xt=xt: e.dma_start(
                        out=dst_fn(r0, T).rearrange("(s p) d -> p s d", p=P), in_=xt[:P, 0:nsub, :]), r=[("XT", b)], w=[("dst", tag, ti)])
                S.barrier()

        def mix_phase():
            with ExitStack() as es:
                WIN = sbt(es, "WIN", [128, 8, 2464], BF16)
                W2 = sbt(es, "W2", [32, 512])
                A2 = sbt(es, "A2", [32, 512])
                G2 = sbt(es, "G2", [96, 512])
                PPt = sbt(es, "PPt", [64, NPP])
                PL = sbt(es, "PL", [96, 3])
                BVB = sbt(es, "BVB", [64, 128])
                ESK = sbt(es, "ESK", [64, 8])
                GB = sbt(es, "GBm", [128, DM])
                CT = {k: sbt(es, "C_" + k, CONST_SHAPES[k]) for k in CONST_SHAPES if k != "ident"}
                ALMC = sbt(es, "ALMC", [16, 512])
                ONESF = sbt(es, "ONESF", [64, 64])
                ONESB = sbt(es, "ONESB", [64, 64], BF16)
                ONESDB = sbt(es, "ONESDB", [64, 64], BF16)
                T1b = sbt(es, "T1b", [64, 8, 64], BF16)
                YSb = sbt(es, "YSb", [64, 8, 64], BF16)
                Xb = [sbt(es, f"Xb{i}", [64, 512], BF16) for i in range(2)]
                Xtb = [sbt(es, f"Xtb{i}", [64, 512], BF16) for i in range(2)]
                Rtb = [sbt(es, f"Rtb{i}", [64, 512], BF16) for i in range(2)]
                HT = [sbt(es, f"HT{b}", [64, DM]) for b in range(2)]
                NTK = [sbt(es, f"NTKm{b}", [64, DM], BF16) for b in range(2)]
                nT = sbt(es, "nTm", [128, 8, 64], BF16)
                ZR = sbt(es, "ZR", [64, 8, 65])
                ZK = sbt(es, "ZK", [64, 8, 65])
                ZV = sbt(es, "ZV", [64, 8, 65])
                ZW = sbt(es, "ZW", [32, 65])
                ZA = sbt(es, "ZA", [32, 65])
                ZG = sbt(es, "ZG", [96, 65])
                CUM = sbt(es, "CUM", [64, 8, 65])
                QTs = [sbt(es, f"QT{i}", [64, 8, 64], BF16) for i in range(2)]
                KTR = sbt(es, "KTR", [64, 2, 4, 64], BF16)
                KTM = sbt(es, "KTMeta", [64, 2, 16], BF16)
                VR = sbt(es, "VR", [64, 4, 128], BF16)
                VM = sbt(es, "VMeta", [16, 128], BF16)
                YMIX = [sbt(es, f"YMIX{b}", [64, 16, 64], BF16) for b in range(2)]
                STT = sbt(es, "STATE", [64, 8, 64])
                names = ["Rr", "Kz", "Vv", "Gg", "Aa", "KKn", "Kk", "Winc", "Wexc", "Winv", "T1", "T2",
                         "Rt", "Kt", "At", "Bt", "BON", "YS"]
                names += ["S1", "S2", "T3"]
                DB_D = ("At", "Rt", "Winc", "BON", "Gg")
                DB_C = ("AAK", "VTMt", "ARB", "ARK", "KTMt", "BTMt", "R1")
                D0 = {n: sbt(es, "D_" + n, [64, 8, 64]) for n in names}
                Dp = [dict(D0), dict(D0)]
                for n in DB_D:
                    Dp[1][n] = sbt(es, "D1_" + n, [64, 8, 64])
                S1b = sbt(es, "S1b", [64, 8, 64], BF16)
                T3b = sbt(es, "T3b", [64, 8, 64], BF16)
                TW = sbt(es, "TW", [32, 64])
                ZAl = sbt(es, "ZAl", [32, 64])
                SGg = sbt(es, "SGg", [96, 64])
                TL = sbt(es, "TL", [96, 64])
                cn = ["KTMt", "BTMt", "VTMt", "AAK", "ARB", "ARK", "X0", "Xt0", "R0", "R1",
                      "RHS0", "UU"]
                Cc0 = {n: sbt(es, "Cc_" + n, [64, 512]) for n in cn}
                Ccp = [dict(Cc0), dict(Cc0)]
                for n in DB_C:
                    Ccp[1][n] = sbt(es, "Cc1_" + n, [64, 512])
                DBSET = set(DB_D) | set(DB_C) | {"QT"}

                def trw(f_, par):
                    def g(fn, r=(), w=()):
                        f_(fn, [((k, par) if (isinstance(k, str) and k in DBSET) else k) for k in r],
                           [((k, par) if (isinstance(k, str) and k in DBSET) else k) for k in w])
                    return g

                PE0, ACT0, DVE0, ST0 = PE, ACT, DVE, ST
                print("mix phase sbuf bytes remaining/partition:", nc.sbuf_bytes_remaining // 128 if nc.sbuf_bytes_remaining > 1 << 20 else nc.sbuf_bytes_remaining)
                SBT = [sbt(es, "SBT0", [64, 512])] * 2
                PT = [sbt(es, f"PT{i}", [64, 512], BF16) for i in range(4)]
                DEN = sbt(es, "DEN", [64, 512])

                for c in range(8):
                    LDC(lambda e, c=c: e.dma_start(out=WIN[:, c, :], in_=win_d[c * 128:(c + 1) * 128, :]), w=[("WIN", c)])
                LD(lambda e: e.dma_start(out=W2[:], in_=w2_d), w=["W2"])
                LD(lambda e: e.dma_start(out=A2[:], in_=a2_d), w=["A2"])
                LD(lambda e: e.dma_start(out=G2[:], in_=g2_d), w=["G2"])
                LD(lambda e: e.dma_start(out=PPt[:], in_=pp_d), w=["PP"])
                LD(lambda e: e.dma_start(out=PL[:], in_=pl_d), w=["PL"])
                LD(lambda e: e.dma_start(out=BVB[:], in_=bv_d.partition_broadcast(64)), w=["BVB"])
                LD(lambda e: e.dma_start(out=ESK[:], in_=sinks_d.partition_broadcast(64)), w=["ESK"])
                LD(lambda e: e.dma_start(out=GB[:], in_=gvec_d[1:2, :].partition_broadcast(128)), w=["GB"])
                for k in CT:
                    LD(lambda e, k=k: e.dma_start(out=CT[k][:], in_=cst[k]), w=[("CT", k)])
                ACT(lambda e: e.activation(out=ESK[:], in_=ESK[:], func=AF.Exp), r=["ESK"], w=["ESK"])
                DVE(lambda e: e.memset(ONESF[:], 1.0), w=["ONESF"])
                DVE(lambda e: e.memset(ONESB[:], 1.0), w=["ONESB"])
                DVE(lambda e: e.memset(ONESDB[:], 1.0 / 64.0), w=["ONESDB"])
                DVE(lambda e: e.memset(STT[:], 0.0), w=["STATE"])
                for Z, k in ((ZR, "ZR"), (ZK, "ZK"), (ZV, "ZV"), (CUM, "CUM")):
                    DVE(lambda e, Z=Z: e.memset(Z[:, :, 0:1], 0.0), w=[k])
                for Z, k in ((ZW, "ZW"), (ZA, "ZA"), (ZG, "ZG")):
                    DVE(lambda e, Z=Z: e.memset(Z[:, 0:1], 0.0), w=[k])

                MU_R, MU_K, MU_V, W0c, A0c, KKc, KAc, RKc, LNWc, LNBc, BQc, BKc = 0, 8, 16, 24, 32, 40, 48, 56, 64, 72, 80, 88

                def v3(t, rows, cols):
                    a = t if isinstance(t, bass.AP) else t[:]
                    return a.rearrange("p (h t) -> p h t", t=64)[0:rows, :, 0:cols]

                def m3(mask, rows, cols):
                    return CT[mask][0:rows, 0:cols].unsqueeze(1).to_broadcast([rows, 8, cols])

                tiles = [(0, NM)] + [(NM + 64 * i, 64) for i in range(NR // 64)]

                def load_tile(ti):
                    r0, T = tiles[ti]
                    b = ti % 2
                    LD(lambda e: e.dma_start(out=HT[b][0:T, :], in_=h1_d[r0:r0 + T, :]), w=[("HT", b)])

                flags = {}

                def genP1(ti):
                    r0, T = tiles[ti]
                    b = ti % 2
                    par = ti % 2
                    ht = HT[b]
                    ym = YMIX[b]
                    ymk = ("YMIX", b)
                    is_meta = (ti == 0)
                    Ck = T
                    gc = ti - 1
                    D = Dp[par]
                    Cc = Ccp[par]
                    PE, ACT, DVE, ST = (trw(f_, par) for f_ in (PE0, ACT0, DVE0, ST0))
                    QT = QTs[par]
                    PSP = "m"
                    if ti + 1 < len(tiles):
                        load_tile(ti + 1)
                    norm_ew(lambda s: ht[0:T, :], T, 1, ("HT", b), GB, NTK)
                    for _ in range(NORM_GAP):
                        yield
                    norm_tr(T, 1, NTK, nT, "nT")
                    yield
                    def ppb(col):
                        return PPt[:, col:col + 8].unsqueeze(2).to_broadcast([64, 8, T])

                    def d(n):
                        return D[n][:, :, 0:T]

                    def p3(ps, rows=64):
                        return ps.rearrange("p (h t) -> p h t", t=64)[0:rows, :, 0:T]

                    def headsum(srct, skey, lhs, lkey):
                        ps, key = ps1(PSP)
                        if T == 64:
                            PE(lambda e, ps=ps: e.matmul(ps[0:64, :], lhsT=lhs[:, :], rhs=srct[:].rearrange("p h t -> p (h t)"), start=True, stop=True),
                               r=[skey, lkey], w=[key])
                        else:
                            for h in range(8):
                                PE(lambda e, ps=ps, h=h: e.matmul(ps[0:64, h * 64:h * 64 + T], lhsT=lhs[:, :], rhs=srct[:, h, 0:T], start=True, stop=True),
                                   r=[skey, lkey], w=[key])
                        return ps, key

                    def rsqrt_inplace(n):
                        ACT(lambda e: e.activation(out=d(n), in_=d(n), func=AF.Ln), r=[n], w=[n])
                        ACT(lambda e: e.activation(out=d(n), in_=d(n), func=AF.Exp, scale=-0.5), r=[n], w=[n])

                    while ti > 0 and not flags.get(("lerp", ti - 1)) and not getattr(S, "dry", False):
                        yield
                    def proj_heads(col0, nheads, ps, key):
                        for h in range(nheads):
                            for c in range(8):
                                PE(lambda e, h=h, c=c: e.matmul(ps[0:64, h * 64:h * 64 + T],
                                                                lhsT=WIN[:, c, col0 + h * 64:col0 + (h + 1) * 64],
                                                                rhs=nT[:, c, 0:T], start=(c == 0), stop=(c == 7)),
                                   r=[("WIN", c), "nT"], w=[key])

                    for (Z, zk, col0) in ((ZR, "ZR", 0), (ZK, "ZK", 512), (ZV, "ZV", 1024)):
                        ps, key = ps1()
                        proj_heads(col0, 8, ps, key)
                        ACT(lambda e, Z=Z, ps=ps: e.copy(out=Z[:, :, 1:1 + T], in_=p3(ps)), r=[key], w=[zk])
                    ps, key = ps1()
                    proj_heads(1696, 8, ps, key)
                    DVE(lambda e, ps=ps: e.tensor_tensor(out=QT[:, :, 0:T], in0=p3(ps), in1=ppb(BQc), op=ALU.add),
                        r=[key, "PP"], w=["QT"])
                    yield
                    ps, key = ps1()
                    for si, (col0, width) in enumerate(((1536, 32), (1568, 32), (1600, 96))):
                        for c in range(8):
                            PE(lambda e, si=si, c=c, col0=col0, width=width, ps=ps: e.matmul(
                                ps[0:width, si * 64:si * 64 + T], lhsT=WIN[:, c, col0:col0 + width], rhs=nT[:, c, 0:T],
                                start=(c == 0), stop=(c == 7)), r=[("WIN", c), "nT"], w=[key])
                    ACT(lambda e, ps=ps: e.copy(out=ZW[:, 1:1 + T], in_=ps[0:32, 0:T]), r=[key], w=["ZW"])
                    ACT(lambda e, ps=ps: e.copy(out=ZA[:, 1:1 + T], in_=ps[0:32, 64:64 + T]), r=[key], w=["ZA"])
                    ACT(lambda e, ps=ps: e.copy(out=ZG[:, 1:1 + T], in_=ps[0:96, 128:128 + T]), r=[key], w=["ZG"])
                    ps, key = ps1()
                    proj_heads(2208, 2, ps, key)
                    for kv in range(2):
                        if is_meta:
                            ACT(lambda e, kv=kv, ps=ps: e.activation(out=KTM[:, kv, :], in_=ps[0:64, kv * 64:kv * 64 + T],
                                                                     func=AF.Identity, bias=PPt[:, BKc + kv:BKc + kv + 1]),
                                r=[key, "PP"], w=["KTMeta"])
                        else:
                            ACT(lambda e, kv=kv, ps=ps: e.activation(
                                out=KTR[:, kv, gc % 4, :], in_=ps[0:64, kv * 64:kv * 64 + 64],
                                func=AF.Identity, bias=PPt[:, BKc + kv:BKc + kv + 1]),
                                r=[key, "PP"], w=[("KTR", gc % 4)])
                    yield
                    pv, kvk = ps1()
                    for c in range(8):
                        PE(lambda e, c=c, pv=pv: e.matmul(pv[0:T, 0:128], lhsT=nT[:, c, 0:T],
                                                          rhs=WIN[:, c, 2336:2464], start=(c == 0), stop=(c == 7)),
                           r=[("WIN", c), "nT"], w=[kvk])
                    if is_meta:
                        DVE(lambda e, pv=pv: e.tensor_tensor(out=VM[:, :], in0=pv[0:16, 0:128], in1=BVB[0:16, :], op=ALU.add),
                            r=[kvk, "BVB"], w=["VMeta"])
                    else:
                        DVE(lambda e, pv=pv: e.tensor_tensor(out=VR[:, gc % 4, :], in0=pv[0:64, 0:128], in1=BVB[:, :],
                                                             op=ALU.add), r=[kvk, "BVB"], w=[("VR", gc % 4)])


                    yield
                def genP2(ti):
                    r0, T = tiles[ti]
                    b = ti % 2
                    par = ti % 2
                    ht = HT[b]
                    ym = YMIX[b]
                    ymk = ("YMIX", b)
                    is_meta = (ti == 0)
                    Ck = T
                    gc = ti - 1
                    D = Dp[par]
                    Cc = Ccp[par]
                    PE, ACT, DVE, ST = (trw(f_, par) for f_ in (PE0, ACT0, DVE0, ST0))
                    QT = QTs[par]
                    PSP = "m"
                    yield
                    def ppb(col):
                        return PPt[:, col:col + 8].unsqueeze(2).to_broadcast([64, 8, T])

                    def d(n):
                        return D[n][:, :, 0:T]

                    def p3(ps, rows=64):
                        return ps.rearrange("p (h t) -> p h t", t=64)[0:rows, :, 0:T]

                    def headsum(srct, skey, lhs, lkey):
                        ps, key = ps1(PSP)
                        if T == 64:
                            PE(lambda e, ps=ps: e.matmul(ps[0:64, :], lhsT=lhs[:, :], rhs=srct[:].rearrange("p h t -> p (h t)"), start=True, stop=True),
                               r=[skey, lkey], w=[key])
                        else:
                            for h in range(8):
                                PE(lambda e, ps=ps, h=h: e.matmul(ps[0:64, h * 64:h * 64 + T], lhsT=lhs[:, :], rhs=srct[:, h, 0:T], start=True, stop=True),
                                   r=[skey, lkey], w=[key])
                        return ps, key

                    def rsqrt_inplace(n):
                        ACT(lambda e: e.activation(out=d(n), in_=d(n), func=AF.Ln), r=[n], w=[n])
                        ACT(lambda e: e.activation(out=d(n), in_=d(n), func=AF.Exp, scale=-0.5), r=[n], w=[n])

                    def lerp3(Z, zk, mucol, dn):
                        DVE(lambda e: e.tensor_tensor(out=d("T1"), in0=Z[:, :, 0:T], in1=Z[:, :, 1:1 + T], op=ALU.subtract),
                            r=[zk], w=["T1"])
                        DVE(lambda e: e.tensor_tensor(out=d("T1"), in0=d("T1"), in1=ppb(mucol), op=ALU.mult),
                            r=["T1", "PP"], w=["T1"])
                        DVE(lambda e: e.tensor_tensor(out=d(dn), in0=d("T1"), in1=Z[:, :, 1:1 + T], op=ALU.add),
                            r=["T1", zk], w=[dn])
                        DVE(lambda e: e.tensor_copy(out=Z[:, :, 0:1], in_=Z[:, :, T:T + 1]), r=[zk], w=[zk])

                    lerp3(ZR, "ZR", MU_R, "Rr")
                    yield
                    lerp3(ZK, "ZK", MU_K, "Kz")
                    yield
                    lerp3(ZV, "ZV", MU_V, "Vv")
                    yield

                    def lerp2(Z, zk, rows, plcol, dst, dk):
                        DVE(lambda e: e.tensor_tensor(out=TL[0:rows, 0:T], in0=Z[:, 0:T], in1=Z[:, 1:1 + T], op=ALU.subtract),
                            r=[zk], w=["TL"])
                        DVE(lambda e: e.scalar_tensor_tensor(out=dst[0:rows, 0:T], in0=TL[0:rows, 0:T], scalar=PL[0:rows, plcol:plcol + 1],
                                                             in1=Z[:, 1:1 + T], op0=ALU.mult, op1=ALU.add), r=["TL", "PL", zk], w=[dk])
                        DVE(lambda e: e.tensor_copy(out=Z[:, 0:1], in_=Z[:, T:T + 1]), r=[zk], w=[zk])

                    lerp2(ZW, "ZW", 32, 0, TW, "TW")
                    yield
                    lerp2(ZA, "ZA", 32, 1, ZAl, "ZAl")
                    yield
                    lerp2(ZG, "ZG", 96, 2, SGg, "SGg")
                    yield
                    ACT(lambda e: e.activation(out=TW[:, 0:T], in_=TW[:, 0:T], func=AF.Tanh), r=["TW"], w=["TW"])
                    ACT(lambda e: e.activation(out=SGg[:, 0:T], in_=SGg[:, 0:T], func=AF.Sigmoid), r=["SGg"], w=["SGg"])
                    flags[("lerp", ti)] = True

                    def lora(Wt, wkey, xin, xkey, rows):
                        ps, key = ps1()
                        for h in range(8):
                            PE(lambda e, h=h, ps=ps: e.matmul(ps[0:64, h * 64:h * 64 + T], lhsT=Wt[0:rows, h * 64:(h + 1) * 64],
                                                              rhs=xin[0:rows, 0:T], start=True, stop=True), r=[wkey, xkey], w=[key])
                        return ps, key

                    yield
                    ps, key = lora(W2, "W2", TW, "TW", 32)
                    DVE(lambda e, ps=ps: e.tensor_tensor(out=d("T2"), in0=p3(ps), in1=ppb(W0c), op=ALU.add),
                        r=[key, "PP"], w=["T2"])
                    yield
                    ACT(lambda e: e.activation(out=d("T2"), in_=d("T2"), func=AF.Sigmoid), r=["T2"], w=["T2"])
                    for h in range(8):
                        DVE(lambda e, h=h: e.tensor_tensor_scan(out=CUM[:, h, 1:1 + T], data0=ONESF[:, 0:T], data1=D["T2"][:, h, 0:T],
                                                                initial=0.0, op0=ALU.mult, op1=ALU.add),
                            r=["T2", "ONESF"], w=["CUM"])
                    ACT(lambda e: e.activation(out=d("Winv"), in_=CUM[:, :, 1:1 + T], func=AF.Exp, scale=DEC), r=["CUM"], w=["Winv"])
                    ACT(lambda e: e.activation(out=d("Winc"), in_=CUM[:, :, 1:1 + T], func=AF.Exp, scale=-DEC), r=["CUM"], w=["Winc"])
                    ACT(lambda e: e.activation(out=d("Wexc"), in_=CUM[:, :, 0:T], func=AF.Exp, scale=-DEC), r=["CUM"], w=["Wexc"])
                    flags[("decay", ti)] = True
                    yield
                    ps, key = lora(A2, "A2", ZAl, "ZAl", 32)
                    DVE(lambda e, ps=ps: e.tensor_tensor(out=d("Aa"), in0=p3(ps), in1=ppb(A0c), op=ALU.add),
                        r=[key, "PP"], w=["Aa"])
                    yield
                    ACT(lambda e: e.activation(out=d("Aa"), in_=d("Aa"), func=AF.Sigmoid), r=["Aa"], w=["Aa"])
                    flags[("a", ti)] = True
                    DVE(lambda e: e.tensor_tensor(out=d("KKn"), in0=d("Kz"), in1=ppb(KKc), op=ALU.mult), r=["Kz", "PP"], w=["KKn"])
                    ACT(lambda e: e.activation(out=T1b[:, :, 0:T], in_=d("KKn"), func=AF.Square), r=["KKn"], w=["T1b"])

                    yield
                    ps, key = headsum(T1b, "T1b", ONESB, "ONESB")
                    DVE(lambda e, ps=ps: e.tensor_scalar(out=d("T2"), in0=p3(ps), scalar1=1e-24, scalar2=None, op0=ALU.max),
                        r=[key], w=["T2"])
                    yield
                    rsqrt_inplace("T2")
                    yield
                    DVE(lambda e: e.tensor_tensor(out=d("KKn"), in0=d("KKn"), in1=d("T2"), op=ALU.mult), r=["KKn", "T2"], w=["KKn"])
                    DVE(lambda e: e.scalar_tensor_tensor(out=d("At"), in0=d("KKn"), scalar=-1.0, in1=d("Wexc"), op0=ALU.mult, op1=ALU.mult),
                        r=["KKn", "Wexc"], w=["At"])
                    DVE(lambda e: e.tensor_tensor(out=d("Bt"), in0=d("KKn"), in1=d("Aa"), op=ALU.mult), r=["KKn", "Aa"], w=["Bt"])
                    DVE(lambda e: e.tensor_tensor(out=d("Bt"), in0=d("Bt"), in1=d("Winv"), op=ALU.mult), r=["Bt", "Winv"], w=["Bt"])


                    flags[("prep", ti)] = True
                    yield
                    def amat(lh, rh, mask, dstn):
                        pa, ka = ps1()
                        for h in range(8):
                            PE(lambda e, h=h, pa=pa: e.matmul(pa[0:Ck, h * 64:h * 64 + Ck], lhsT=D[lh][:, h, 0:T], rhs=D[rh][:, h, 0:T],
                                                              start=True, stop=True), r=[lh, rh], w=[ka])
                        DVE(lambda e, pa=pa: e.tensor_tensor(out=v3(Cc[dstn], Ck, Ck), in0=v3(pa, Ck, Ck),
                                                             in1=m3(mask, Ck, Ck), op=ALU.mult), r=[ka, ("CT", mask)], w=[dstn])

                    amat("Bt", "At", "msu", "X0")
                    yield
                    amat("At", "Bt", "msl", "Xt0")
                    yield
                    DVE(lambda e: e.tensor_tensor(out=v3(Cc["R0"], Ck, Ck), in0=v3(Cc["X0"], Ck, Ck), in1=m3("idt", Ck, Ck), op=ALU.add),
                        r=["X0", ("CT", "idt")], w=["R0"])
                    ACT(lambda e: e.copy(out=v3(Rtb[0], Ck, Ck), in_=v3(Cc["R0"], Ck, Ck)), r=["R0"], w=[("Rtb", 0)])
                    nl = 5 if Ck == 64 else 3
                    cur = 0

                    def sq(lt, lk, rt, rk):
                        pa, ka = ps1()
                        for h in range(8):
                            PE(lambda e, h=h, pa=pa: e.matmul(pa[0:Ck, h * 64:h * 64 + Ck], lhsT=lt[0:Ck, h * 64:h * 64 + Ck],
                                                              rhs=rt[0:Ck, h * 64:h * 64 + Ck], start=True, stop=True),
                               r=[lk, rk], w=[ka])
                        return pa, ka

                    for lvl in range(1, nl + 1):
                        nx = 1 - cur
                        if lvl == 1:
                            Xo, Xok, Xto, Xtok = Cc["X0"], "X0", Cc["Xt0"], "Xt0"
                        else:
                            Xo, Xok, Xto, Xtok = Xb[cur], ("Xb", cur), Xtb[cur], ("Xtb", cur)
                        pa, ka = sq(Xo, Xok, Xto, Xtok)
                        ACT(lambda e, pa=pa, nx=nx: e.copy(out=v3(Xtb[nx], Ck, Ck), in_=v3(pa, Ck, Ck)), r=[ka], w=[("Xtb", nx)])
                        yield
                        if lvl < nl:
                            pa, ka = sq(Xto, Xtok, Xo, Xok)
                            ACT(lambda e, pa=pa, nx=nx: e.copy(out=v3(Xb[nx], Ck, Ck), in_=v3(pa, Ck, Ck)), r=[ka], w=[("Xb", nx)])
                            yield
                        R, Rn = f"R{cur}", f"R{nx}"
                        pa, ka = sq(Xtb[nx], ("Xtb", nx), Rtb[cur], ("Rtb", cur))
                        DVE(lambda e, pa=pa, R=R, Rn=Rn: e.tensor_tensor(out=v3(Cc[Rn], Ck, Ck), in0=v3(pa, Ck, Ck),
                                                                         in1=v3(Cc[R], Ck, Ck), op=ALU.add), r=[ka, R], w=[Rn])
                        if lvl < nl:
                            ACT(lambda e, Rn=Rn, nx=nx: e.copy(out=v3(Rtb[nx], Ck, Ck), in_=v3(Cc[Rn], Ck, Ck)), r=[Rn], w=[("Rtb", nx)])
                        yield
                        cur = nx
                    Rfin = f"R{cur}"

                    yield
                def genP2b(ti):
                    r0, T = tiles[ti]
                    b = ti % 2
                    par = ti % 2
                    ht = HT[b]
                    ym = YMIX[b]
                    ymk = ("YMIX", b)
                    is_meta = (ti == 0)
                    Ck = T
                    gc = ti - 1
                    D = Dp[par]
                    Cc = Ccp[par]
                    PE, ACT, DVE, ST = (trw(f_, par) for f_ in (PE0, ACT0, DVE0, ST0))
                    QT = QTs[par]
                    PSP = "m"
                    yield
                    def ppb(col):
                        return PPt[:, col:col + 8].unsqueeze(2).to_broadcast([64, 8, T])

                    def d(n):
                        return D[n][:, :, 0:T]

                    def p3(ps, rows=64):
                        return ps.rearrange("p (h t) -> p h t", t=64)[0:rows, :, 0:T]

                    def headsum(srct, skey, lhs, lkey):
                        ps, key = ps1(PSP)
                        if T == 64:
                            PE(lambda e, ps=ps: e.matmul(ps[0:64, :], lhsT=lhs[:, :], rhs=srct[:].rearrange("p h t -> p (h t)"), start=True, stop=True),
                               r=[skey, lkey], w=[key])
                        else:
                            for h in range(8):
                                PE(lambda e, ps=ps, h=h: e.matmul(ps[0:64, h * 64:h * 64 + T], lhsT=lhs[:, :], rhs=srct[:, h, 0:T], start=True, stop=True),
                                   r=[skey, lkey], w=[key])
                        return ps, key

                    def rsqrt_inplace(n):
                        ACT(lambda e: e.activation(out=d(n), in_=d(n), func=AF.Ln), r=[n], w=[n])
                        ACT(lambda e: e.activation(out=d(n), in_=d(n), func=AF.Exp, scale=-0.5), r=[n], w=[n])

                    def lora(Wt, wkey, xin, xkey, rows):
                        ps, key = ps1()
                        for h in range(8):
                            PE(lambda e, h=h, ps=ps: e.matmul(ps[0:64, h * 64:h * 64 + T], lhsT=Wt[0:rows, h * 64:(h + 1) * 64],
                                                              rhs=xin[0:rows, 0:T], start=True, stop=True), r=[wkey, xkey], w=[key])
                        return ps, key

                    def waitf(name):
                        while not flags.get((name, ti)) and not getattr(S, "dry", False):
                            yield

                    def tr_tm(src, dstn):
                        pt, kt = ps1()
                        for h in range(8):
                            PE(lambda e, h=h, src=src, pt=pt: e.transpose(pt[0:Ck, h * 64:(h + 1) * 64], D[src][:, h, 0:T], IDF[0:64, 0:64]),
                               r=[src, "IDF"], w=[kt])
                        ACT(lambda e, pt=pt, dstn=dstn: e.copy(out=Cc[dstn][0:Ck, :], in_=pt[0:Ck, :]), r=[kt], w=[dstn])

                    def amat(lh, rh, mask, dstn):
                        pa, ka = ps1()
                        for h in range(8):
                            PE(lambda e, h=h, pa=pa: e.matmul(pa[0:Ck, h * 64:h * 64 + Ck], lhsT=D[lh][:, h, 0:T], rhs=D[rh][:, h, 0:T],
                                                              start=True, stop=True), r=[lh, rh], w=[ka])
                        DVE(lambda e, pa=pa: e.tensor_tensor(out=v3(Cc[dstn], Ck, Ck), in0=v3(pa, Ck, Ck),
                                                             in1=m3(mask, Ck, Ck), op=ALU.mult), r=[ka, ("CT", mask)], w=[dstn])

                    yield from waitf("lerp")
                    ps, key = lora(G2, "G2", SGg, "SGg", 96)
                    ACT(lambda e, ps=ps: e.copy(out=d("Gg"), in_=p3(ps)), r=[key], w=["Gg"])
                    yield
                    tr_tm("Vv", "VTMt")
                    yield
                    yield from waitf("decay")
                    DVE(lambda e: e.tensor_tensor(out=d("Rt"), in0=d("Rr"), in1=d("Winc"), op=ALU.mult), r=["Rr", "Winc"], w=["Rt"])
                    yield
                    yield from waitf("a")
                    DVE(lambda e: e.scalar_tensor_tensor(out=d("T3"), in0=d("Aa"), scalar=-1.0, in1=ppb(KAc), op0=ALU.add, op1=ALU.mult),
                        r=["Aa", "PP"], w=["T3"])
                    DVE(lambda e: e.scalar_tensor_tensor(out=d("Kk"), in0=d("T3"), scalar=1.0, in1=d("Kz"), op0=ALU.add, op1=ALU.mult),
                        r=["T3", "Kz"], w=["Kk"])
                    DVE(lambda e: e.tensor_tensor(out=d("Kt"), in0=d("Kk"), in1=d("Winv"), op=ALU.mult), r=["Kk", "Winv"], w=["Kt"])
                    DVE(lambda e: e.tensor_tensor(out=d("T3"), in0=d("Rr"), in1=d("Kk"), op=ALU.mult), r=["Rr", "Kk"], w=["T3"])
                    DVE(lambda e: e.tensor_tensor(out=T3b[:, :, 0:T], in0=d("T3"), in1=ppb(RKc), op=ALU.mult), r=["T3", "PP"], w=["T3b"])
                    yield
                    tr_tm("Kt", "KTMt")
                    yield
                    ps, key = headsum(T3b, "T3b", ONESB, "ONESB")
                    DVE(lambda e, ps=ps: e.tensor_tensor(out=d("BON"), in0=p3(ps), in1=d("Vv"), op=ALU.mult), r=[key, "Vv"], w=["BON"])
                    yield
                    amat("Kt", "Rt", "mui", "ARK")
                    yield
                    yield from waitf("prep")
                    tr_tm("Bt", "BTMt")
                    yield
                    amat("Kt", "At", "msu", "AAK")
                    yield
                    amat("Bt", "Rt", "mui", "ARB")
                    yield
                def genAtt(ti):
                    r0, T = tiles[ti]
                    b = ti % 2
                    par = ti % 2
                    ht = HT[b]
                    ym = YMIX[b]
                    ymk = ("YMIX", b)
                    is_meta = (ti == 0)
                    Ck = T
                    gc = ti - 1
                    D = Dp[par]
                    Cc = Ccp[par]
                    PE, ACT, DVE, ST = (trw(f_, par) for f_ in (PE0, ACT0, DVE0, ST0))
                    QT = QTs[par]
                    PSP = "m"
                    yield
                    def ppb(col):
                        return PPt[:, col:col + 8].unsqueeze(2).to_broadcast([64, 8, T])

                    def d(n):
                        return D[n][:, :, 0:T]

                    def p3(ps, rows=64):
                        return ps.rearrange("p (h t) -> p h t", t=64)[0:rows, :, 0:T]

                    def headsum(srct, skey, lhs, lkey):
                        ps, key = ps1(PSP)
                        if T == 64:
                            PE(lambda e, ps=ps: e.matmul(ps[0:64, :], lhsT=lhs[:, :], rhs=srct[:].rearrange("p h t -> p (h t)"), start=True, stop=True),
                               r=[skey, lkey], w=[key])
                        else:
                            for h in range(8):
                                PE(lambda e, ps=ps, h=h: e.matmul(ps[0:64, h * 64:h * 64 + T], lhsT=lhs[:, :], rhs=srct[:, h, 0:T], start=True, stop=True),
                                   r=[skey, lkey], w=[key])
                        return ps, key

                    def rsqrt_inplace(n):
                        ACT(lambda e: e.activation(out=d(n), in_=d(n), func=AF.Ln), r=[n], w=[n])
                        ACT(lambda e: e.activation(out=d(n), in_=d(n), func=AF.Exp, scale=-0.5), r=[n], w=[n])

                    if is_meta:
                        blocks = [(lambda kv: KTM[:, kv, :], "KTMeta", lambda kv: VM[0:16, kv * 64:(kv + 1) * 64], "VMeta", 16,
                                   CT["almm"][:].rearrange("p (h t) -> p h t", t=16), ("CT", "almm"))]
                    else:
                        DVE(lambda e: e.scalar_tensor_tensor(out=ALMC[:, :], in0=CT["almd"][:, :], scalar=float(gc), in1=CT["alm0"][:, :],
                                                             op0=ALU.mult, op1=ALU.add), r=[("CT", "almd"), ("CT", "alm0")], w=["ALMC"])
                        blocks = []
                        for back, aln in ((2, "al2"), (1, "al1"), (0, "al0")):
                            g2_ = gc - back
                            if g2_ < 0:
                                continue
                            sl_ = g2_ % 4
                            blocks.append((lambda kv, sl_=sl_: KTR[:, kv, sl_, :], ("KTR", sl_),
                                           lambda kv, sl_=sl_: VR[:, sl_, kv * 64:(kv + 1) * 64], ("VR", sl_), 64,
                                           CT[aln][:].rearrange("p (h t) -> p h t", t=64), ("CT", aln)))
                        blocks.append((lambda kv: KTM[:, kv, :], "KTMeta", lambda kv: VM[0:16, kv * 64:(kv + 1) * 64], "VMeta", 16,
                                       ALMC[:].rearrange("p (h t) -> p h t", t=64), "ALMC"))
                    nq = Ck
                    pts = []
                    for bi, (kf, kkey, vf, vkey, nk, bias3, bkey) in enumerate(blocks):
                        pa, ka = ps1()
                        for h in range(8):
                            PE(lambda e, h=h, pa=pa, kf=kf, nk=nk: e.matmul(pa[0:nk, h * 64:h * 64 + nq], lhsT=kf(h // 4), rhs=QT[:, h, 0:T],
                                                                            start=True, stop=True), r=[kkey, "QT"], w=[ka])
                        sbt_ = SBT[bi % 2]
                        DVE(lambda e, pa=pa, nk=nk, bias3=bias3, sbt_=sbt_: e.scalar_tensor_tensor(
                            out=v3(sbt_, nk, nq), in0=v3(pa, nk, nq), scalar=0.125,
                            in1=bias3[0:nk, :, 0:nq], op0=ALU.mult, op1=ALU.add), r=[ka, bkey], w=[("SBT", 0)])
                        ACT(lambda e, nk=nk, sbt_=sbt_, bi=bi: e.activation(out=v3(PT[bi], nk, nq), in_=v3(sbt_, nk, nq), func=AF.Exp),
                            r=[("SBT", 0)], w=[("PT", bi)])
                        pts.append((bi, nk, vf, vkey))
                        yield
                    pd, kd = ps1()
                    po, ko = ps1()
                    nb_ = len(pts)
                    for i_, (bi, nk, vf, vkey) in enumerate(pts):
                        if nq == 64:
                            PE(lambda e, bi=bi, nk=nk, pd=pd, i_=i_: e.matmul(
                                pd[0:64, :], lhsT=ONESB[0:nk, :], rhs=PT[bi][0:nk, :],
                                start=(i_ == 0), stop=(i_ == nb_ - 1)), r=[("PT", bi), "ONESB"], w=[kd])
                        else:
                            for h in range(8):
                                PE(lambda e, bi=bi, nk=nk, pd=pd, i_=i_, h=h: e.matmul(
                                    pd[0:64, h * 64:h * 64 + nq], lhsT=ONESB[0:nk, :], rhs=PT[bi][0:nk, h * 64:h * 64 + nq],
                                    start=(i_ == 0), stop=(i_ == nb_ - 1)), r=[("PT", bi), "ONESB"], w=[kd])
                    for kv in range(2):
                        for i_, (bi, nk, vf, vkey) in enumerate(pts):
                            if nq == 64:
                                PE(lambda e, bi=bi, nk=nk, po=po, i_=i_, kv=kv, vf=vf: e.matmul(
                                    po[0:64, kv * 256:(kv + 1) * 256], lhsT=vf(kv),
                                    rhs=PT[bi][0:nk, kv * 256:(kv + 1) * 256], start=(i_ == 0), stop=(i_ == nb_ - 1)),
                                   r=[("PT", bi), vkey], w=[ko])
                            else:
                                for hh in range(4):
                                    h = kv * 4 + hh
                                    PE(lambda e, bi=bi, nk=nk, po=po, i_=i_, kv=kv, vf=vf, h=h: e.matmul(
                                        po[0:64, h * 64:h * 64 + nq], lhsT=vf(kv),
                                        rhs=PT[bi][0:nk, h * 64:h * 64 + nq], start=(i_ == 0), stop=(i_ == nb_ - 1)),
                                       r=[("PT", bi), vkey], w=[ko])
                    DVE(lambda e, pd=pd: e.tensor_tensor(out=v3(DEN, 64, nq), in0=v3(pd, 64, nq),
                                                         in1=ESK[:, :].unsqueeze(2).to_broadcast([64, 8, nq]), op=ALU.add),
                        r=[kd, "ESK"], w=["DEN"])
                    ACT(lambda e: e.activation(out=v3(DEN, 64, nq), in_=v3(DEN, 64, nq), func=AF.Ln), r=["DEN"], w=["DEN"])
                    ACT(lambda e: e.activation(out=v3(DEN, 64, nq), in_=v3(DEN, 64, nq), func=AF.Exp, scale=-1.0), r=["DEN"], w=["DEN"])
                    DVE(lambda e, po=po: e.tensor_tensor(out=ym[:, 8:16, 0:T], in0=v3(po, 64, nq),
                                                         in1=v3(DEN, 64, nq), op=ALU.mult), r=[ko, "DEN"], w=[ymk])


                    yield
                def genS(ti):
                    r0, T = tiles[ti]
                    b = ti % 2
                    par = ti % 2
                    ht = HT[b]
                    ym = YMIX[b]
                    ymk = ("YMIX", b)
                    is_meta = (ti == 0)
                    Ck = T
                    gc = ti - 1
                    D = Dp[par]
                    Cc = Ccp[par]
                    PE, ACT, DVE, ST = (trw(f_, par) for f_ in (PE0, ACT0, DVE0, ST0))
                    QT = QTs[par]
                    PSP = "s"
                    Rfin = "R1"
                    def ppb(col):
                        return PPt[:, col:col + 8].unsqueeze(2).to_broadcast([64, 8, T])

                    def d(n):
                        return D[n][:, :, 0:T]

                    def p3(ps, rows=64):
                        return ps.rearrange("p (h t) -> p h t", t=64)[0:rows, :, 0:T]

                    def headsum(srct, skey, lhs, lkey):
                        ps, key = ps1(PSP)
                        if T == 64:
                            PE(lambda e, ps=ps: e.matmul(ps[0:64, :], lhsT=lhs[:, :], rhs=srct[:].rearrange("p h t -> p (h t)"), start=True, stop=True),
                               r=[skey, lkey], w=[key])
                        else:
                            for h in range(8):
                                PE(lambda e, ps=ps, h=h: e.matmul(ps[0:64, h * 64:h * 64 + T], lhsT=lhs[:, :], rhs=srct[:, h, 0:T], start=True, stop=True),
                                   r=[skey, lkey], w=[key])
                        return ps, key

                    def rsqrt_inplace(n):
                        ACT(lambda e: e.activation(out=d(n), in_=d(n), func=AF.Ln), r=[n], w=[n])
                        ACT(lambda e: e.activation(out=d(n), in_=d(n), func=AF.Exp, scale=-0.5), r=[n], w=[n])

                    pa, ka = ps1("s")
                    for h in range(8):
                        PE(lambda e, h=h, pa=pa: e.matmul(pa[0:Ck, h * 64:(h + 1) * 64], lhsT=D["At"][:, h, 0:T], rhs=STT[:, h, :],
                                                          start=True, stop=False), r=["At", "STATE"], w=[ka])
                        PE(lambda e, h=h, pa=pa: e.matmul(pa[0:Ck, h * 64:(h + 1) * 64], lhsT=Cc["AAK"][0:Ck, h * 64:h * 64 + Ck],
                                                          rhs=Cc["VTMt"][0:Ck, h * 64:(h + 1) * 64], start=False, stop=True),
                           r=["AAK", "VTMt"], w=[ka])
                    ACT(lambda e, pa=pa: e.copy(out=Cc["RHS0"][0:Ck, :], in_=pa[0:Ck, :]), r=[ka], w=["RHS0"])
                    yield
                    pa, ka = ps1("s")
                    for h in range(8):
                        PE(lambda e, h=h, pa=pa: e.matmul(pa[0:Ck, h * 64:(h + 1) * 64], lhsT=Cc[Rfin][0:Ck, h * 64:h * 64 + Ck],
                                                          rhs=Cc["RHS0"][0:Ck, h * 64:(h + 1) * 64], start=True, stop=True),
                           r=[Rfin, "RHS0"], w=[ka])
                    DVE(lambda e, pa=pa: e.tensor_copy(out=Cc["UU"][0:Ck, :], in_=pa[0:Ck, :]), r=[ka], w=["UU"])
                    yield
                    py, ky = ps1("s")
                    for h in range(8):
                        PE(lambda e, h=h, py=py: e.matmul(py[0:64, h * 64:h * 64 + Ck], lhsT=STT[:, h, :], rhs=D["Rt"][:, h, 0:T],
                                                          start=True, stop=False), r=["STATE", "Rt"], w=[ky])
                        PE(lambda e, h=h, py=py: e.matmul(py[0:64, h * 64:h * 64 + Ck], lhsT=Cc["UU"][0:Ck, h * 64:(h + 1) * 64],
                                                          rhs=Cc["ARB"][0:Ck, h * 64:h * 64 + Ck], start=False, stop=False),
                           r=["UU", "ARB"], w=[ky])
                        PE(lambda e, h=h, py=py: e.matmul(py[0:64, h * 64:h * 64 + Ck], lhsT=Cc["VTMt"][0:Ck, h * 64:(h + 1) * 64],
                                                          rhs=Cc["ARK"][0:Ck, h * 64:h * 64 + Ck], start=False, stop=True),
                           r=["VTMt", "ARK"], w=[ky])
                    ACT(lambda e, py=py: e.copy(out=d("YS"), in_=p3(py)), r=[ky], w=["YS"])
                    ACT(lambda e, py=py: e.copy(out=YSb[:, :, 0:T], in_=p3(py)), r=[ky], w=["YSb"])
                    yield
                    pst, kst = ps1("s")
                    for h in range(8):
                        PE(lambda e, h=h, pst=pst: e.matmul(pst[0:64, h * 64:(h + 1) * 64], lhsT=Cc["KTMt"][0:Ck, h * 64:(h + 1) * 64],
                                                            rhs=Cc["VTMt"][0:Ck, h * 64:(h + 1) * 64], start=True, stop=False),
                           r=["KTMt", "VTMt"], w=[kst])
                        PE(lambda e, h=h, pst=pst: e.matmul(pst[0:64, h * 64:(h + 1) * 64], lhsT=Cc["BTMt"][0:Ck, h * 64:(h + 1) * 64],
                                                            rhs=Cc["UU"][0:Ck, h * 64:(h + 1) * 64], start=False, stop=True),
                           r=["BTMt", "UU"], w=[kst])
                    DVE(lambda e, pst=pst: e.tensor_tensor(out=Cc["RHS0"][:, :], in0=pst[0:64, :], in1=STT[:].rearrange("p h v -> p (h v)"),
                                                           op=ALU.add), r=[kst, "STATE"], w=["RHS0"])
                    DVE(lambda e: e.tensor_tensor(out=STT[:, :, :], in0=Cc["RHS0"][:].rearrange("p (h v) -> p h v", v=64),
                                                  in1=D["Winc"][:, :, T - 1:T].to_broadcast([64, 8, 64]), op=ALU.mult),
                        r=["RHS0", "Winc"], w=["STATE"])


                    yield
                    yield
                    ps, key = headsum(YSb, "YSb", ONESDB, "ONESDB")
                    DVE(lambda e, ps=ps: e.tensor_tensor(out=d("S1"), in0=d("YS"), in1=p3(ps), op=ALU.subtract), r=[key, "YS"], w=["S1"])
                    ACT(lambda e: e.activation(out=S1b[:, :, 0:T], in_=d("S1"), func=AF.Square), r=["S1"], w=["S1b"])
                    yield
                    ps, key = headsum(S1b, "S1b", ONESDB, "ONESDB")
                    DVE(lambda e, ps=ps: e.tensor_scalar(out=d("S2"), in0=p3(ps), scalar1=64e-5, scalar2=None, op0=ALU.add), r=[key], w=["S2"])
                    yield
                    rsqrt_inplace("S2")
                    yield
                    DVE(lambda e: e.tensor_tensor(out=d("S1"), in0=d("S1"), in1=d("S2"), op=ALU.mult), r=["S1", "S2"], w=["S1"])
                    DVE(lambda e: e.tensor_tensor(out=d("S1"), in0=d("S1"), in1=ppb(LNWc), op=ALU.mult), r=["S1", "PP"], w=["S1"])
                    DVE(lambda e: e.tensor_tensor(out=d("S1"), in0=d("S1"), in1=ppb(LNBc), op=ALU.add), r=["S1", "PP"], w=["S1"])
                    DVE(lambda e: e.tensor_tensor(out=d("S1"), in0=d("S1"), in1=d("BON"), op=ALU.add), r=["S1", "BON"], w=["S1"])
                    DVE(lambda e: e.tensor_tensor(out=ym[:, 0:8, 0:T], in0=d("S1"), in1=d("Gg"), op=ALU.mult), r=["S1", "Gg"], w=[ymk])
                    ST(lambda e, r0=r0, T=T, ym=ym: e.dma_start(out=ym_d[:, r0:r0 + T].rearrange("(g p) t -> p g t", p=64), in_=ym[:, :, 0:T]),
                       r=[ymk], w=[("ymd", ti)])
                    yield
                def drive_dyn(gens):
                    live = [[g, 0.0] for g, _ in gens]
                    while live:
                        tpe = S.tfree["tensor"] + SLACK
                        cands = [it for it in live if it[1] + S.lat <= tpe]
                        item = cands[0] if cands else min(live, key=lambda it: it[1])
                        S.step_fin = 0.0
                        try:
                            next(item[0])
                            if S.step_fin > 0.0:
                                item[1] = S.step_fin
                            else:
                                item[1] += 0.3
                        except StopIteration:
                            live.remove(item)

                def count_steps(genf, ti):
                    S.dry = True
                    sv = dict(rot)
                    fl = dict(flags)
                    n = 0
                    for _ in genf(ti):
                        n += 1
                    rot.update(sv)
                    flags.clear()
                    flags.update(fl)
                    S.dry = False
                    return n + 1

                def drive_frac(gens):
                    live = [[g, 0, float(tot)] for g, tot in gens]
                    while live:
                        item = min(live, key=lambda it: (it[1] + BIAS.get(id(it[0]), 0.0)) / it[2])
                        try:
                            next(item[0])
                            item[1] += 1
                        except StopIteration:
                            live.remove(item)

                BIAS = {}

                def drive(gens):
                    live = [[g, n] for g, n in gens]
                    while live:
                        for item in list(live):
                            g, n = item
                            for _ in range(n):
                                try:
                                    next(g)
                                except StopIteration:
                                    live.remove(item)
                                    break

                rot["pipe"] = True
                load_tile(0)
                nt_ = len(tiles)
                drive([(genP1(0), 1)])
                drive([(genP2(0), 1), (genP2b(0), 1), (genAtt(0), 1)] + ([(genP1(1), 1)] if nt_ > 1 else []))
                for ti in range(nt_):
                    if FRAC:
                        gl = [(genS(ti), count_steps(genS, ti) * FW[0])]
                        if ti + 1 < nt_:
                            gl += [(genP2(ti + 1), count_steps(genP2, ti + 1) * FW[1]), (genAtt(ti + 1), count_steps(genAtt, ti + 1) * FW[2])]
                        if ti + 2 < nt_:
                            gl += [(genP1(ti + 2), count_steps(genP1, ti + 2) * FW[3])]
                        drive_frac(gl)
                        continue
                    gl = [(genS(ti), WGT[0])]
                    if ti + 1 < nt_:
                        gl += [(genP2(ti + 1), WGT[1]), (genP2b(ti + 1), WGT[4]), (genAtt(ti + 1), WGT[2])]
                    if ti + 2 < nt_:
                        gl += [(genP1(ti + 2), WGT[3])]
                    (drive_dyn if DYN else drive)(gl)
                rot["pipe"] = False
                S.barrier()

        def outproj_phase():
            with ExitStack() as es:
                WOUT = sbt(es, "WOUT", [128, 8, DM], BF16)
                HT = [sbt(es, f"HTo{b}", [128, 2, DM]) for b in range(2)]
                YT = [sbt(es, f"YTo{b}", [128, 8, 256], BF16) for b in range(2)]
                for c in range(8):
                    LDC(lambda e, c=c: e.dma_start(out=WOUT[:, c, :], in_=wout_d[c * 128:(c + 1) * 128, :]), w=[("WOUT", c)])
                tiles = [(0, NM)] + [(NM + 256 * i, 256) for i in range(NR // 256)]

                def load_tile(ti):
                    r0, T = tiles[ti]
                    P = min(T, 128)
                    nsub = (T + 127) // 128
                    b = ti % 2
                    LD(lambda e: e.dma_start(out=HT[b][:P, 0:nsub, :], in_=h1_d[r0:r0 + T, :].rearrange("(s p) d -> p s d", p=P)),
                       w=[("HTo", b)])
                    LD(lambda e: e.dma_start(out=YT[b][:, :, 0:T], in_=ym_d[:, r0:r0 + T].rearrange("(c p) t -> p c t", p=128)),
                       w=[("YTo", b)])

                load_tile(0)
                for ti, (r0, T) in enumerate(tiles):
                    P = min(T, 128)
                    nsub = (T + 127) // 128
                    b = ti % 2
                    if ti + 1 < len(tiles):
                        load_tile(ti + 1)
                    for s in range(nsub):
                        for hf in range(2):
                            po, ko = ps1()
                            for c in range(8):
                                PE(lambda e, c=c, s=s, hf=hf, po=po: e.matmul(po[0:P, :], lhsT=YT[b][:, c, s * 128:s * 128 + P],
                                                                              rhs=WOUT[:, c, hf * 512:(hf + 1) * 512],
                                                                              start=(c == 0), stop=(c == 7)),
                                   r=[("YTo", b), ("WOUT", c)], w=[ko])
                            DVE(lambda e, s=s, hf=hf, po=po: e.tensor_tensor(out=HT[b][0:P, s, hf * 512:(hf + 1) * 512], in0=po[0:P, :],
                                                                             in1=HT[b][0:P, s, hf * 512:(hf + 1) * 512], op=ALU.add),
                                r=[ko, ("HTo", b)], w=[("HTo", b)])
                    ST(lambda e, r0=r0, T=T, P=P, nsub=nsub, b=b: e.dma_start(
                        out=h2_d[r0:r0 + T, :].rearrange("(s p) d -> p s d", p=P), in_=HT[b][:P, 0:nsub, :]), r=[("HTo", b)], w=[("h2", ti)])
                S.barrier()

        tilesA = [(0, NM)] + [(NM + 256 * i, 256) for i in range(NR // 256)]
        tilesC = [(NM + 256 * i, 256) for i in range(NR // 256)]

        def srcA(r0, T):
            return meta_d[0:NM, :] if r0 == 0 else x_d[r0 - NM:r0 - NM + T, :]

        if "A" in phases:
            ffn_phase("A", tilesA, srcA, lambda r0, T: h1_d[r0:r0 + T, :], 0, 0, False)
        if "B" in phases:
            mix_phase()
        if "O" in phases:
            outproj_phase()
        if "C" in phases:
            ffn_phase("C", tilesC, lambda r0, T: h2_d[r0:r0 + T, :], lambda r0, T: out_d[r0 - NM:r0 - NM + T, :], 1, 2, True)
        if dbg:
            dh1 = nc.dram_tensor("dbg_h1", [LT, DM], F32, kind="ExternalOutput").ap()
            dh2 = nc.dram_tensor("dbg_h2", [LT, DM], F32, kind="ExternalOutput").ap()
            dym = nc.dram_tensor("dbg_ym", [DM, LT], BF16, kind="ExternalOutput").ap()
            ST(lambda e: e.dma_start(out=dh1, in_=h1_d), w=["dbg1"])
            ST(lambda e: e.dma_start(out=dh2, in_=h2_d), w=["dbg2"])
            ST(lambda e: e.dma_start(out=dym, in_=ym_d), w=["dbg3"])
        S.barrier(["sync"])
    return nc, S


def host_inputs(inp, b, NR=4096):
    f = lambda a: np.ascontiguousarray(np.asarray(a, dtype=np.float32))
    m = {}
    m["x"] = f(inp["x"][b][:NR])
    m["meta"] = f(inp["meta_tokens"])
    m["gvec"] = f(np.stack([inp["ffn1_norm"][0], inp["mix_norm"][0], inp["ffn2_norm"][0], inp["final_norm"]], 0))
    m["wg1"] = f(inp["ffn1_w_gate"][0]); m["wu1"] = f(inp["ffn1_w_up"][0]); m["wd1"] = f(inp["ffn1_w_down"][0])
    m["wg2"] = f(inp["ffn2_w_gate"][0]); m["wu2"] = f(inp["ffn2_w_up"][0]); m["wd2"] = f(inp["ffn2_w_down"][0])
    m["w_in"] = f(inp["w_in"][0]); m["w_out"] = f(inp["w_out"][0])
    m["w2"] = f(inp["rwkv_w2"][0]); m["a2"] = f(inp["rwkv_a2"][0]); m["g2"] = f(inp["rwkv_g2"][0])
    mu = np.asarray(inp["rwkv_mu"][0], np.float32)
    hp = lambda v: np.asarray(v, np.float32).reshape(-1, 64).T
    b_attn = np.asarray(inp["b_attn"][0], np.float32)
    cols = [hp(mu[0:512]), hp(mu[512:1024]), hp(mu[1024:1536]), hp(inp["rwkv_w0"][0]), hp(inp["rwkv_a0"][0]),
            hp(inp["rwkv_k_k"][0]), hp(inp["rwkv_k_a"][0]), hp(np.asarray(inp["rwkv_r_k"][0]).reshape(-1)),
            hp(inp["rwkv_ln_w"][0]), hp(inp["rwkv_ln_b"][0]), hp(b_attn[0:512]), hp(b_attn[512:640])]
    m["pp"] = f(np.concatenate(cols, axis=1))
    pl = np.zeros((96, 3), np.float32)
    pl[:32, 0] = mu[1536:1568]; pl[:32, 1] = mu[1568:1600]; pl[:96, 2] = mu[1600:1696]
    m["pl"] = pl
    m["bv"] = f(b_attn[640:768].reshape(1, 128))
    m["sinks"] = f(np.asarray(inp["attn_sinks"][0]).reshape(1, 8))
    return m


_CACHE = {}


def kernel(**inputs):
    B = inputs["x"].shape[0]
    NR = inputs["x"].shape[1]
    if NR not in _CACHE:
        _CACHE[NR] = build_nc(NR)[0]
    nc = _CACHE[NR]
    consts = {"c_" + k: v for k, v in make_consts().items()}
    in_maps = []
    for b in range(B):
        m = host_inputs(inputs, b, NR)
        m.update(consts)
        in_maps.append(m)
    res = run_bass_kernel_spmd(nc, in_maps, core_ids=list(range(B)))
    out = np.stack([np.asarray(r["out"], dtype=np.float32) for r in res.results], axis=0)
    return out
```

```python
import numpy as np
from contextlib import ExitStack
import concourse.bass as bass
import concourse.mybir as mybir
from concourse.bass_utils import run_bass_kernel_spmd

F32 = mybir.dt.float32
BF16 = mybir.dt.bfloat16
AF = mybir.ActivationFunctionType
ALU = mybir.AluOpType

NM = 16
DM = 1024
FF = 2816
NFC = 22
DEC = 0.6065306597126334


import os
STRICT = os.environ.get("K_STRICT", "1") == "1"
DYN = os.environ.get("K_DYN", "0") == "1"
FRAC = os.environ.get("K_FRAC", "0") == "1"
OQ = int(os.environ.get("K_OQ", "2"))
NORM_GAP = int(os.environ.get("K_NGAP", "3"))
FW = [float(v) for v in os.environ.get("K_FW", "1,1,1,1").split(",")]
SLACK = float(os.environ.get("K_SLACK", "0.0"))
WGT = [int(v) for v in os.environ.get("K_WGT", "1,2,1,1,1").split(",")]


class Sched:
    STREAMS = ("sync", "scalar", "vector", "gpsimd", "tensor")

    def __init__(self, nc):
        self.nc = nc
        self.prog = {}
        self.groups = {}
        self.w = {}
        self.r = {}
        self.waited = {n: {} for n in self.STREAMS}
        self.nops = 0
        self.tfree = {n: 0.0 for n in self.STREAMS}
        self.fin = {}
        self.step_fin = 0.0
        self.cost = {"sync": 2.0, "scalar": 0.55, "vector": 0.5, "gpsimd": 2.0, "tensor": 0.08}
        self.lat = 1.2

    def add_prog(self, name, sem, inc, inorder):
        self.prog[name] = dict(sem=sem, count=0, inc=inc, inorder=inorder)

    def add_group(self, name, sems):
        subs = []
        for i, s in enumerate(sems):
            sub = f"{name}_{i}"
            self.add_prog(sub, s, 16, False)
            subs.append(sub)
        self.groups[name] = dict(subs=subs, i=0)

    def op(self, stream, prog, fn, reads=(), writes=(), cost=None):
        if getattr(self, "dry", False):
            return
        deps = {}
        if prog in self.groups:
            g = self.groups[prog]
            prog = g["subs"][g["i"] % len(g["subs"])]
            g["i"] += 1
            if self.prog[prog]["count"] > 0:
                deps[prog] = self.prog[prog]["count"]
        inorder = self.prog[prog]["inorder"] and not (STRICT and prog != "pe")
        for k in reads:
            w = self.w.get(k)
            if w is not None and not (w[0] == prog and prog == "pe"):
                deps[w[0]] = max(deps.get(w[0], 0), w[1])
        for k in writes:
            w = self.w.get(k)
            if w is not None and not (inorder and w[0] == prog):
                deps[w[0]] = max(deps.get(w[0], 0), w[1])
            for (p, n) in self.r.get(k, ()):
                if inorder and p == prog:
                    continue
                deps[p] = max(deps.get(p, 0), n)
        eng = getattr(self.nc, stream)
        wd = self.waited[stream]
        t0 = self.tfree[stream]
        for p, n in deps.items():
            f_ = self.fin.get((p, n), 0.0) + (0.0 if p == prog else self.lat)
            if f_ > t0:
                t0 = f_
            if wd.get(p, 0) < n:
                wd[p] = n
                eng.wait_ge(self.prog[p]["sem"], n)
        pr = self.prog[prog]
        pr["count"] += pr["inc"]
        cnt = pr["count"]
        t1 = t0 + (cost if cost is not None else self.cost[stream])
        self.tfree[stream] = t1 if pr["inorder"] else t0 + 0.1
        self.fin[(prog, cnt)] = t1
        if t1 > self.step_fin:
            self.step_fin = t1
        fn(eng).then_inc(pr["sem"], pr["inc"])
        for k in writes:
            self.w[k] = (prog, cnt)
            self.r[k] = []
        for k in reads:
            lst = [x for x in self.r.get(k, []) if x[0] != prog]
            lst.append((prog, cnt))
            self.r[k] = lst
        self.nops += 1

    def barrier(self, streams=None):
        waits_all = [(n, pr["sem"], pr["count"]) for n, pr in self.prog.items() if pr["count"] > 0]
        for st in (streams or self.STREAMS):
            wd = self.waited[st]
            eng = getattr(self.nc, st)
            for (n, s, c) in waits_all:
                if wd.get(n, 0) < c:
                    wd[n] = c
                    eng.wait_ge(s, c)


def make_consts():
    c = {}
    c["ident"] = np.eye(128, dtype=np.float32)
    s = np.arange(64)[:, None]
    t = np.arange(64)[None, :]
    rep8 = lambda m: np.ascontiguousarray(np.tile(m.astype(np.float32)[:, None, :], (1, 8, 1)).reshape(m.shape[0], -1))
    c["msu"] = (t > s).astype(np.float32)
    c["msl"] = (t < s).astype(np.float32)
    c["mui"] = (t >= s).astype(np.float32)
    c["idt"] = (t == s).astype(np.float32)
    slopes = 2.0 ** (-(np.arange(8) + 1.0))
    j = np.arange(64)[:, None].astype(np.float64)
    i = np.arange(64)[None, :].astype(np.float64)

    def alibi(dist):
        return np.ascontiguousarray(
            (-slopes[None, :, None] * dist[:, None, :]).astype(np.float32).reshape(dist.shape[0], -1))

    c["al2"] = alibi(128 + i - j)
    c["al1"] = alibi(64 + i - j)
    c["al0"] = alibi(np.abs(i - j))
    m = np.arange(16)[:, None].astype(np.float64)
    c["alm0"] = alibi(16 + i - m)
    c["almd"] = alibi(np.full((16, 64), 64.0))
    i16 = np.arange(16)[None, :].astype(np.float64)
    c["almm"] = alibi(np.abs(i16 - m))
    return c


CONST_SHAPES = dict(ident=[128, 128], msu=[64, 64], msl=[64, 64], mui=[64, 64], idt=[64, 64],
                    al2=[64, 512], al1=[64, 512], al0=[64, 512], alm0=[16, 512], almd=[16, 512],
                    almm=[16, 128])
NPP = 90


def build_nc(NR=4096, phases="ABOC", dbg=False):
    nc = bass.Bass("TRN2", target_bir_lowering=False)
    LT = NM + NR
    S = Sched(nc)

    def din(name, shape):
        return nc.dram_tensor(name, list(shape), F32, kind="ExternalInput").ap()

    x_d = din("x", [NR, DM])
    meta_d = din("meta", [NM, DM])
    gvec_d = din("gvec", [4, DM])
    wg_d = [din("wg1", [DM, FF]), din("wg2", [DM, FF])]
    wu_d = [din("wu1", [DM, FF]), din("wu2", [DM, FF])]
    wd_d = [din("wd1", [FF, DM]), din("wd2", [FF, DM])]
    win_d = din("w_in", [DM, 2464])
    wout_d = din("w_out", [DM, DM])
    w2_d = din("w2", [32, 512])
    a2_d = din("a2", [32, 512])
    g2_d = din("g2", [96, 512])
    pp_d = din("pp", [64, NPP])
    pl_d = din("pl", [96, 3])
    bv_d = din("bv", [1, 128])
    sinks_d = din("sinks", [1, 8])
    cst = {k: din("c_" + k, shp) for k, shp in CONST_SHAPES.items()}
    out_d = nc.dram_tensor("out", [NR, DM], F32, kind="ExternalOutput").ap()
    h1_d = nc.dram_tensor("h1s", [LT, DM], F32).ap()
    h2_d = nc.dram_tensor("h2s", [LT, DM], F32).ap()
    ym_d = nc.dram_tensor("yms", [DM, LT], BF16).ap()

    def OPf(stream, prog):
        def f(fn, r=(), w=()):
            S.op(stream, prog, fn, r, w)
        return f

    PE = OPf("tensor", "pe")
    ACT = OPf("scalar", "act")
    DVE = OPf("vector", "dve")
    POOL = OPf("gpsimd", "pool")
    LD = OPf("sync", "dq0")
    ST = OPf("sync", "dq2")
    LDC = OPf("gpsimd", "dq1")
    LD2 = OPf("scalar", "dq0")
    ST2 = OPf("scalar", "dq2")

    with ExitStack() as top:
        sems = {n: top.enter_context(nc.semaphore(n)) for n in ("pe", "act", "dve", "pool")}
        for n in ("pe", "act", "dve", "pool"):
            S.add_prog(n, sems[n], 1, True)
        for n, K_ in (("dq0", 8), ("dq1", 12), ("dq2", 6)):
            S.add_group(n, [top.enter_context(nc.semaphore(f"{n}_{i}")) for i in range(K_)])

        pp_t = [top.enter_context(nc.psum_tensor(f"pp{i}", [128, 1024], F32)) for i in range(3)]
        p_single = top.enter_context(nc.psum_tensor("psg", [128, 512], F32))
        PSB = top.enter_context(nc.psum_tensor("psb", [128, 1024], BF16))
        banks = []
        for i in range(3):
            banks.append((pp_t[i], 0, f"ps{2 * i}"))
            banks.append((pp_t[i], 512, f"ps{2 * i + 1}"))
        banks.append((p_single, 0, "ps6"))
        rot = {"s": 0, "p": 0}

        rot["m"] = 0
        rot["pipe"] = False

        def ps1(pool="m"):
            if not rot["pipe"]:
                t, off, k = banks[rot["s"] % 7]
                rot["s"] += 1
            elif pool == "s":
                t, off, k = banks[5 + rot["s"] % 2]
                rot["s"] += 1
            else:
                t, off, k = banks[rot["m"] % 5]
                rot["m"] += 1
            return t[:, off:off + 512], k

        def ps2():
            i = rot["p"] % 3
            rot["p"] += 1
            return pp_t[i][:, :], [f"ps{2 * i}", f"ps{2 * i + 1}"]

        def sbt(es, name, shape, dt=F32):
            return es.enter_context(nc.sbuf_tensor(name, list(shape), dt))

        IDF = sbt(top, "IDF", [128, 128])
        IDB = sbt(top, "IDB", [128, 128], BF16)
        LD(lambda e: e.dma_start(out=IDF[:], in_=cst["ident"]), w=["IDF"])
        DVE(lambda e: e.tensor_copy(out=IDB[:], in_=IDF[:]), r=["IDF"], w=["IDB"])
        SS = sbt(top, "SS", [128, 8])
        RS = sbt(top, "RS", [128, 8])
        JUNK = sbt(top, "JUNK", [128, 1024], BF16)

        def rms_rstd(src_fn, P, nsub, rkey, wtag):
            DVE(lambda e: e.memset(SS[:P, 0:nsub], 0.0), w=["SS"])
            for s in range(nsub):
                ACT(lambda e, s=s: e.activation(out=JUNK[:P, :], in_=src_fn(s), func=AF.Square,
                                                accum_out=SS[:P, s:s + 1]), r=[rkey, "SS"], w=["JUNK", "SS"])
            DVE(lambda e: e.tensor_scalar(out=RS[:P, 0:nsub], in0=SS[:P, 0:nsub], scalar1=1.0 / DM, scalar2=1e-5,
                                          op0=ALU.mult, op1=ALU.add), r=["SS"], w=["RS"])
            ACT(lambda e: e.activation(out=RS[:P, 0:nsub], in_=RS[:P, 0:nsub], func=AF.Sqrt), r=["RS"], w=["RS"])
            DVE(lambda e: e.reciprocal(out=RS[:P, 0:nsub], in_=RS[:P, 0:nsub]), r=["RS"], w=["RS"])

        def norm_ew(src_fn, P, nsub, rkey, GB, NTK):
            rms_rstd(src_fn, P, nsub, rkey, None)
            for s in range(nsub):
                nb = s % 2
                DVE(lambda e, s=s, nb=nb: e.scalar_tensor_tensor(out=NTK[nb][:P, :], in0=src_fn(s), scalar=RS[:P, s:s + 1],
                                                                 in1=GB[:P, :], op0=ALU.mult, op1=ALU.mult),
                    r=[rkey, "RS", "GB"], w=[("NTK", nb)])

        def norm_tr(P, nsub, NTK, nT, nTkey):
            for s in range(nsub):
                nb = s % 2
                for c in range(8):
                    PE(lambda e, c=c, nb=nb: e.transpose(PSB[:, c * 128:c * 128 + P], NTK[nb][:P, c * 128:(c + 1) * 128],
                                                         IDB[:P, :P]), r=[("NTK", nb), "IDB"], w=["psb"])
                ACT(lambda e, s=s: e.copy(out=nT[:, :, s * 128:s * 128 + P],
                                          in_=PSB[:].rearrange("p (c t) -> p c t", t=128)[:, :, 0:P]),
                    r=["psb"], w=[nTkey])

        def norm_transpose(src_fn, P, nsub, rkey, GB, NTK, nT, nTkey):
            rms_rstd(src_fn, P, nsub, rkey, None)
            for s in range(nsub):
                nb = s % 2
                DVE(lambda e, s=s, nb=nb: e.scalar_tensor_tensor(out=NTK[nb][:P, :], in0=src_fn(s), scalar=RS[:P, s:s + 1],
                                                                 in1=GB[:P, :], op0=ALU.mult, op1=ALU.mult),
                    r=[rkey, "RS", "GB"], w=[("NTK", nb)])
                for c in range(8):
                    PE(lambda e, c=c, nb=nb: e.transpose(PSB[:, c * 128:c * 128 + P], NTK[nb][:P, c * 128:(c + 1) * 128],
                                                         IDB[:P, :P]), r=[("NTK", nb), "IDB"], w=["psb"])
                ACT(lambda e, s=s: e.copy(out=nT[:, :, s * 128:s * 128 + P],
                                          in_=PSB[:].rearrange("p (c t) -> p c t", t=128)[:, :, 0:P]),
                    r=["psb"], w=[nTkey])

        def ffn_phase(tag, tiles, src_fn, dst_fn, wi, gidx, final):
            with ExitStack() as es:
                Wg = sbt(es, "Wg" + tag, [128, 8, FF], BF16)
                Wu = sbt(es, "Wu" + tag, [128, 8, FF], BF16)
                Wd = sbt(es, "Wd" + tag, [128, NFC, DM], BF16)
                GB = sbt(es, "GB" + tag, [128, DM])
                GF = sbt(es, "GF" + tag, [128, DM]) if final else None
                XT = [sbt(es, f"XT{b}" + tag, [128, 2, DM]) for b in range(2)]
                NTK = [sbt(es, f"NTK{b}" + tag, [128, DM], BF16) for b in range(2)]
                nTs = [sbt(es, f"nT{b}" + tag, [128, 8, 256], BF16) for b in range(2)]
                actT = sbt(es, "actT" + tag, [128, NFC, 256], BF16)
                SG = [sbt(es, f"SG{b}" + tag, [128, 256]) for b in range(2)]
                LD(lambda e: e.dma_start(out=GB[:], in_=gvec_d[gidx:gidx + 1, :].partition_broadcast(128)), w=["GB"])
                if final:
                    LD(lambda e: e.dma_start(out=GF[:], in_=gvec_d[3:4, :].partition_broadcast(128)), w=["GF"])

                def load_tile(ti):
                    r0, T = tiles[ti]
                    P = min(T, 128)
                    nsub = (T + 127) // 128
                    b = ti % 2
                    LD(lambda e: e.dma_start(out=XT[b][:P, 0:nsub, :],
                                             in_=src_fn(r0, T).rearrange("(s p) d -> p s d", p=P)), w=[("XT", b)])

                load_tile(0)
                for c in range(8):
                    LDC(lambda e, c=c: e.dma_start(out=Wg[:, c, :], in_=wg_d[wi][c * 128:(c + 1) * 128, :]), w=[("Wg", c)])
                    LDC(lambda e, c=c: e.dma_start(out=Wu[:, c, :], in_=wu_d[wi][c * 128:(c + 1) * 128, :]), w=[("Wu", c)])
                for fc in range(NFC):
                    LDC(lambda e, fc=fc: e.dma_start(out=Wd[:, fc, :], in_=wd_d[wi][fc * 128:(fc + 1) * 128, :]), w=[("Wd", fc)])

                def tile_geo(ti):
                    r0_, T_ = tiles[ti]
                    return min(T_, 128), (T_ + 127) // 128

                P0_, ns0_ = tile_geo(0)
                norm_ew(lambda s: XT[0][:P0_, s, :], P0_, ns0_, ("XT", 0), GB, NTK)
                norm_tr(P0_, ns0_, NTK, nTs[0], ("nT", 0))
                for ti, (r0, T) in enumerate(tiles):
                    P = min(T, 128)
                    nsub = (T + 127) // 128
                    b = ti % 2
                    if ti + 1 < len(tiles):
                        load_tile(ti + 1)
                    xt = XT[b]
                    nT = nTs[b]
                    nTk = ("nT", b)
                    if ti + 1 < len(tiles):
                        Pn_, nsn_ = tile_geo(ti + 1)
                        xn_ = XT[1 - b]
                        norm_ew(lambda s, xn_=xn_, Pn_=Pn_: xn_[:Pn_, s, :], Pn_, nsn_, ("XT", 1 - b), GB, NTK)
                    for fc in range(NFC):
                        pg, kg = ps1()
                        pu, ku = ps1()
                        for c in range(8):
                            PE(lambda e, c=c, fc=fc, pg=pg: e.matmul(pg[:, 0:T], lhsT=Wg[:, c, fc * 128:(fc + 1) * 128],
                                                                     rhs=nT[:, c, 0:T], start=(c == 0), stop=(c == 7)),
                               r=[("Wg", c), nTk], w=[kg])
                        for c in range(8):
                            PE(lambda e, c=c, fc=fc, pu=pu: e.matmul(pu[:, 0:T], lhsT=Wu[:, c, fc * 128:(fc + 1) * 128],
                                                                     rhs=nT[:, c, 0:T], start=(c == 0), stop=(c == 7)),
                               r=[("Wu", c), nTk], w=[ku])
                        sb_ = fc % 2
                        ACT(lambda e, pg=pg, sb_=sb_: e.activation(out=SG[sb_][:, 0:T], in_=pg[:, 0:T], func=AF.Silu),
                            r=[kg], w=[("SG", sb_)])
                        DVE(lambda e, pu=pu, sb_=sb_, fc=fc: e.tensor_tensor(out=actT[:, fc, 0:T], in0=SG[sb_][:, 0:T],
                                                                              in1=pu[:, 0:T], op=ALU.mult),
                            r=[("SG", sb_), ku], w=[("actT", fc)])
                    if ti + 1 < len(tiles):
                        norm_tr(Pn_, nsn_, NTK, nTs[1 - b], ("nT", 1 - b))
                    for s in range(nsub):
                        for hf in range(2):
                            po, ko = ps1()
                            for fc in range(NFC):
                                PE(lambda e, fc=fc, s=s, hf=hf, po=po: e.matmul(
                                    po[:P, :], lhsT=actT[:, fc, s * 128:s * 128 + P], rhs=Wd[:, fc, hf * 512:(hf + 1) * 512],
                                    start=(fc == 0), stop=(fc == NFC - 1)), r=[("actT", fc), ("Wd", fc)], w=[ko])
                            DVE(lambda e, s=s, hf=hf, po=po: e.scalar_tensor_tensor(
                                out=xt[:P, s, hf * 512:(hf + 1) * 512], in0=po[:P, :], scalar=0.5,
                                in1=xt[:P, s, hf * 512:(hf + 1) * 512], op0=ALU.mult, op1=ALU.add),
                                r=[ko, ("XT", b)], w=[("XT", b)])
                    if final:
                        rms_rstd(lambda s: xt[:P, s, :], P, nsub, ("XT", b), None)
                        for s in range(nsub):
                            DVE(lambda e, s=s: e.scalar_tensor_tensor(out=xt[:P, s, :], in0=xt[:P, s, :], scalar=RS[:P, s:s + 1],
                                                                      in1=GF[:P, :], op0=ALU.mult, op1=ALU.mult),
                                r=[("XT", b), "RS", "GF"], w=[("XT", b)])
                    ST(lambda e, r0=r0, T=T, P=P, nsub=nsub, xt=xt: e.dma_start(
                        out=dst_fn(r0, T).rearrange("(s p) d -> p s d", p=P), in_=xt[:P, 0:nsub, :]), r=[("XT", b)], w=[("dst", tag, ti)])
                S.barrier()

        def mix_phase():
            with ExitStack() as es:
                WIN = sbt(es, "WIN", [128, 8, 2464], BF16)
                W2 = sbt(es, "W2", [32, 512])
                A2 = sbt(es, "A2", [32, 512])
                G2 = sbt(es, "G2", [96, 512])
                PPt = sbt(es, "PPt", [64, NPP])
                PL = sbt(es, "PL", [96, 3])
                BVB = sbt(es, "BVB", [64, 128])
                ESK = sbt(es, "ESK", [64, 8])
                GB = sbt(es, "GBm", [128, DM])
                CT = {k: sbt(es, "C_" + k, CONST_SHAPES[k]) for k in CONST_SHAPES if k != "ident"}
                ALMC = sbt(es, "ALMC", [16, 512])
                ONESF = sbt(es, "ONESF", [64, 64])
                ONESB = sbt(es, "ONESB", [64, 64], BF16)
                ONESDB = sbt(es, "ONESDB", [64, 64], BF16)
                T1b = sbt(es, "T1b", [64, 8, 64], BF16)
                YSb = sbt(es, "YSb", [64, 8, 64], BF16)
                Xb = [sbt(es, f"Xb{i}", [64, 512], BF16) for i in range(2)]
                Xtb = [sbt(es, f"Xtb{i}", [64, 512], BF16) for i in range(2)]
                Rtb = [sbt(es, f"Rtb{i}", [64, 512], BF16) for i in range(2)]
                HT = [sbt(es, f"HT{b}", [64, DM]) for b in range(2)]
                NTK = [sbt(es, f"NTKm{b}", [64, DM], BF16) for b in range(2)]
                nT = sbt(es, "nTm", [128, 8, 64], BF16)
                ZR = sbt(es, "ZR", [64, 8, 65])
                ZK = sbt(es, "ZK", [64, 8, 65])
                ZV = sbt(es, "ZV", [64, 8, 65])
                ZW = sbt(es, "ZW", [32, 65])
                ZA = sbt(es, "ZA", [32, 65])
                ZG = sbt(es, "ZG", [96, 65])
                CUM = sbt(es, "CUM", [64, 8, 65])
                QTs = [sbt(es, f"QT{i}", [64, 8, 64], BF16) for i in range(2)]
                KTR = sbt(es, "KTR", [64, 2, 4, 64], BF16)
                KTM = sbt(es, "KTMeta", [64, 2, 16], BF16)
                VR = sbt(es, "VR", [64, 4, 128], BF16)
                VM = sbt(es, "VMeta", [16, 128], BF16)
                YMIX = [sbt(es, f"YMIX{b}", [64, 16, 64], BF16) for b in range(2)]
                STT = sbt(es, "STATE", [64, 8, 64])
                names = ["Rr", "Kz", "Vv", "Gg", "Aa", "KKn", "Kk", "Winc", "Wexc", "Winv", "T1", "T2",
                         "Rt", "Kt", "At", "Bt", "BON", "YS"]
                names += ["S1", "S2", "T3"]
                DB_D = ("At", "Rt", "Winc", "BON", "Gg")
                DB_C = ("AAK", "VTMt", "ARB", "ARK", "KTMt", "BTMt", "R1")
                D0 = {n: sbt(es, "D_" + n, [64, 8, 64]) for n in names}
                Dp = [dict(D0), dict(D0)]
                for n in DB_D:
                    Dp[1][n] = sbt(es, "D1_" + n, [64, 8, 64])
                S1b = sbt(es, "S1b", [64, 8, 64], BF16)
                T3b = sbt(es, "T3b", [64, 8, 64], BF16)
                TW = sbt(es, "TW", [32, 64])
                ZAl = sbt(es, "ZAl", [32, 64])
                SGg = sbt(es, "SGg", [96, 64])
                TL = sbt(es, "TL", [96, 64])
                cn = ["KTMt", "BTMt", "VTMt", "AAK", "ARB", "ARK", "X0", "Xt0", "R0", "R1",
                      "RHS0", "UU"]
                Cc0 = {n: sbt(es, "Cc_" + n, [64, 512]) for n in cn}
                Ccp = [dict(Cc0), dict(Cc0)]
                for n in DB_C:
                    Ccp[1][n] = sbt(es, "Cc1_" + n, [64, 512])
                DBSET = set(DB_D) | set(DB_C) | {"QT"}

                def trw(f_, par):
                    def g(fn, r=(), w=()):
                        f_(fn, [((k, par) if (isinstance(k, str) and k in DBSET) else k) for k in r],
                           [((k, par) if (isinstance(k, str) and k in DBSET) else k) for k in w])
                    return g

                PE0, ACT0, DVE0, ST0 = PE, ACT, DVE, ST
                print("mix phase sbuf bytes remaining/partition:", nc.sbuf_bytes_remaining // 128 if nc.sbuf_bytes_remaining > 1 << 20 else nc.sbuf_bytes_remaining)
                SBT = [sbt(es, "SBT0", [64, 512])] * 2
                PT = [sbt(es, f"PT{i}", [64, 512], BF16) for i in range(4)]
                DEN = sbt(es, "DEN", [64, 512])

                for c in range(8):
                    LDC(lambda e, c=c: e.dma_start(out=WIN[:, c, :], in_=win_d[c * 128:(c + 1) * 128, :]), w=[("WIN", c)])
                LD(lambda e: e.dma_start(out=W2[:], in_=w2_d), w=["W2"])
                LD(lambda e: e.dma_start(out=A2[:], in_=a2_d), w=["A2"])
                LD(lambda e: e.dma_start(out=G2[:], in_=g2_d), w=["G2"])
                LD(lambda e: e.dma_start(out=PPt[:], in_=pp_d), w=["PP"])
                LD(lambda e: e.dma_start(out=PL[:], in_=pl_d), w=["PL"])
                LD(lambda e: e.dma_start(out=BVB[:], in_=bv_d.partition_broadcast(64)), w=["BVB"])
                LD(lambda e: e.dma_start(out=ESK[:], in_=sinks_d.partition_broadcast(64)), w=["ESK"])
                LD(lambda e: e.dma_start(out=GB[:], in_=gvec_d[1:2, :].partition_broadcast(128)), w=["GB"])
                for k in CT:
                    LD(lambda e, k=k: e.dma_start(out=CT[k][:], in_=cst[k]), w=[("CT", k)])
                ACT(lambda e: e.activation(out=ESK[:], in_=ESK[:], func=AF.Exp), r=["ESK"], w=["ESK"])
                DVE(lambda e: e.memset(ONESF[:], 1.0), w=["ONESF"])
                DVE(lambda e: e.memset(ONESB[:], 1.0), w=["ONESB"])
                DVE(lambda e: e.memset(ONESDB[:], 1.0 / 64.0), w=["ONESDB"])
                DVE(lambda e: e.memset(STT[:], 0.0), w=["STATE"])
                for Z, k in ((ZR, "ZR"), (ZK, "ZK"), (ZV, "ZV"), (CUM, "CUM")):
                    DVE(lambda e, Z=Z: e.memset(Z[:, :, 0:1], 0.0), w=[k])
                for Z, k in ((ZW, "ZW"), (ZA, "ZA"), (ZG, "ZG")):
                    DVE(lambda e, Z=Z: e.memset(Z[:, 0:1], 0.0), w=[k])

                MU_R, MU_K, MU_V, W0c, A0c, KKc, KAc, RKc, LNWc, LNBc, BQc, BKc = 0, 8, 16, 24, 32, 40, 48, 56, 64, 72, 80, 88

                def v3(t, rows, cols):
                    a = t if isinstance(t, bass.AP) else t[:]
                    return a.rearrange("p (h t) -> p h t", t=64)[0:rows, :, 0:cols]

                def m3(mask, rows, cols):
                    return CT[mask][0:rows, 0:cols].unsqueeze(1).to_broadcast([rows, 8, cols])

                tiles = [(0, NM)] + [(NM + 64 * i, 64) for i in range(NR // 64)]

                def load_tile(ti):
                    r0, T = tiles[ti]
                    b = ti % 2
                    LD(lambda e: e.dma_start(out=HT[b][0:T, :], in_=h1_d[r0:r0 + T, :]), w=[("HT", b)])

                flags = {}

                def genP1(ti):
                    r0, T = tiles[ti]
                    b = ti % 2
                    par = ti % 2
                    ht = HT[b]
                    ym = YMIX[b]
                    ymk = ("YMIX", b)
                    is_meta = (ti == 0)
                    Ck = T
                    gc = ti - 1
                    D = Dp[par]
                    Cc = Ccp[par]
                    PE, ACT, DVE, ST = (trw(f_, par) for f_ in (PE0, ACT0, DVE0, ST0))
                    QT = QTs[par]
                    PSP = "m"
                    if ti + 1 < len(tiles):
                        load_tile(ti + 1)
                    norm_ew(lambda s: ht[0:T, :], T, 1, ("HT", b), GB, NTK)
                    for _ in range(NORM_GAP):
                        yield
                    norm_tr(T, 1, NTK, nT, "nT")
                    yield
                    def ppb(col):
                        return PPt[:, col:col + 8].unsqueeze(2).to_broadcast([64, 8, T])

                    def d(n):
                        return D[n][:, :, 0:T]

                    def p3(ps, rows=64):
                        return ps.rearrange("p (h t) -> p h t", t=64)[0:rows, :, 0:T]

                    def headsum(srct, skey, lhs, lkey):
                        ps, key = ps1(PSP)
                        if T == 64:
                            PE(lambda e, ps=ps: e.matmul(ps[0:64, :], lhsT=lhs[:, :], rhs=srct[:].rearrange("p h t -> p (h t)"), start=True, stop=True),
                               r=[skey, lkey], w=[key])
                        else:
                            for h in range(8):
                                PE(lambda e, ps=ps, h=h: e.matmul(ps[0:64, h * 64:h * 64 + T], lhsT=lhs[:, :], rhs=srct[:, h, 0:T], start=True, stop=True),
                                   r=[skey, lkey], w=[key])
                        return ps, key

                    def rsqrt_inplace(n):
                        ACT(lambda e: e.activation(out=d(n), in_=d(n), func=AF.Ln), r=[n], w=[n])
                        ACT(lambda e: e.activation(out=d(n), in_=d(n), func=AF.Exp, scale=-0.5), r=[n], w=[n])

                    while ti > 0 and not flags.get(("lerp", ti - 1)) and not getattr(S, "dry", False):
                        yield
                    def proj_heads(col0, nheads, ps, key):
                        for h in range(nheads):
                            for c in range(8):
                                PE(lambda e, h=h, c=c: e.matmul(ps[0:64, h * 64:h * 64 + T],
                                                                lhsT=WIN[:, c, col0 + h * 64:col0 + (h + 1) * 64],
                                                                rhs=nT[:, c, 0:T], start=(c == 0), stop=(c == 7)),
                                   r=[("WIN", c), "nT"], w=[key])

                    for (Z, zk, col0) in ((ZR, "ZR", 0), (ZK, "ZK", 512), (ZV, "ZV", 1024)):
                        ps, key = ps1()
                        proj_heads(col0, 8, ps, key)
                        ACT(lambda e, Z=Z, ps=ps: e.copy(out=Z[:, :, 1:1 + T], in_=p3(ps)), r=[key], w=[zk])
                    ps, key = ps1()
                    proj_heads(1696, 8, ps, key)
                    DVE(lambda e, ps=ps: e.tensor_tensor(out=QT[:, :, 0:T], in0=p3(ps), in1=ppb(BQc), op=ALU.add),
                        r=[key, "PP"], w=["QT"])
                    yield
                    ps, key = ps1()
                    for si, (col0, width) in enumerate(((1536, 32), (1568, 32), (1600, 96))):
                        for c in range(8):
                            PE(lambda e, si=si, c=c, col0=col0, width=width, ps=ps: e.matmul(
                                ps[0:width, si * 64:si * 64 + T], lhsT=WIN[:, c, col0:col0 + width], rhs=nT[:, c, 0:T],
                                start=(c == 0), stop=(c == 7)), r=[("WIN", c), "nT"], w=[key])
                    ACT(lambda e, ps=ps: e.copy(out=ZW[:, 1:1 + T], in_=ps[0:32, 0:T]), r=[key], w=["ZW"])
                    ACT(lambda e, ps=ps: e.copy(out=ZA[:, 1:1 + T], in_=ps[0:32, 64:64 + T]), r=[key], w=["ZA"])
                    ACT(lambda e, ps=ps: e.copy(out=ZG[:, 1:1 + T], in_=ps[0:96, 128:128 + T]), r=[key], w=["ZG"])
                    ps, key = ps1()
                    proj_heads(2208, 2, ps, key)
                    for kv in range(2):
                        if is_meta:
                            ACT(lambda e, kv=kv, ps=ps: e.activation(out=KTM[:, kv, :], in_=ps[0:64, kv * 64:kv * 64 + T],
                                                                     func=AF.Identity, bias=PPt[:, BKc + kv:BKc + kv + 1]),
                                r=[key, "PP"], w=["KTMeta"])
                        else:
                            ACT(lambda e, kv=kv, ps=ps: e.activation(
                                out=KTR[:, kv, gc % 4, :], in_=ps[0:64, kv * 64:kv * 64 + 64],
                                func=AF.Identity, bias=PPt[:, BKc + kv:BKc + kv + 1]),
                                r=[key, "PP"], w=[("KTR", gc % 4)])
                    yield
                    pv, kvk = ps1()
                    for c in range(8):
                        PE(lambda e, c=c, pv=pv: e.matmul(pv[0:T, 0:128], lhsT=nT[:, c, 0:T],
                                                          rhs=WIN[:, c, 2336:2464], start=(c == 0), stop=(c == 7)),
                           r=[("WIN", c), "nT"], w=[kvk])
                    if is_meta:
                        DVE(lambda e, pv=pv: e.tensor_tensor(out=VM[:, :], in0=pv[0:16, 0:128], in1=BVB[0:16, :], op=ALU.add),
                            r=[kvk, "BVB"], w=["VMeta"])
                    else:
                        DVE(lambda e, pv=pv: e.tensor_tensor(out=VR[:, gc % 4, :], in0=pv[0:64, 0:128], in1=BVB[:, :],
                                                             op=ALU.add), r=[kvk, "BVB"], w=[("VR", gc % 4)])


                    yield
                def genP2(ti):
                    r0, T = tiles[ti]
                    b = ti % 2
                    par = ti % 2
                    ht = HT[b]
                    ym = YMIX[b]
                    ymk = ("YMIX", b)
                    is_meta = (ti == 0)
                    Ck = T
                    gc = ti - 1
                    D = Dp[par]
                    Cc = Ccp[par]
                    PE, ACT, DVE, ST = (trw(f_, par) for f_ in (PE0, ACT0, DVE0, ST0))
                    QT = QTs[par]
                    PSP = "m"
                    yield
                    def ppb(col):
                        return PPt[:, col:col + 8].unsqueeze(2).to_broadcast([64, 8, T])

                    def d(n):
                        return D[n][:, :, 0:T]

                    def p3(ps, rows=64):
                        return ps.rearrange("p (h t) -> p h t", t=64)[0:rows, :, 0:T]

                    def headsum(srct, skey, lhs, lkey):
                        ps, key = ps1(PSP)
                        if T == 64:
                            PE(lambda e, ps=ps: e.matmul(ps[0:64, :], lhsT=lhs[:, :], rhs=srct[:].rearrange("p h t -> p (h t)"), start=True, stop=True),
                               r=[skey, lkey], w=[key])
                        else:
                            for h in range(8):
                                PE(lambda e, ps=ps, h=h: e.matmul(ps[0:64, h * 64:h * 64 + T], lhsT=lhs[:, :], rhs=srct[:, h, 0:T], start=True, stop=True),
                                   r=[skey, lkey], w=[key])
                        return ps, key

                    def rsqrt_inplace(n):
                        ACT(lambda e: e.activation(out=d(n), in_=d(n), func=AF.Ln), r=[n], w=[n])
                        ACT(lambda e: e.activation(out=d(n), in_=d(n), func=AF.Exp, scale=-0.5), r=[n], w=[n])

                    def lerp3(Z, zk, mucol, dn):
                        DVE(lambda e: e.tensor_tensor(out=d("T1"), in0=Z[:, :, 0:T], in1=Z[:, :, 1:1 + T], op=ALU.subtract),
                            r=[zk], w=["T1"])
                        DVE(lambda e: e.tensor_tensor(out=d("T1"), in0=d("T1"), in1=ppb(mucol), op=ALU.mult),
                            r=["T1", "PP"], w=["T1"])
                        DVE(lambda e: e.tensor_tensor(out=d(dn), in0=d("T1"), in1=Z[:, :, 1:1 + T], op=ALU.add),
                            r=["T1", zk], w=[dn])
                        DVE(lambda e: e.tensor_copy(out=Z[:, :, 0:1], in_=Z[:, :, T:T + 1]), r=[zk], w=[zk])

                    lerp3(ZR, "ZR", MU_R, "Rr")
                    yield
                    lerp3(ZK, "ZK", MU_K, "Kz")
                    yield
                    lerp3(ZV, "ZV", MU_V, "Vv")
                    yield

                    def lerp2(Z, zk, rows, plcol, dst, dk):
                        DVE(lambda e: e.tensor_tensor(out=TL[0:rows, 0:T], in0=Z[:, 0:T], in1=Z[:, 1:1 + T], op=ALU.subtract),
                            r=[zk], w=["TL"])
                        DVE(lambda e: e.scalar_tensor_tensor(out=dst[0:rows, 0:T], in0=TL[0:rows, 0:T], scalar=PL[0:rows, plcol:plcol + 1],
                                                             in1=Z[:, 1:1 + T], op0=ALU.mult, op1=ALU.add), r=["TL", "PL", zk], w=[dk])
                        DVE(lambda e: e.tensor_copy(out=Z[:, 0:1], in_=Z[:, T:T + 1]), r=[zk], w=[zk])

                    lerp2(ZW, "ZW", 32, 0, TW, "TW")
                    yield
                    lerp2(ZA, "ZA", 32, 1, ZAl, "ZAl")
                    yield
                    lerp2(ZG, "ZG", 96, 2, SGg, "SGg")
                    yield
                    ACT(lambda e: e.activation(out=TW[:, 0:T], in_=TW[:, 0:T], func=AF.Tanh), r=["TW"], w=["TW"])
                    ACT(lambda e: e.activation(out=SGg[:, 0:T], in_=SGg[:, 0:T], func=AF.Sigmoid), r=["SGg"], w=["SGg"])
                    flags[("lerp", ti)] = True

                    def lora(Wt, wkey, xin, xkey, rows):
                        ps, key = ps1()
                        for h in range(8):
                            PE(lambda e, h=h, ps=ps: e.matmul(ps[0:64, h * 64:h * 64 + T], lhsT=Wt[0:rows, h * 64:(h + 1) * 64],
                                                              rhs=xin[0:rows, 0:T], start=True, stop=True), r=[wkey, xkey], w=[key])
                        return ps, key

                    yield
                    ps, key = lora(W2, "W2", TW, "TW", 32)
                    DVE(lambda e, ps=ps: e.tensor_tensor(out=d("T2"), in0=p3(ps), in1=ppb(W0c), op=ALU.add),
                        r=[key, "PP"], w=["T2"])
                    yield
                    ACT(lambda e: e.activation(out=d("T2"), in_=d("T2"), func=AF.Sigmoid), r=["T2"], w=["T2"])
                    for h in range(8):
                        DVE(lambda e, h=h: e.tensor_tensor_scan(out=CUM[:, h, 1:1 + T], data0=ONESF[:, 0:T], data1=D["T2"][:, h, 0:T],
                                                                initial=0.0, op0=ALU.mult, op1=ALU.add),
                            r=["T2", "ONESF"], w=["CUM"])
                    ACT(lambda e: e.activation(out=d("Winv"), in_=CUM[:, :, 1:1 + T], func=AF.Exp, scale=DEC), r=["CUM"], w=["Winv"])
                    ACT(lambda e: e.activation(out=d("Winc"), in_=CUM[:, :, 1:1 + T], func=AF.Exp, scale=-DEC), r=["CUM"], w=["Winc"])
                    ACT(lambda e: e.activation(out=d("Wexc"), in_=CUM[:, :, 0:T], func=AF.Exp, scale=-DEC), r=["CUM"], w=["Wexc"])
                    flags[("decay", ti)] = True
                    yield
                    ps, key = lora(A2, "A2", ZAl, "ZAl", 32)
                    DVE(lambda e, ps=ps: e.tensor_tensor(out=d("Aa"), in0=p3(ps), in1=ppb(A0c), op=ALU.add),
                        r=[key, "PP"], w=["Aa"])
                    yield
                    ACT(lambda e: e.activation(out=d("Aa"), in_=d("Aa"), func=AF.Sigmoid), r=["Aa"], w=["Aa"])
                    flags[("a", ti)] = True
                    DVE(lambda e: e.tensor_tensor(out=d("KKn"), in0=d("Kz"), in1=ppb(KKc), op=ALU.mult), r=["Kz", "PP"], w=["KKn"])
                    ACT(lambda e: e.activation(out=T1b[:, :, 0:T], in_=d("KKn"), func=AF.Square), r=["KKn"], w=["T1b"])

                    yield
                    ps, key = headsum(T1b, "T1b", ONESB, "ONESB")
                    DVE(lambda e, ps=ps: e.tensor_scalar(out=d("T2"), in0=p3(ps), scalar1=1e-24, scalar2=None, op0=ALU.max),
                        r=[key], w=["T2"])
                    yield
                    rsqrt_inplace("T2")
                    yield
                    DVE(lambda e: e.tensor_tensor(out=d("KKn"), in0=d("KKn"), in1=d("T2"), op=ALU.mult), r=["KKn", "T2"], w=["KKn"])
                    DVE(lambda e: e.scalar_tensor_tensor(out=d("At"), in0=d("KKn"), scalar=-1.0, in1=d("Wexc"), op0=ALU.mult, op1=ALU.mult),
                        r=["KKn", "Wexc"], w=["At"])
                    DVE(lambda e: e.tensor_tensor(out=d("Bt"), in0=d("KKn"), in1=d("Aa"), op=ALU.mult), r=["KKn", "Aa"], w=["Bt"])
                    DVE(lambda e: e.tensor_tensor(out=d("Bt"), in0=d("Bt"), in1=d("Winv"), op=ALU.mult), r=["Bt", "Winv"], w=["Bt"])


                    flags[("prep", ti)] = True
                    yield
                    def amat(lh, rh, mask, dstn):
                        pa, ka = ps1()
                        for h in range(8):
                            PE(lambda e, h=h, pa=pa: e.matmul(pa[0:Ck, h * 64:h * 64 + Ck], lhsT=D[lh][:, h, 0:T], rhs=D[rh][:, h, 0:T],
                                                              start=True, stop=True), r=[lh, rh], w=[ka])
                        DVE(lambda e, pa=pa: e.tensor_tensor(out=v3(Cc[dstn], Ck, Ck), in0=v3(pa, Ck, Ck),
                                                             in1=m3(mask, Ck, Ck), op=ALU.mult), r=[ka, ("CT", mask)], w=[dstn])

                    amat("Bt", "At", "msu", "X0")
                    yield
                    amat("At", "Bt", "msl", "Xt0")
                    yield
                    DVE(lambda e: e.tensor_tensor(out=v3(Cc["R0"], Ck, Ck), in0=v3(Cc["X0"], Ck, Ck), in1=m3("idt", Ck, Ck), op=ALU.add),
                        r=["X0", ("CT", "idt")], w=["R0"])
                    ACT(lambda e: e.copy(out=v3(Rtb[0], Ck, Ck), in_=v3(Cc["R0"], Ck, Ck)), r=["R0"], w=[("Rtb", 0)])
                    nl = 5 if Ck == 64 else 3
                    cur = 0

                    def sq(lt, lk, rt, rk):
                        pa, ka = ps1()
                        for h in range(8):
                            PE(lambda e, h=h, pa=pa: e.matmul(pa[0:Ck, h * 64:h * 64 + Ck], lhsT=lt[0:Ck, h * 64:h * 64 + Ck],
                                                              rhs=rt[0:Ck, h * 64:h * 64 + Ck], start=True, stop=True),
                               r=[lk, rk], w=[ka])
                        return pa, ka

                    for lvl in range(1, nl + 1):
                        nx = 1 - cur
                        if lvl == 1:
                            Xo, Xok, Xto, Xtok = Cc["X0"], "X0", Cc["Xt0"], "Xt0"
                        else:
                            Xo, Xok, Xto, Xtok = Xb[cur], ("Xb", cur), Xtb[cur], ("Xtb", cur)
                        pa, ka = sq(Xo, Xok, Xto, Xtok)
                        ACT(lambda e, pa=pa, nx=nx: e.copy(out=v3(Xtb[nx], Ck, Ck), in_=v3(pa, Ck, Ck)), r=[ka], w=[("Xtb", nx)])
                        yield
                        if lvl < nl:
                            pa, ka = sq(Xto, Xtok, Xo, Xok)
                            ACT(lambda e, pa=pa, nx=nx: e.copy(out=v3(Xb[nx], Ck, Ck), in_=v3(pa, Ck, Ck)), r=[ka], w=[("Xb", nx)])
                            yield
                        R, Rn = f"R{cur}", f"R{nx}"
                        pa, ka = sq(Xtb[nx], ("Xtb", nx), Rtb[cur], ("Rtb", cur))
                        DVE(lambda e, pa=pa, R=R, Rn=Rn: e.tensor_tensor(out=v3(Cc[Rn], Ck, Ck), in0=v3(pa, Ck, Ck),
                                                                         in1=v3(Cc[R], Ck, Ck), op=ALU.add), r=[ka, R], w=[Rn])
                        if lvl < nl:
                            ACT(lambda e, Rn=Rn, nx=nx: e.copy(out=v3(Rtb[nx], Ck, Ck), in_=v3(Cc[Rn], Ck, Ck)), r=[Rn], w=[("Rtb", nx)])
                        yield
                        cur = nx
                    Rfin = f"R{cur}"

                    yield
                def genP2b(ti):
                    r0, T = tiles[ti]
                    b = ti % 2
                    par = ti % 2
                    ht = HT[b]
                    ym = YMIX[b]
                    ymk = ("YMIX", b)
                    is_meta = (ti == 0)
                    Ck = T
                    gc = ti - 1
                    D = Dp[par]
                    Cc = Ccp[par]
                    PE, ACT, DVE, ST = (trw(f_, par) for f_ in (PE0, ACT0, DVE0, ST0))
                    QT = QTs[par]
                    PSP = "m"
                    yield
                    def ppb(col):
                        return PPt[:, col:col + 8].unsqueeze(2).to_broadcast([64, 8, T])

                    def d(n):
                        return D[n][:, :, 0:T]

                    def p3(ps, rows=64):
                        return ps.rearrange("p (h t) -> p h t", t=64)[0:rows, :, 0:T]

                    def headsum(srct, skey, lhs, lkey):
                        ps, key = ps1(PSP)
                        if T == 64:
                            PE(lambda e, ps=ps: e.matmul(ps[0:64, :], lhsT=lhs[:, :], rhs=srct[:].rearrange("p h t -> p (h t)"), start=True, stop=True),
                               r=[skey, lkey], w=[key])
                        else:
                            for h in range(8):
                                PE(lambda e, ps=ps, h=h: e.matmul(ps[0:64, h * 64:h * 64 + T], lhsT=lhs[:, :], rhs=srct[:, h, 0:T], start=True, stop=True),
                                   r=[skey, lkey], w=[key])
                        return ps, key

                    def rsqrt_inplace(n):
                        ACT(lambda e: e.activation(out=d(n), in_=d(n), func=AF.Ln), r=[n], w=[n])
                        ACT(lambda e: e.activation(out=d(n), in_=d(n), func=AF.Exp, scale=-0.5), r=[n], w=[n])

                    def lora(Wt, wkey, xin, xkey, rows):
                        ps, key = ps1()
                        for h in range(8):
                            PE(lambda e, h=h, ps=ps: e.matmul(ps[0:64, h * 64:h * 64 + T], lhsT=Wt[0:rows, h * 64:(h + 1) * 64],
                                                              rhs=xin[0:rows, 0:T], start=True, stop=True), r=[wkey, xkey], w=[key])
                        return ps, key

                    def waitf(name):
                        while not flags.get((name, ti)) and not getattr(S, "dry", False):
                            yield

                    def tr_tm(src, dstn):
                        pt, kt = ps1()
                        for h in range(8):
                            PE(lambda e, h=h, src=src, pt=pt: e.transpose(pt[0:Ck, h * 64:(h + 1) * 64], D[src][:, h, 0:T], IDF[0:64, 0:64]),
                               r=[src, "IDF"], w=[kt])
                        ACT(lambda e, pt=pt, dstn=dstn: e.copy(out=Cc[dstn][0:Ck, :], in_=pt[0:Ck, :]), r=[kt], w=[dstn])

                    def amat(lh, rh, mask, dstn):
                        pa, ka = ps1()
                        for h in range(8):
                            PE(lambda e, h=h, pa=pa: e.matmul(pa[0:Ck, h * 64:h * 64 + Ck], lhsT=D[lh][:, h, 0:T], rhs=D[rh][:, h, 0:T],
                                                              start=True, stop=True), r=[lh, rh], w=[ka])
                        DVE(lambda e, pa=pa: e.tensor_tensor(out=v3(Cc[dstn], Ck, Ck), in0=v3(pa, Ck, Ck),
                                                             in1=m3(mask, Ck, Ck), op=ALU.mult), r=[ka, ("CT", mask)], w=[dstn])

                    yield from waitf("lerp")
                    ps, key = lora(G2, "G2", SGg, "SGg", 96)
                    ACT(lambda e, ps=ps: e.copy(out=d("Gg"), in_=p3(ps)), r=[key], w=["Gg"])
                    yield
                    tr_tm("Vv", "VTMt")
                    yield
                    yield from waitf("decay")
                    DVE(lambda e: e.tensor_tensor(out=d("Rt"), in0=d("Rr"), in1=d("Winc"), op=ALU.mult), r=["Rr", "Winc"], w=["Rt"])
                    yield
                    yield from waitf("a")
                    DVE(lambda e: e.scalar_tensor_tensor(out=d("T3"), in0=d("Aa"), scalar=-1.0, in1=ppb(KAc), op0=ALU.add, op1=ALU.mult),
                        r=["Aa", "PP"], w=["T3"])
                    DVE(lambda e: e.scalar_tensor_tensor(out=d("Kk"), in0=d("T3"), scalar=1.0, in1=d("Kz"), op0=ALU.add, op1=ALU.mult),
                        r=["T3", "Kz"], w=["Kk"])
                    DVE(lambda e: e.tensor_tensor(out=d("Kt"), in0=d("Kk"), in1=d("Winv"), op=ALU.mult), r=["Kk", "Winv"], w=["Kt"])
                    DVE(lambda e: e.tensor_tensor(out=d("T3"), in0=d("Rr"), in1=d("Kk"), op=ALU.mult), r=["Rr", "Kk"], w=["T3"])
                    DVE(lambda e: e.tensor_tensor(out=T3b[:, :, 0:T], in0=d("T3"), in1=ppb(RKc), op=ALU.mult), r=["T3", "PP"], w=["T3b"])
                    yield
                    tr_tm("Kt", "KTMt")
                    yield
                    ps, key = headsum(T3b, "T3b", ONESB, "ONESB")
                    DVE(lambda e, ps=ps: e.tensor_tensor(out=d("BON"), in0=p3(ps), in1=d("Vv"), op=ALU.mult), r=[key, "Vv"], w=["BON"])
                    yield
                    amat("Kt", "Rt", "mui", "ARK")
                    yield
                    yield from waitf("prep")
                    tr_tm("Bt", "BTMt")
                    yield
                    amat("Kt", "At", "msu", "AAK")
                    yield
                    amat("Bt", "Rt", "mui", "ARB")
                    yield
                def genAtt(ti):
                    r0, T = tiles[ti]
                    b = ti % 2
                    par = ti % 2
                    ht = HT[b]
                    ym = YMIX[b]
                    ymk = ("YMIX", b)
                    is_meta = (ti == 0)
                    Ck = T
                    gc = ti - 1
                    D = Dp[par]
                    Cc = Ccp[par]
                    PE, ACT, DVE, ST = (trw(f_, par) for f_ in (PE0, ACT0, DVE0, ST0))
                    QT = QTs[par]
                    PSP = "m"
                    yield
                    def ppb(col):
                        return PPt[:, col:col + 8].unsqueeze(2).to_broadcast([64, 8, T])

                    def d(n):
                        return D[n][:, :, 0:T]

                    def p3(ps, rows=64):
                        return ps.rearrange("p (h t) -> p h t", t=64)[0:rows, :, 0:T]

                    def headsum(srct, skey, lhs, lkey):
                        ps, key = ps1(PSP)
                        if T == 64:
                            PE(lambda e, ps=ps: e.matmul(ps[0:64, :], lhsT=lhs[:, :], rhs=srct[:].rearrange("p h t -> p (h t)"), start=True, stop=True),
                               r=[skey, lkey], w=[key])
                        else:
                            for h in range(8):
                                PE(lambda e, ps=ps, h=h: e.matmul(ps[0:64, h * 64:h * 64 + T], lhsT=lhs[:, :], rhs=srct[:, h, 0:T], start=True, stop=True),
                                   r=[skey, lkey], w=[key])
                        return ps, key

                    def rsqrt_inplace(n):
                        ACT(lambda e: e.activation(out=d(n), in_=d(n), func=AF.Ln), r=[n], w=[n])
                        ACT(lambda e: e.activation(out=d(n), in_=d(n), func=AF.Exp, scale=-0.5), r=[n], w=[n])

                    if is_meta:
                        blocks = [(lambda kv: KTM[:, kv, :], "KTMeta", lambda kv: VM[0:16, kv * 64:(kv + 1) * 64], "VMeta", 16,
                                   CT["almm"][:].rearrange("p (h t) -> p h t", t=16), ("CT", "almm"))]
                    else:
                        DVE(lambda e: e.scalar_tensor_tensor(out=ALMC[:, :], in0=CT["almd"][:, :], scalar=float(gc), in1=CT["alm0"][:, :],
                                                             op0=ALU.mult, op1=ALU.add), r=[("CT", "almd"), ("CT", "alm0")], w=["ALMC"])
                        blocks = []
                        for back, aln in ((2, "al2"), (1, "al1"), (0, "al0")):
                            g2_ = gc - back
                            if g2_ < 0:
                                continue
                            sl_ = g2_ % 4
                            blocks.append((lambda kv, sl_=sl_: KTR[:, kv, sl_, :], ("KTR", sl_),
                                           lambda kv, sl_=sl_: VR[:, sl_, kv * 64:(kv + 1) * 64], ("VR", sl_), 64,
                                           CT[aln][:].rearrange("p (h t) -> p h t", t=64), ("CT", aln)))
                        blocks.append((lambda kv: KTM[:, kv, :], "KTMeta", lambda kv: VM[0:16, kv * 64:(kv + 1) * 64], "VMeta", 16,
                                       ALMC[:].rearrange("p (h t) -> p h t", t=64), "ALMC"))
                    nq = Ck
                    pts = []
                    for bi, (kf, kkey, vf, vkey, nk, bias3, bkey) in enumerate(blocks):
                        pa, ka = ps1()
                        for h in range(8):
                            PE(lambda e, h=h, pa=pa, kf=kf, nk=nk: e.matmul(pa[0:nk, h * 64:h * 64 + nq], lhsT=kf(h // 4), rhs=QT[:, h, 0:T],
                                                                            start=True, stop=True), r=[kkey, "QT"], w=[ka])
                        sbt_ = SBT[bi % 2]
                        DVE(lambda e, pa=pa, nk=nk, bias3=bias3, sbt_=sbt_: e.scalar_tensor_tensor(
                            out=v3(sbt_, nk, nq), in0=v3(pa, nk, nq), scalar=0.125,
                            in1=bias3[0:nk, :, 0:nq], op0=ALU.mult, op1=ALU.add), r=[ka, bkey], w=[("SBT", 0)])
                        ACT(lambda e, nk=nk, sbt_=sbt_, bi=bi: e.activation(out=v3(PT[bi], nk, nq), in_=v3(sbt_, nk, nq), func=AF.Exp),
                            r=[("SBT", 0)], w=[("PT", bi)])
                        pts.append((bi, nk, vf, vkey))
                        yield
                    pd, kd = ps1()
                    po, ko = ps1()
                    nb_ = len(pts)
                    for i_, (bi, nk, vf, vkey) in enumerate(pts):
                        if nq == 64:
                            PE(lambda e, bi=bi, nk=nk, pd=pd, i_=i_: e.matmul(
                                pd[0:64, :], lhsT=ONESB[0:nk, :], rhs=PT[bi][0:nk, :],
                                start=(i_ == 0), stop=(i_ == nb_ - 1)), r=[("PT", bi), "ONESB"], w=[kd])
                        else:
                            for h in range(8):
                                PE(lambda e, bi=bi, nk=nk, pd=pd, i_=i_, h=h: e.matmul(
                                    pd[0:64, h * 64:h * 64 + nq], lhsT=ONESB[0:nk, :], rhs=PT[bi][0:nk, h * 64:h * 64 + nq],
                                    start=(i_ == 0), stop=(i_ == nb_ - 1)), r=[("PT", bi), "ONESB"], w=[kd])
                    for kv in range(2):
                        for i_, (bi, nk, vf, vkey) in enumerate(pts):
                            if nq == 64:
                                PE(lambda e, bi=bi, nk=nk, po=po, i_=i_, kv=kv, vf=vf: e.matmul(
                                    po[0:64, kv * 256:(kv + 1) * 256], lhsT=vf(kv),
                                    rhs=PT[bi][0:nk, kv * 256:(kv + 1) * 256], start=(i_ == 0), stop=(i_ == nb_ - 1)),
                                   r=[("PT", bi), vkey], w=[ko])
                            else:
                                for hh in range(4):
                                    h = kv * 4 + hh
                                    PE(lambda e, bi=bi, nk=nk, po=po, i_=i_, kv=kv, vf=vf, h=h: e.matmul(
                                        po[0:64, h * 64:h * 64 + nq], lhsT=vf(kv),
                                        rhs=PT[bi][0:nk, h * 64:h * 64 + nq], start=(i_ == 0), stop=(i_ == nb_ - 1)),
                                       r=[("PT", bi), vkey], w=[ko])
                    DVE(lambda e, pd=pd: e.tensor_tensor(out=v3(DEN, 64, nq), in0=v3(pd, 64, nq),
                                                         in1=ESK[:, :].unsqueeze(2).to_broadcast([64, 8, nq]), op=ALU.add),
                        r=[kd, "ESK"], w=["DEN"])
                    ACT(lambda e: e.activation(out=v3(DEN, 64, nq), in_=v3(DEN, 64, nq), func=AF.Ln), r=["DEN"], w=["DEN"])
                    ACT(lambda e: e.activation(out=v3(DEN, 64, nq), in_=v3(DEN, 64, nq), func=AF.Exp, scale=-1.0), r=["DEN"], w=["DEN"])
                    DVE(lambda e, po=po: e.tensor_tensor(out=ym[:, 8:16, 0:T], in0=v3(po, 64, nq),
                                                         in1=v3(DEN, 64, nq), op=ALU.mult), r=[ko, "DEN"], w=[ymk])


                    yield
                def genS(ti):
                    r0, T = tiles[ti]
                    b = ti % 2
                    par = ti % 2
                    ht = HT[b]
                    ym = YMIX[b]
                    ymk = ("YMIX", b)
                    is_meta = (ti == 0)
                    Ck = T
                    gc = ti - 1
                    D = Dp[par]
                    Cc = Ccp[par]
                    PE, ACT, DVE, ST = (trw(f_, par) for f_ in (PE0, ACT0, DVE0, ST0))
                    QT = QTs[par]
                    PSP = "s"
                    Rfin = "R1"
                    def ppb(col):
                        return PPt[:, col:col + 8].unsqueeze(2).to_broadcast([64, 8, T])

                    def d(n):
                        return D[n][:, :, 0:T]

                    def p3(ps, rows=64):
                        return ps.rearrange("p (h t) -> p h t", t=64)[0:rows, :, 0:T]

                    def headsum(srct, skey, lhs, lkey):
                        ps, key = ps1(PSP)
                        if T == 64:
                            PE(lambda e, ps=ps: e.matmul(ps[0:64, :], lhsT=lhs[:, :], rhs=srct[:].rearrange("p h t -> p (h t)"), start=True, stop=True),
                               r=[skey, lkey], w=[key])
                        else:
                            for h in range(8):
                                PE(lambda e, ps=ps, h=h: e.matmul(ps[0:64, h * 64:h * 64 + T], lhsT=lhs[:, :], rhs=srct[:, h, 0:T], start=True, stop=True),
                                   r=[skey, lkey], w=[key])
                        return ps, key

                    def rsqrt_inplace(n):
                        ACT(lambda e: e.activation(out=d(n), in_=d(n), func=AF.Ln), r=[n], w=[n])
                        ACT(lambda e: e.activation(out=d(n), in_=d(n), func=AF.Exp, scale=-0.5), r=[n], w=[n])

                    pa, ka = ps1("s")
                    for h in range(8):
                        PE(lambda e, h=h, pa=pa: e.matmul(pa[0:Ck, h * 64:(h + 1) * 64], lhsT=D["At"][:, h, 0:T], rhs=STT[:, h, :],
                                                          start=True, stop=False), r=["At", "STATE"], w=[ka])
                        PE(lambda e, h=h, pa=pa: e.matmul(pa[0:Ck, h * 64:(h + 1) * 64], lhsT=Cc["AAK"][0:Ck, h * 64:h * 64 + Ck],
                                                          rhs=Cc["VTMt"][0:Ck, h * 64:(h + 1) * 64], start=False, stop=True),
                           r=["AAK", "VTMt"], w=[ka])
                    ACT(lambda e, pa=pa: e.copy(out=Cc["RHS0"][0:Ck, :], in_=pa[0:Ck, :]), r=[ka], w=["RHS0"])
                    yield
                    pa, ka = ps1("s")
                    for h in range(8):
                        PE(lambda e, h=h, pa=pa: e.matmul(pa[0:Ck, h * 64:(h + 1) * 64], lhsT=Cc[Rfin][0:Ck, h * 64:h * 64 + Ck],
                                                          rhs=Cc["RHS0"][0:Ck, h * 64:(h + 1) * 64], start=True, stop=True),
                           r=[Rfin, "RHS0"], w=[ka])
                    DVE(lambda e, pa=pa: e.tensor_copy(out=Cc["UU"][0:Ck, :], in_=pa[0:Ck, :]), r=[ka], w=["UU"])
                    yield
                    py, ky = ps1("s")
                    for h in range(8):
                        PE(lambda e, h=h, py=py: e.matmul(py[0:64, h * 64:h * 64 + Ck], lhsT=STT[:, h, :], rhs=D["Rt"][:, h, 0:T],
                                                          start=True, stop=False), r=["STATE", "Rt"], w=[ky])
                        PE(lambda e, h=h, py=py: e.matmul(py[0:64, h * 64:h * 64 + Ck], lhsT=Cc["UU"][0:Ck, h * 64:(h + 1) * 64],
                                                          rhs=Cc["ARB"][0:Ck, h * 64:h * 64 + Ck], start=False, stop=False),
                           r=["UU", "ARB"], w=[ky])
                        PE(lambda e, h=h, py=py: e.matmul(py[0:64, h * 64:h * 64 + Ck], lhsT=Cc["VTMt"][0:Ck, h * 64:(h + 1) * 64],
                                                          rhs=Cc["ARK"][0:Ck, h * 64:h * 64 + Ck], start=False, stop=True),
                           r=["VTMt", "ARK"], w=[ky])
                    ACT(lambda e, py=py: e.copy(out=d("YS"), in_=p3(py)), r=[ky], w=["YS"])
                    ACT(lambda e, py=py: e.copy(out=YSb[:, :, 0:T], in_=p3(py)), r=[ky], w=["YSb"])
                    yield
                    pst, kst = ps1("s")
                    for h in range(8):
                        PE(lambda e, h=h, pst=pst: e.matmul(pst[0:64, h * 64:(h + 1) * 64], lhsT=Cc["KTMt"][0:Ck, h * 64:(h + 1) * 64],
                                                            rhs=Cc["VTMt"][0:Ck, h * 64:(h + 1) * 64], start=True, stop=False),
                           r=["KTMt", "VTMt"], w=[kst])
                        PE(lambda e, h=h, pst=pst: e.matmul(pst[0:64, h * 64:(h + 1) * 64], lhsT=Cc["BTMt"][0:Ck, h * 64:(h + 1) * 64],
                                                            rhs=Cc["UU"][0:Ck, h * 64:(h + 1) * 64], start=False, stop=True),
                           r=["BTMt", "UU"], w=[kst])
                    DVE(lambda e, pst=pst: e.tensor_tensor(out=Cc["RHS0"][:, :], in0=pst[0:64, :], in1=STT[:].rearrange("p h v -> p (h v)"),
                                                           op=ALU.add), r=[kst, "STATE"], w=["RHS0"])
                    DVE(lambda e: e.tensor_tensor(out=STT[:, :, :], in0=Cc["RHS0"][:].rearrange("p (h v) -> p h v", v=64),
                                                  in1=D["Winc"][:, :, T - 1:T].to_broadcast([64, 8, 64]), op=ALU.mult),
                        r=["RHS0", "Winc"], w=["STATE"])


                    yield
                    yield
                    ps, key = headsum(YSb, "YSb", ONESDB, "ONESDB")
                    DVE(lambda e, ps=ps: e.tensor_tensor(out=d("S1"), in0=d("YS"), in1=p3(ps), op=ALU.subtract), r=[key, "YS"], w=["S1"])
                    ACT(lambda e: e.activation(out=S1b[:, :, 0:T], in_=d("S1"), func=AF.Square), r=["S1"], w=["S1b"])
                    yield
                    ps, key = headsum(S1b, "S1b", ONESDB, "ONESDB")
                    DVE(lambda e, ps=ps: e.tensor_scalar(out=d("S2"), in0=p3(ps), scalar1=64e-5, scalar2=None, op0=ALU.add), r=[key], w=["S2"])
                    yield
                    rsqrt_inplace("S2")
                    yield
                    DVE(lambda e: e.tensor_tensor(out=d("S1"), in0=d("S1"), in1=d("S2"), op=ALU.mult), r=["S1", "S2"], w=["S1"])
                    DVE(lambda e: e.tensor_tensor(out=d("S1"), in0=d("S1"), in1=ppb(LNWc), op=ALU.mult), r=["S1", "PP"], w=["S1"])
                    DVE(lambda e: e.tensor_tensor(out=d("S1"), in0=d("S1"), in1=ppb(LNBc), op=ALU.add), r=["S1", "PP"], w=["S1"])
                    DVE(lambda e: e.tensor_tensor(out=d("S1"), in0=d("S1"), in1=d("BON"), op=ALU.add), r=["S1", "BON"], w=["S1"])
                    DVE(lambda e: e.tensor_tensor(out=ym[:, 0:8, 0:T], in0=d("S1"), in1=d("Gg"), op=ALU.mult), r=["S1", "Gg"], w=[ymk])
                    ST(lambda e, r0=r0, T=T, ym=ym: e.dma_start(out=ym_d[:, r0:r0 + T].rearrange("(g p) t -> p g t", p=64), in_=ym[:, :, 0:T]),
                       r=[ymk], w=[("ymd", ti)])
                    yield
                def drive_dyn(gens):
                    live = [[g, 0.0] for g, _ in gens]
                    while live:
                        tpe = S.tfree["tensor"] + SLACK
                        cands = [it for it in live if it[1] + S.lat <= tpe]
                        item = cands[0] if cands else min(live, key=lambda it: it[1])
                        S.step_fin = 0.0
                        try:
                            next(item[0])
                            if S.step_fin > 0.0:
                                item[1] = S.step_fin
                            else:
                                item[1] += 0.3
                        except StopIteration:
                            live.remove(item)

                def count_steps(genf, ti):
                    S.dry = True
                    sv = dict(rot)
                    fl = dict(flags)
                    n = 0
                    for _ in genf(ti):
                        n += 1
                    rot.update(sv)
                    flags.clear()
                    flags.update(fl)
                    S.dry = False
                    return n + 1

                def drive_frac(gens):
                    live = [[g, 0, float(tot)] for g, tot in gens]
                    while live:
                        item = min(live, key=lambda it: (it[1] + BIAS.get(id(it[0]), 0.0)) / it[2])
                        try:
                            next(item[0])
                            item[1] += 1
                        except StopIteration:
                            live.remove(item)

                BIAS = {}

                def drive(gens):
                    live = [[g, n] for g, n in gens]
                    while live:
                        for item in list(live):
                            g, n = item
                            for _ in range(n):
                                try:
                                    next(g)
                                except StopIteration:
                                    live.remove(item)
                                    break

                rot["pipe"] = True
                load_tile(0)
                nt_ = len(tiles)
                drive([(genP1(0), 1)])
                drive([(genP2(0), 1), (genP2b(0), 1), (genAtt(0), 1)] + ([(genP1(1), 1)] if nt_ > 1 else []))
                for ti in range(nt_):
                    if FRAC:
                        gl = [(genS(ti), count_steps(genS, ti) * FW[0])]
                        if ti + 1 < nt_:
                            gl += [(genP2(ti + 1), count_steps(genP2, ti + 1) * FW[1]), (genAtt(ti + 1), count_steps(genAtt, ti + 1) * FW[2])]
                        if ti + 2 < nt_:
                            gl += [(genP1(ti + 2), count_steps(genP1, ti + 2) * FW[3])]
                        drive_frac(gl)
                        continue
                    gl = [(genS(ti), WGT[0])]
                    if ti + 1 < nt_:
                        gl += [(genP2(ti + 1), WGT[1]), (genP2b(ti + 1), WGT[4]), (genAtt(ti + 1), WGT[2])]
                    if ti + 2 < nt_:
                        gl += [(genP1(ti + 2), WGT[3])]
                    (drive_dyn if DYN else drive)(gl)
                rot["pipe"] = False
                S.barrier()

        def outproj_phase():
            with ExitStack() as es:
                WOUT = sbt(es, "WOUT", [128, 8, DM], BF16)
                HT = [sbt(es, f"HTo{b}", [128, 2, DM]) for b in range(2)]
                YT = [sbt(es, f"YTo{b}", [128, 8, 256], BF16) for b in range(2)]
                for c in range(8):
                    LDC(lambda e, c=c: e.dma_start(out=WOUT[:, c, :], in_=wout_d[c * 128:(c + 1) * 128, :]), w=[("WOUT", c)])
                tiles = [(0, NM)] + [(NM + 256 * i, 256) for i in range(NR // 256)]

                def load_tile(ti):
                    r0, T = tiles[ti]
                    P = min(T, 128)
                    nsub = (T + 127) // 128
                    b = ti % 2
                    LD(lambda e: e.dma_start(out=HT[b][:P, 0:nsub, :], in_=h1_d[r0:r0 + T, :].rearrange("(s p) d -> p s d", p=P)),
                       w=[("HTo", b)])
                    (LD2 if OQ >= 1 else LD)(lambda e: e.dma_start(out=YT[b][:, :, 0:T], in_=ym_d[:, r0:r0 + T].rearrange("(c p) t -> p c t", p=128)),
                       w=[("YTo", b)])

                load_tile(0)
                for ti, (r0, T) in enumerate(tiles):
                    P = min(T, 128)
                    nsub = (T + 127) // 128
                    b = ti % 2
                    if ti + 1 < len(tiles):
                        load_tile(ti + 1)
                    for s in range(nsub):
                        for hf in range(2):
                            po, ko = ps1()
                            for c in range(8):
                                PE(lambda e, c=c, s=s, hf=hf, po=po: e.matmul(po[0:P, :], lhsT=YT[b][:, c, s * 128:s * 128 + P],
                                                                              rhs=WOUT[:, c, hf * 512:(hf + 1) * 512],
                                                                              start=(c == 0), stop=(c == 7)),
                                   r=[("YTo", b), ("WOUT", c)], w=[ko])
                            DVE(lambda e, s=s, hf=hf, po=po: e.tensor_tensor(out=HT[b][0:P, s, hf * 512:(hf + 1) * 512], in0=po[0:P, :],
                                                                             in1=HT[b][0:P, s, hf * 512:(hf + 1) * 512], op=ALU.add),
                                r=[ko, ("HTo", b)], w=[("HTo", b)])
                    (ST2 if OQ >= 2 else ST)(lambda e, r0=r0, T=T, P=P, nsub=nsub, b=b: e.dma_start(
                        out=h2_d[r0:r0 + T, :].rearrange("(s p) d -> p s d", p=P), in_=HT[b][:P, 0:nsub, :]), r=[("HTo", b)], w=[("h2", ti)])
                S.barrier()

        tilesA = [(0, NM)] + [(NM + 256 * i, 256) for i in range(NR // 256)]
        tilesC = [(NM + 256 * i, 256) for i in range(NR // 256)]

        def srcA(r0, T):
            return meta_d[0:NM, :] if r0 == 0 else x_d[r0 - NM:r0 - NM + T, :]

        if "A" in phases:
            ffn_phase("A", tilesA, srcA, lambda r0, T: h1_d[r0:r0 + T, :], 0, 0, False)
        if "B" in phases:
            mix_phase()
        if "O" in phases:
            outproj_phase()
        if "C" in phases:
            ffn_phase("C", tilesC, lambda r0, T: h2_d[r0:r0 + T, :], lambda r0, T: out_d[r0 - NM:r0 - NM + T, :], 1, 2, True)
        if dbg:
            dh1 = nc.dram_tensor("dbg_h1", [LT, DM], F32, kind="ExternalOutput").ap()
            dh2 = nc.dram_tensor("dbg_h2", [LT, DM], F32, kind="ExternalOutput").ap()
            dym = nc.dram_tensor("dbg_ym", [DM, LT], BF16, kind="ExternalOutput").ap()
            ST(lambda e: e.dma_start(out=dh1, in_=h1_d), w=["dbg1"])
            ST(lambda e: e.dma_start(out=dh2, in_=h2_d), w=["dbg2"])
            ST(lambda e: e.dma_start(out=dym, in_=ym_d), w=["dbg3"])
        S.barrier(["sync"])
    return nc, S


def host_inputs(inp, b, NR=4096):
    f = lambda a: np.ascontiguousarray(np.asarray(a, dtype=np.float32))
    m = {}
    m["x"] = f(inp["x"][b][:NR])
    m["meta"] = f(inp["meta_tokens"])
    m["gvec"] = f(np.stack([inp["ffn1_norm"][0], inp["mix_norm"][0], inp["ffn2_norm"][0], inp["final_norm"]], 0))
    m["wg1"] = f(inp["ffn1_w_gate"][0]); m["wu1"] = f(inp["ffn1_w_up"][0]); m["wd1"] = f(inp["ffn1_w_down"][0])
    m["wg2"] = f(inp["ffn2_w_gate"][0]); m["wu2"] = f(inp["ffn2_w_up"][0]); m["wd2"] = f(inp["ffn2_w_down"][0])
    m["w_in"] = f(inp["w_in"][0]); m["w_out"] = f(inp["w_out"][0])
    m["w2"] = f(inp["rwkv_w2"][0]); m["a2"] = f(inp["rwkv_a2"][0]); m["g2"] = f(inp["rwkv_g2"][0])
    mu = np.asarray(inp["rwkv_mu"][0], np.float32)
    hp = lambda v: np.asarray(v, np.float32).reshape(-1, 64).T
    b_attn = np.asarray(inp["b_attn"][0], np.float32)
    cols = [hp(mu[0:512]), hp(mu[512:1024]), hp(mu[1024:1536]), hp(inp["rwkv_w0"][0]), hp(inp["rwkv_a0"][0]),
            hp(inp["rwkv_k_k"][0]), hp(inp["rwkv_k_a"][0]), hp(np.asarray(inp["rwkv_r_k"][0]).reshape(-1)),
            hp(inp["rwkv_ln_w"][0]), hp(inp["rwkv_ln_b"][0]), hp(b_attn[0:512]), hp(b_attn[512:640])]
    m["pp"] = f(np.concatenate(cols, axis=1))
    pl = np.zeros((96, 3), np.float32)
    pl[:32, 0] = mu[1536:1568]; pl[:32, 1] = mu[1568:1600]; pl[:96, 2] = mu[1600:1696]
    m["pl"] = pl
    m["bv"] = f(b_attn[640:768].reshape(1, 128))
    m["sinks"] = f(np.asarray(inp["attn_sinks"][0]).reshape(1, 8))
    return m


_CACHE = {}


def kernel(**inputs):
    B = inputs["x"].shape[0]
    NR = inputs["x"].shape[1]
    if NR not in _CACHE:
        _CACHE[NR] = build_nc(NR)[0]
    nc = _CACHE[NR]
    consts = {"c_" + k: v for k, v in make_consts().items()}
    in_maps = []
    for b in range(B):
        m = host_inputs(inputs, b, NR)
        m.update(consts)
        in_maps.append(m)
    res = run_bass_kernel_spmd(nc, in_maps, core_ids=list(range(B)))
    out = np.stack([np.asarray(r["out"], dtype=np.float32) for r in res.results], axis=0)
    return out
```

```python
import numpy as np
from contextlib import ExitStack
import concourse.bass as bass
import concourse.mybir as mybir
from concourse.bass_utils import run_bass_kernel_spmd

F32 = mybir.dt.float32
BF16 = mybir.dt.bfloat16
AF = mybir.ActivationFunctionType
ALU = mybir.AluOpType

NM = 16
DM = 1024
FF = 2816
NFC = 22
DEC = 0.6065306597126334


import os
STRICT = os.environ.get("K_STRICT", "1") == "1"
DYN = os.environ.get("K_DYN", "0") == "1"
FRAC = os.environ.get("K_FRAC", "0") == "1"
OQ = int(os.environ.get("K_OQ", "2"))
NORM_GAP = int(os.environ.get("K_NGAP", "3"))
FW = [float(v) for v in os.environ.get("K_FW", "1,1,1,1").split(",")]
SLACK = float(os.environ.get("K_SLACK", "0.0"))
WGT = [int(v) for v in os.environ.get("K_WGT", "1,2,1,1,1").split(",")]


class Sched:
    STREAMS = ("sync", "scalar", "vector", "gpsimd", "tensor")

    def __init__(self, nc):
        self.nc = nc
        self.prog = {}
        self.groups = {}
        self.w = {}
        self.r = {}
        self.waited = {n: {} for n in self.STREAMS}
        self.nops = 0
        self.tfree = {n: 0.0 for n in self.STREAMS}
        self.fin = {}
        self.step_fin = 0.0
        self.cost = {"sync": 2.0, "scalar": 0.55, "vector": 0.5, "gpsimd": 2.0, "tensor": 0.08}
        self.lat = 1.2

    def add_prog(self, name, sem, inc, inorder):
        self.prog[name] = dict(sem=sem, count=0, inc=inc, inorder=inorder)

    def add_group(self, name, sems):
        subs = []
        for i, s in enumerate(sems):
            sub = f"{name}_{i}"
            self.add_prog(sub, s, 16, False)
            subs.append(sub)
        self.groups[name] = dict(subs=subs, i=0)

    def op(self, stream, prog, fn, reads=(), writes=(), cost=None):
        if getattr(self, "dry", False):
            return
        deps = {}
        if prog in self.groups:
            g = self.groups[prog]
            prog = g["subs"][g["i"] % len(g["subs"])]
            g["i"] += 1
            if self.prog[prog]["count"] > 0:
                deps[prog] = self.prog[prog]["count"]
        inorder = self.prog[prog]["inorder"] and not (STRICT and prog != "pe")
        for k in reads:
            w = self.w.get(k)
            if w is not None and not (w[0] == prog and prog == "pe"):
                deps[w[0]] = max(deps.get(w[0], 0), w[1])
        for k in writes:
            w = self.w.get(k)
            if w is not None and not (inorder and w[0] == prog):
                deps[w[0]] = max(deps.get(w[0], 0), w[1])
            for (p, n) in self.r.get(k, ()):
                if inorder and p == prog:
                    continue
                deps[p] = max(deps.get(p, 0), n)
        eng = getattr(self.nc, stream)
        wd = self.waited[stream]
        t0 = self.tfree[stream]
        for p, n in deps.items():
            f_ = self.fin.get((p, n), 0.0) + (0.0 if p == prog else self.lat)
            if f_ > t0:
                t0 = f_
            if wd.get(p, 0) < n:
                wd[p] = n
                eng.wait_ge(self.prog[p]["sem"], n)
        pr = self.prog[prog]
        pr["count"] += pr["inc"]
        cnt = pr["count"]
        t1 = t0 + (cost if cost is not None else self.cost[stream])
        self.tfree[stream] = t1 if pr["inorder"] else t0 + 0.1
        self.fin[(prog, cnt)] = t1
        if t1 > self.step_fin:
            self.step_fin = t1
        fn(eng).then_inc(pr["sem"], pr["inc"])
        for k in writes:
            self.w[k] = (prog, cnt)
            self.r[k] = []
        for k in reads:
            lst = [x for x in self.r.get(k, []) if x[0] != prog]
            lst.append((prog, cnt))
            self.r[k] = lst
        self.nops += 1

    def barrier(self, streams=None):
        waits_all = [(n, pr["sem"], pr["count"]) for n, pr in self.prog.items() if pr["count"] > 0]
        for st in (streams or self.STREAMS):
            wd = self.waited[st]
            eng = getattr(self.nc, st)
            for (n, s, c) in waits_all:
                if wd.get(n, 0) < c:
                    wd[n] = c
                    eng.wait_ge(s, c)


def make_consts():
    c = {}
    c["ident"] = np.eye(128, dtype=np.float32)
    s = np.arange(64)[:, None]
    t = np.arange(64)[None, :]
    rep8 = lambda m: np.ascontiguousarray(np.tile(m.astype(np.float32)[:, None, :], (1, 8, 1)).reshape(m.shape[0], -1))
    c["msu"] = (t > s).astype(np.float32)
    c["msl"] = (t < s).astype(np.float32)
    c["mui"] = (t >= s).astype(np.float32)
    c["idt"] = (t == s).astype(np.float32)
    slopes = 2.0 ** (-(np.arange(8) + 1.0))
    j = np.arange(64)[:, None].astype(np.float64)
    i = np.arange(64)[None, :].astype(np.float64)

    def alibi(dist):
        return np.ascontiguousarray(
            (-slopes[None, :, None] * dist[:, None, :]).astype(np.float32).reshape(dist.shape[0], -1))

    c["al2"] = alibi(128 + i - j)
    c["al1"] = alibi(64 + i - j)
    c["al0"] = alibi(np.abs(i - j))
    m = np.arange(16)[:, None].astype(np.float64)
    c["alm0"] = alibi(16 + i - m)
    c["almd"] = alibi(np.full((16, 64), 64.0))
    i16 = np.arange(16)[None, :].astype(np.float64)
    c["almm"] = alibi(np.abs(i16 - m))
    return c


CONST_SHAPES = dict(ident=[128, 128], msu=[64, 64], msl=[64, 64], mui=[64, 64], idt=[64, 64],
                    al2=[64, 512], al1=[64, 512], al0=[64, 512], alm0=[16, 512], almd=[16, 512],
                    almm=[16, 128])
NPP = 90


def build_nc(NR=4096, phases="ABOC", dbg=False):
    nc = bass.Bass("TRN2", target_bir_lowering=False)
    LT = NM + NR
    S = Sched(nc)

    def din(name, shape):
        return nc.dram_tensor(name, list(shape), F32, kind="ExternalInput").ap()

    x_d = din("x", [NR, DM])
    meta_d = din("meta", [NM, DM])
    gvec_d = din("gvec", [4, DM])
    wg_d = [din("wg1", [DM, FF]), din("wg2", [DM, FF])]
    wu_d = [din("wu1", [DM, FF]), din("wu2", [DM, FF])]
    wd_d = [din("wd1", [FF, DM]), din("wd2", [FF, DM])]
    win_d = din("w_in", [DM, 2464])
    wout_d = din("w_out", [DM, DM])
    w2_d = din("w2", [32, 512])
    a2_d = din("a2", [32, 512])
    g2_d = din("g2", [96, 512])
    pp_d = din("pp", [64, NPP])
    pl_d = din("pl", [96, 3])
    bv_d = din("bv", [1, 128])
    sinks_d = din("sinks", [1, 8])
    cst = {k: din("c_" + k, shp) for k, shp in CONST_SHAPES.items()}
    out_d = nc.dram_tensor("out", [NR, DM], F32, kind="ExternalOutput").ap()
    h1_d = nc.dram_tensor("h1s", [LT, DM], F32).ap()
    h2_d = nc.dram_tensor("h2s", [LT, DM], F32).ap()
    ym_d = nc.dram_tensor("yms", [DM, LT], BF16).ap()

    def OPf(stream, prog):
        def f(fn, r=(), w=()):
            S.op(stream, prog, fn, r, w)
        return f

    PE = OPf("tensor", "pe")
    ACT = OPf("scalar", "act")
    DVE = OPf("vector", "dve")
    POOL = OPf("gpsimd", "pool")
    LD = OPf("sync", "dq0")
    ST = OPf("sync", "dq2")
    LDC = OPf("gpsimd", "dq1")
    LD2 = OPf("scalar", "dq0")
    ST2 = OPf("scalar", "dq2")

    with ExitStack() as top:
        sems = {n: top.enter_context(nc.semaphore(n)) for n in ("pe", "act", "dve", "pool")}
        for n in ("pe", "act", "dve", "pool"):
            S.add_prog(n, sems[n], 1, True)
        for n, K_ in (("dq0", 8), ("dq1", 12), ("dq2", 6)):
            S.add_group(n, [top.enter_context(nc.semaphore(f"{n}_{i}")) for i in range(K_)])

        pp_t = [top.enter_context(nc.psum_tensor(f"pp{i}", [128, 1024], F32)) for i in range(3)]
        p_single = top.enter_context(nc.psum_tensor("psg", [128, 512], F32))
        PSB = top.enter_context(nc.psum_tensor("psb", [128, 1024], BF16))
        banks = []
        for i in range(3):
            banks.append((pp_t[i], 0, f"ps{2 * i}"))
            banks.append((pp_t[i], 512, f"ps{2 * i + 1}"))
        banks.append((p_single, 0, "ps6"))
        rot = {"s": 0, "p": 0}

        rot["m"] = 0
        rot["pipe"] = False

        def ps1(pool="m"):
            if not rot["pipe"]:
                t, off, k = banks[rot["s"] % 7]
                rot["s"] += 1
            elif pool == "s":
                t, off, k = banks[5 + rot["s"] % 2]
                rot["s"] += 1
            else:
                t, off, k = banks[rot["m"] % 5]
                rot["m"] += 1
            return t[:, off:off + 512], k

        def ps2():
            i = rot["p"] % 3
            rot["p"] += 1
            return pp_t[i][:, :], [f"ps{2 * i}", f"ps{2 * i + 1}"]

        def sbt(es, name, shape, dt=F32):
            return es.enter_context(nc.sbuf_tensor(name, list(shape), dt))

        IDF = sbt(top, "IDF", [128, 128])
        IDB = sbt(top, "IDB", [128, 128], BF16)
        LD(lambda e: e.dma_start(out=IDF[:], in_=cst["ident"]), w=["IDF"])
        DVE(lambda e: e.tensor_copy(out=IDB[:], in_=IDF[:]), r=["IDF"], w=["IDB"])
        SS = sbt(top, "SS", [128, 8])
        RS = sbt(top, "RS", [128, 8])
        JUNK = sbt(top, "JUNK", [128, 1024], BF16)

        def rms_rstd(src_fn, P, nsub, rkey, wtag):
            DVE(lambda e: e.memset(SS[:P, 0:nsub], 0.0), w=["SS"])
            for s in range(nsub):
                ACT(lambda e, s=s: e.activation(out=JUNK[:P, :], in_=src_fn(s), func=AF.Square,
                                                accum_out=SS[:P, s:s + 1]), r=[rkey, "SS"], w=["JUNK", "SS"])
            DVE(lambda e: e.tensor_scalar(out=RS[:P, 0:nsub], in0=SS[:P, 0:nsub], scalar1=1.0 / DM, scalar2=1e-5,
                                          op0=ALU.mult, op1=ALU.add), r=["SS"], w=["RS"])
            ACT(lambda e: e.activation(out=RS[:P, 0:nsub], in_=RS[:P, 0:nsub], func=AF.Sqrt), r=["RS"], w=["RS"])
            DVE(lambda e: e.reciprocal(out=RS[:P, 0:nsub], in_=RS[:P, 0:nsub]), r=["RS"], w=["RS"])

        def norm_ew(src_fn, P, nsub, rkey, GB, NTK):
            rms_rstd(src_fn, P, nsub, rkey, None)
            for s in range(nsub):
                nb = s % 2
                DVE(lambda e, s=s, nb=nb: e.scalar_tensor_tensor(out=NTK[nb][:P, :], in0=src_fn(s), scalar=RS[:P, s:s + 1],
                                                                 in1=GB[:P, :], op0=ALU.mult, op1=ALU.mult),
                    r=[rkey, "RS", "GB"], w=[("NTK", nb)])

        def norm_tr(P, nsub, NTK, nT, nTkey):
            for s in range(nsub):
                nb = s % 2
                for c in range(8):
                    PE(lambda e, c=c, nb=nb: e.transpose(PSB[:, c * 128:c * 128 + P], NTK[nb][:P, c * 128:(c + 1) * 128],
                                                         IDB[:P, :P]), r=[("NTK", nb), "IDB"], w=["psb"])
                ACT(lambda e, s=s: e.copy(out=nT[:, :, s * 128:s * 128 + P],
                                          in_=PSB[:].rearrange("p (c t) -> p c t", t=128)[:, :, 0:P]),
                    r=["psb"], w=[nTkey])

        def norm_transpose(src_fn, P, nsub, rkey, GB, NTK, nT, nTkey):
            rms_rstd(src_fn, P, nsub, rkey, None)
            for s in range(nsub):
                nb = s % 2
                DVE(lambda e, s=s, nb=nb: e.scalar_tensor_tensor(out=NTK[nb][:P, :], in0=src_fn(s), scalar=RS[:P, s:s + 1],
                                                                 in1=GB[:P, :], op0=ALU.mult, op1=ALU.mult),
                    r=[rkey, "RS", "GB"], w=[("NTK", nb)])
                for c in range(8):
                    PE(lambda e, c=c, nb=nb: e.transpose(PSB[:, c * 128:c * 128 + P], NTK[nb][:P, c * 128:(c + 1) * 128],
                                                         IDB[:P, :P]), r=[("NTK", nb), "IDB"], w=["psb"])
                ACT(lambda e, s=s: e.copy(out=nT[:, :, s * 128:s * 128 + P],
                                          in_=PSB[:].rearrange("p (c t) -> p c t", t=128)[:, :, 0:P]),
                    r=["psb"], w=[nTkey])

        def ffn_phase(tag, tiles, src_fn, dst_fn, wi, gidx, final):
            with ExitStack() as es:
                Wg = sbt(es, "Wg" + tag, [128, 8, FF], BF16)
                Wu = sbt(es, "Wu" + tag, [128, 8, FF], BF16)
                Wd = sbt(es, "Wd" + tag, [128, NFC, DM], BF16)
                GB = sbt(es, "GB" + tag, [128, DM])
                GF = sbt(es, "GF" + tag, [128, DM]) if final else None
                XT = [sbt(es, f"XT{b}" + tag, [128, 2, DM]) for b in range(2)]
                NTK = [sbt(es, f"NTK{b}" + tag, [128, DM], BF16) for b in range(2)]
                nTs = [sbt(es, f"nT{b}" + tag, [128, 8, 256], BF16) for b in range(2)]
                actT = sbt(es, "actT" + tag, [128, NFC, 256], BF16)
                SG = [sbt(es, f"SG{b}" + tag, [128, 256]) for b in range(2)]
                LD(lambda e: e.dma_start(out=GB[:], in_=gvec_d[gidx:gidx + 1, :].partition_broadcast(128)), w=["GB"])
                if final:
                    LD(lambda e: e.dma_start(out=GF[:], in_=gvec_d[3:4, :].partition_broadcast(128)), w=["GF"])

                def load_tile(ti):
                    r0, T = tiles[ti]
                    P = min(T, 128)
                    nsub = (T + 127) // 128
                    b = ti % 2
                    LD(lambda e: e.dma_start(out=XT[b][:P, 0:nsub, :],
                                             in_=src_fn(r0, T).rearrange("(s p) d -> p s d", p=P)), w=[("XT", b)])

                load_tile(0)
                for c in range(8):
                    LDC(lambda e, c=c: e.dma_start(out=Wg[:, c, :], in_=wg_d[wi][c * 128:(c + 1) * 128, :]), w=[("Wg", c)])
                    LDC(lambda e, c=c: e.dma_start(out=Wu[:, c, :], in_=wu_d[wi][c * 128:(c + 1) * 128, :]), w=[("Wu", c)])
                for fc in range(NFC):
                    LDC(lambda e, fc=fc: e.dma_start(out=Wd[:, fc, :], in_=wd_d[wi][fc * 128:(fc + 1) * 128, :]), w=[("Wd", fc)])

                def tile_geo(ti):
                    r0_, T_ = tiles[ti]
                    return min(T_, 128), (T_ + 127) // 128

                P0_, ns0_ = tile_geo(0)
                norm_ew(lambda s: XT[0][:P0_, s, :], P0_, ns0_, ("XT", 0), GB, NTK)
                norm_tr(P0_, ns0_, NTK, nTs[0], ("nT", 0))
                for ti, (r0, T) in enumerate(tiles):
                    P = min(T, 128)
                    nsub = (T + 127) // 128
                    b = ti % 2
                    if ti + 1 < len(tiles):
                        load_tile(ti + 1)
                    xt = XT[b]
                    nT = nTs[b]
                    nTk = ("nT", b)
                    if ti + 1 < len(tiles):
                        Pn_, nsn_ = tile_geo(ti + 1)
                        xn_ = XT[1 - b]
                        norm_ew(lambda s, xn_=xn_, Pn_=Pn_: xn_[:Pn_, s, :], Pn_, nsn_, ("XT", 1 - b), GB, NTK)
                    for fc in range(NFC):
                        pg, kg = ps1()
                        pu, ku = ps1()
                        for c in range(8):
                            PE(lambda e, c=c, fc=fc, pg=pg: e.matmul(pg[:, 0:T], lhsT=Wg[:, c, fc * 128:(fc + 1) * 128],
                                                                     rhs=nT[:, c, 0:T], start=(c == 0), stop=(c == 7)),
                               r=[("Wg", c), nTk], w=[kg])
                        for c in range(8):
                            PE(lambda e, c=c, fc=fc, pu=pu: e.matmul(pu[:, 0:T], lhsT=Wu[:, c, fc * 128:(fc + 1) * 128],
                                                                     rhs=nT[:, c, 0:T], start=(c == 0), stop=(c == 7)),
                               r=[("Wu", c), nTk], w=[ku])
                        sb_ = fc % 2
                        ACT(lambda e, pg=pg, sb_=sb_: e.activation(out=SG[sb_][:, 0:T], in_=pg[:, 0:T], func=AF.Silu),
                            r=[kg], w=[("SG", sb_)])
                        DVE(lambda e, pu=pu, sb_=sb_, fc=fc: e.tensor_tensor(out=actT[:, fc, 0:T], in0=SG[sb_][:, 0:T],
                                                                              in1=pu[:, 0:T], op=ALU.mult),
                            r=[("SG", sb_), ku], w=[("actT", fc)])
                    if ti + 1 < len(tiles):
                        norm_tr(Pn_, nsn_, NTK, nTs[1 - b], ("nT", 1 - b))
                    for s in range(nsub):
                        for hf in range(2):
                            po, ko = ps1()
                            for fc in range(NFC):
                                PE(lambda e, fc=fc, s=s, hf=hf, po=po: e.matmul(
                                    po[:P, :], lhsT=actT[:, fc, s * 128:s * 128 + P], rhs=Wd[:, fc, hf * 512:(hf + 1) * 512],
                                    start=(fc == 0), stop=(fc == NFC - 1)), r=[("actT", fc), ("Wd", fc)], w=[ko])
                            DVE(lambda e, s=s, hf=hf, po=po: e.scalar_tensor_tensor(
                                out=xt[:P, s, hf * 512:(hf + 1) * 512], in0=po[:P, :], scalar=0.5,
                                in1=xt[:P, s, hf * 512:(hf + 1) * 512], op0=ALU.mult, op1=ALU.add),
                                r=[ko, ("XT", b)], w=[("XT", b)])
                    if final:
                        rms_rstd(lambda s: xt[:P, s, :], P, nsub, ("XT", b), None)
                        for s in range(nsub):
                            DVE(lambda e, s=s: e.scalar_tensor_tensor(out=xt[:P, s, :], in0=xt[:P, s, :], scalar=RS[:P, s:s + 1],
                                                                      in1=GF[:P, :], op0=ALU.mult, op1=ALU.mult),
                                r=[("XT", b), "RS", "GF"], w=[("XT", b)])
                    ST(lambda e, r0=r0, T=T, P=P, nsub=nsub, xt=xt: e.dma_start(
                        out=dst_fn(r0, T).rearrange("(s p) d -> p s d", p=P), in_=xt[:P, 0:nsub, :]), r=[("XT", b)], w=[("dst", tag, ti)])
                S.barrier()

        def mix_phase():
            with ExitStack() as es:
                WIN = sbt(es, "WIN", [128, 8, 2464], BF16)
                W2 = sbt(es, "W2", [32, 512])
                A2 = sbt(es, "A2", [32, 512], BF16)
                G2 = sbt(es, "G2", [96, 512], BF16)
                PPt = sbt(es, "PPt", [64, NPP])
                PL = sbt(es, "PL", [96, 3])
                BVB = sbt(es, "BVB", [64, 128])
                ESK = sbt(es, "ESK", [64, 8])
                GB = sbt(es, "GBm", [128, DM])
                CT = {k: sbt(es, "C_" + k, CONST_SHAPES[k]) for k in CONST_SHAPES if k != "ident"}
                ALMC = sbt(es, "ALMC", [16, 512])
                ONESF = sbt(es, "ONESF", [64, 64])
                ONESB = sbt(es, "ONESB", [64, 64], BF16)
                ONESDB = sbt(es, "ONESDB", [64, 64], BF16)
                T1b = sbt(es, "T1b", [64, 8, 64], BF16)
                YSb = sbt(es, "YSb", [64, 8, 64], BF16)
                Xb = [sbt(es, f"Xb{i}", [64, 512], BF16) for i in range(2)]
                Xtb = [sbt(es, f"Xtb{i}", [64, 512], BF16) for i in range(2)]
                Rtb = [sbt(es, f"Rtb{i}", [64, 512], BF16) for i in range(2)]
                HT = [sbt(es, f"HT{b}", [64, DM]) for b in range(2)]
                NTK = [sbt(es, f"NTKm{b}", [64, DM], BF16) for b in range(2)]
                nT = sbt(es, "nTm", [128, 8, 64], BF16)
                ZR = sbt(es, "ZR", [64, 8, 65])
                ZK = sbt(es, "ZK", [64, 8, 65])
                ZV = sbt(es, "ZV", [64, 8, 65])
                ZW = sbt(es, "ZW", [32, 65])
                ZA = sbt(es, "ZA", [32, 65])
                ZG = sbt(es, "ZG", [96, 65])
                CUM = sbt(es, "CUM", [64, 8, 65])
                QTs = [sbt(es, f"QT{i}", [64, 8, 64], BF16) for i in range(2)]
                KTR = sbt(es, "KTR", [64, 2, 4, 64], BF16)
                KTM = sbt(es, "KTMeta", [64, 2, 16], BF16)
                VR = sbt(es, "VR", [64, 4, 128], BF16)
                VM = sbt(es, "VMeta", [16, 128], BF16)
                YMIX = [sbt(es, f"YMIX{b}", [64, 16, 64], BF16) for b in range(2)]
                STT = sbt(es, "STATE", [64, 8, 64])
                names = ["Rr", "Kz", "Vv", "Gg", "Aa", "KKn", "Kk", "Winc", "Wexc", "Winv", "T1", "T2",
                         "Rt", "Kt", "At", "Bt", "BON", "YS"]
                names += ["S1", "S2", "T3"]
                DB_D = ("At", "Rt", "Winc", "BON", "Gg")
                DB_C = ("AAK", "VTMt", "ARB", "ARK", "KTMt", "BTMt", "R1")
                D0 = {n: sbt(es, "D_" + n, [64, 8, 64]) for n in names}
                Dp = [dict(D0), dict(D0)]
                for n in DB_D:
                    Dp[1][n] = sbt(es, "D1_" + n, [64, 8, 64])
                S1b = sbt(es, "S1b", [64, 8, 64], BF16)
                T3b = sbt(es, "T3b", [64, 8, 64], BF16)
                TW = sbt(es, "TW", [32, 64])
                ZAl = sbt(es, "ZAl", [32, 64], BF16)
                SGg = sbt(es, "SGg", [96, 64])
                SGb = sbt(es, "SGb", [96, 64], BF16)
                TL = sbt(es, "TL", [96, 64])
                cn = ["KTMt", "BTMt", "VTMt", "AAK", "ARB", "ARK", "X0", "Xt0", "R0", "R1",
                      "RHS0", "UU"]
                Cc0 = {n: sbt(es, "Cc_" + n, [64, 512]) for n in cn}
                Ccp = [dict(Cc0), dict(Cc0)]
                for n in DB_C:
                    Ccp[1][n] = sbt(es, "Cc1_" + n, [64, 512])
                DBSET = set(DB_D) | set(DB_C) | {"QT"}

                def trw(f_, par):
                    def g(fn, r=(), w=()):
                        f_(fn, [((k, par) if (isinstance(k, str) and k in DBSET) else k) for k in r],
                           [((k, par) if (isinstance(k, str) and k in DBSET) else k) for k in w])
                    return g

                PE0, ACT0, DVE0, ST0 = PE, ACT, DVE, ST
                print("mix phase sbuf bytes remaining/partition:", nc.sbuf_bytes_remaining // 128 if nc.sbuf_bytes_remaining > 1 << 20 else nc.sbuf_bytes_remaining)
                SBT = [sbt(es, "SBT0", [64, 512])] * 2
                PT = [sbt(es, f"PT{i}", [64, 512], BF16) for i in range(4)]
                DEN = sbt(es, "DEN", [64, 512])

                for c in range(8):
                    LDC(lambda e, c=c: e.dma_start(out=WIN[:, c, :], in_=win_d[c * 128:(c + 1) * 128, :]), w=[("WIN", c)])
                LD(lambda e: e.dma_start(out=W2[:], in_=w2_d), w=["W2"])
                LDC(lambda e: e.dma_start(out=A2[:], in_=a2_d), w=["A2"])
                LDC(lambda e: e.dma_start(out=G2[:], in_=g2_d), w=["G2"])
                LD(lambda e: e.dma_start(out=PPt[:], in_=pp_d), w=["PP"])
                LD(lambda e: e.dma_start(out=PL[:], in_=pl_d), w=["PL"])
                LD(lambda e: e.dma_start(out=BVB[:], in_=bv_d.partition_broadcast(64)), w=["BVB"])
                LD(lambda e: e.dma_start(out=ESK[:], in_=sinks_d.partition_broadcast(64)), w=["ESK"])
                LD(lambda e: e.dma_start(out=GB[:], in_=gvec_d[1:2, :].partition_broadcast(128)), w=["GB"])
                for k in CT:
                    LD(lambda e, k=k: e.dma_start(out=CT[k][:], in_=cst[k]), w=[("CT", k)])
                ACT(lambda e: e.activation(out=ESK[:], in_=ESK[:], func=AF.Exp), r=["ESK"], w=["ESK"])
                DVE(lambda e: e.memset(ONESF[:], 1.0), w=["ONESF"])
                DVE(lambda e: e.memset(ONESB[:], 1.0), w=["ONESB"])
                DVE(lambda e: e.memset(ONESDB[:], 1.0 / 64.0), w=["ONESDB"])
                DVE(lambda e: e.memset(STT[:], 0.0), w=["STATE"])
                for Z, k in ((ZR, "ZR"), (ZK, "ZK"), (ZV, "ZV"), (CUM, "CUM")):
                    DVE(lambda e, Z=Z: e.memset(Z[:, :, 0:1], 0.0), w=[k])
                for Z, k in ((ZW, "ZW"), (ZA, "ZA"), (ZG, "ZG")):
                    DVE(lambda e, Z=Z: e.memset(Z[:, 0:1], 0.0), w=[k])

                MU_R, MU_K, MU_V, W0c, A0c, KKc, KAc, RKc, LNWc, LNBc, BQc, BKc = 0, 8, 16, 24, 32, 40, 48, 56, 64, 72, 80, 88

                def v3(t, rows, cols):
                    a = t if isinstance(t, bass.AP) else t[:]
                    return a.rearrange("p (h t) -> p h t", t=64)[0:rows, :, 0:cols]

                def m3(mask, rows, cols):
                    return CT[mask][0:rows, 0:cols].unsqueeze(1).to_broadcast([rows, 8, cols])

                tiles = [(0, NM)] + [(NM + 64 * i, 64) for i in range(NR // 64)]

                def load_tile(ti):
                    r0, T = tiles[ti]
                    b = ti % 2
                    LD(lambda e: e.dma_start(out=HT[b][0:T, :], in_=h1_d[r0:r0 + T, :]), w=[("HT", b)])

                flags = {}

                def genP1(ti):
                    r0, T = tiles[ti]
                    b = ti % 2
                    par = ti % 2
                    ht = HT[b]
                    ym = YMIX[b]
                    ymk = ("YMIX", b)
                    is_meta = (ti == 0)
                    Ck = T
                    gc = ti - 1
                    D = Dp[par]
                    Cc = Ccp[par]
                    PE, ACT, DVE, ST = (trw(f_, par) for f_ in (PE0, ACT0, DVE0, ST0))
                    QT = QTs[par]
                    PSP = "m"
                    if ti + 1 < len(tiles):
                        load_tile(ti + 1)
                    norm_ew(lambda s: ht[0:T, :], T, 1, ("HT", b), GB, NTK)
                    for _ in range(NORM_GAP):
                        yield
                    norm_tr(T, 1, NTK, nT, "nT")
                    yield
                    def ppb(col):
                        return PPt[:, col:col + 8].unsqueeze(2).to_broadcast([64, 8, T])

                    def d(n):
                        return D[n][:, :, 0:T]

                    def p3(ps, rows=64):
                        return ps.rearrange("p (h t) -> p h t", t=64)[0:rows, :, 0:T]

                    def headsum(srct, skey, lhs, lkey):
                        ps, key = ps1(PSP)
                        if T == 64:
                            PE(lambda e, ps=ps: e.matmul(ps[0:64, :], lhsT=lhs[:, :], rhs=srct[:].rearrange("p h t -> p (h t)"), start=True, stop=True),
                               r=[skey, lkey], w=[key])
                        else:
                            for h in range(8):
                                PE(lambda e, ps=ps, h=h: e.matmul(ps[0:64, h * 64:h * 64 + T], lhsT=lhs[:, :], rhs=srct[:, h, 0:T], start=True, stop=True),
                                   r=[skey, lkey], w=[key])
                        return ps, key

                    def rsqrt_inplace(n):
                        ACT(lambda e: e.activation(out=d(n), in_=d(n), func=AF.Ln), r=[n], w=[n])
                        ACT(lambda e: e.activation(out=d(n), in_=d(n), func=AF.Exp, scale=-0.5), r=[n], w=[n])

                    while ti > 0 and not flags.get(("lerp", ti - 1)) and not getattr(S, "dry", False):
                        yield
                    def proj_heads(col0, nheads, ps, key):
                        for h in range(nheads):
                            for c in range(8):
                                PE(lambda e, h=h, c=c: e.matmul(ps[0:64, h * 64:h * 64 + T],
                                                                lhsT=WIN[:, c, col0 + h * 64:col0 + (h + 1) * 64],
                                                                rhs=nT[:, c, 0:T], start=(c == 0), stop=(c == 7)),
                                   r=[("WIN", c), "nT"], w=[key])

                    for (Z, zk, col0) in ((ZR, "ZR", 0), (ZK, "ZK", 512), (ZV, "ZV", 1024)):
                        ps, key = ps1()
                        proj_heads(col0, 8, ps, key)
                        ACT(lambda e, Z=Z, ps=ps: e.copy(out=Z[:, :, 1:1 + T], in_=p3(ps)), r=[key], w=[zk])
                    ps, key = ps1()
                    proj_heads(1696, 8, ps, key)
                    DVE(lambda e, ps=ps: e.tensor_tensor(out=QT[:, :, 0:T], in0=p3(ps), in1=ppb(BQc), op=ALU.add),
                        r=[key, "PP"], w=["QT"])
                    yield
                    ps, key = ps1()
                    for si, (col0, width) in enumerate(((1536, 32), (1568, 32), (1600, 96))):
                        for c in range(8):
                            PE(lambda e, si=si, c=c, col0=col0, width=width, ps=ps: e.matmul(
                                ps[0:width, si * 64:si * 64 + T], lhsT=WIN[:, c, col0:col0 + width], rhs=nT[:, c, 0:T],
                                start=(c == 0), stop=(c == 7)), r=[("WIN", c), "nT"], w=[key])
                    ACT(lambda e, ps=ps: e.copy(out=ZW[:, 1:1 + T], in_=ps[0:32, 0:T]), r=[key], w=["ZW"])
                    ACT(lambda e, ps=ps: e.copy(out=ZA[:, 1:1 + T], in_=ps[0:32, 64:64 + T]), r=[key], w=["ZA"])
                    ACT(lambda e, ps=ps: e.copy(out=ZG[:, 1:1 + T], in_=ps[0:96, 128:128 + T]), r=[key], w=["ZG"])
                    ps, key = ps1()
                    proj_heads(2208, 2, ps, key)
                    for kv in range(2):
                        if is_meta:
                            ACT(lambda e, kv=kv, ps=ps: e.activation(out=KTM[:, kv, :], in_=ps[0:64, kv * 64:kv * 64 + T],
                                                                     func=AF.Identity, bias=PPt[:, BKc + kv:BKc + kv + 1]),
                                r=[key, "PP"], w=["KTMeta"])
                        else:
                            ACT(lambda e, kv=kv, ps=ps: e.activation(
                                out=KTR[:, kv, gc % 4, :], in_=ps[0:64, kv * 64:kv * 64 + 64],
                                func=AF.Identity, bias=PPt[:, BKc + kv:BKc + kv + 1]),
                                r=[key, "PP"], w=[("KTR", gc % 4)])
                    yield
                    pv, kvk = ps1()
                    for c in range(8):
                        PE(lambda e, c=c, pv=pv: e.matmul(pv[0:T, 0:128], lhsT=nT[:, c, 0:T],
                                                          rhs=WIN[:, c, 2336:2464], start=(c == 0), stop=(c == 7)),
                           r=[("WIN", c), "nT"], w=[kvk])
                    if is_meta:
                        DVE(lambda e, pv=pv: e.tensor_tensor(out=VM[:, :], in0=pv[0:16, 0:128], in1=BVB[0:16, :], op=ALU.add),
                            r=[kvk, "BVB"], w=["VMeta"])
                    else:
                        DVE(lambda e, pv=pv: e.tensor_tensor(out=VR[:, gc % 4, :], in0=pv[0:64, 0:128], in1=BVB[:, :],
                                                             op=ALU.add), r=[kvk, "BVB"], w=[("VR", gc % 4)])


                    yield
                def genP2(ti):
                    r0, T = tiles[ti]
                    b = ti % 2
                    par = ti % 2
                    ht = HT[b]
                    ym = YMIX[b]
                    ymk = ("YMIX", b)
                    is_meta = (ti == 0)
                    Ck = T
                    gc = ti - 1
                    D = Dp[par]
                    Cc = Ccp[par]
                    PE, ACT, DVE, ST = (trw(f_, par) for f_ in (PE0, ACT0, DVE0, ST0))
                    QT = QTs[par]
                    PSP = "m"
                    yield
                    def ppb(col):
                        return PPt[:, col:col + 8].unsqueeze(2).to_broadcast([64, 8, T])

                    def d(n):
                        return D[n][:, :, 0:T]

                    def p3(ps, rows=64):
                        return ps.rearrange("p (h t) -> p h t", t=64)[0:rows, :, 0:T]

                    def headsum(srct, skey, lhs, lkey):
                        ps, key = ps1(PSP)
                        if T == 64:
                            PE(lambda e, ps=ps: e.matmul(ps[0:64, :], lhsT=lhs[:, :], rhs=srct[:].rearrange("p h t -> p (h t)"), start=True, stop=True),
                               r=[skey, lkey], w=[key])
                        else:
                            for h in range(8):
                                PE(lambda e, ps=ps, h=h: e.matmul(ps[0:64, h * 64:h * 64 + T], lhsT=lhs[:, :], rhs=srct[:, h, 0:T], start=True, stop=True),
                                   r=[skey, lkey], w=[key])
                        return ps, key

                    def rsqrt_inplace(n):
                        ACT(lambda e: e.activation(out=d(n), in_=d(n), func=AF.Ln), r=[n], w=[n])
                        ACT(lambda e: e.activation(out=d(n), in_=d(n), func=AF.Exp, scale=-0.5), r=[n], w=[n])

                    def lerp3(Z, zk, mucol, dn):
                        DVE(lambda e: e.tensor_tensor(out=d("T1"), in0=Z[:, :, 0:T], in1=Z[:, :, 1:1 + T], op=ALU.subtract),
                            r=[zk], w=["T1"])
                        DVE(lambda e: e.tensor_tensor(out=d("T1"), in0=d("T1"), in1=ppb(mucol), op=ALU.mult),
                            r=["T1", "PP"], w=["T1"])
                        DVE(lambda e: e.tensor_tensor(out=d(dn), in0=d("T1"), in1=Z[:, :, 1:1 + T], op=ALU.add),
                            r=["T1", zk], w=[dn])
                        DVE(lambda e: e.tensor_copy(out=Z[:, :, 0:1], in_=Z[:, :, T:T + 1]), r=[zk], w=[zk])

                    lerp3(ZR, "ZR", MU_R, "Rr")
                    yield
                    lerp3(ZK, "ZK", MU_K, "Kz")
                    yield
                    lerp3(ZV, "ZV", MU_V, "Vv")
                    yield

                    def lerp2(Z, zk, rows, plcol, dst, dk):
                        DVE(lambda e: e.tensor_tensor(out=TL[0:rows, 0:T], in0=Z[:, 0:T], in1=Z[:, 1:1 + T], op=ALU.subtract),
                            r=[zk], w=["TL"])
                        DVE(lambda e: e.scalar_tensor_tensor(out=dst[0:rows, 0:T], in0=TL[0:rows, 0:T], scalar=PL[0:rows, plcol:plcol + 1],
                                                             in1=Z[:, 1:1 + T], op0=ALU.mult, op1=ALU.add), r=["TL", "PL", zk], w=[dk])
                        DVE(lambda e: e.tensor_copy(out=Z[:, 0:1], in_=Z[:, T:T + 1]), r=[zk], w=[zk])

                    lerp2(ZW, "ZW", 32, 0, TW, "TW")
                    yield
                    lerp2(ZA, "ZA", 32, 1, ZAl, "ZAl")
                    yield
                    lerp2(ZG, "ZG", 96, 2, SGg, "SGg")
                    yield
                    ACT(lambda e: e.activation(out=TW[:, 0:T], in_=TW[:, 0:T], func=AF.Tanh), r=["TW"], w=["TW"])
                    ACT(lambda e: e.activation(out=SGb[:, 0:T], in_=SGg[:, 0:T], func=AF.Sigmoid), r=["SGg"], w=["SGb"])
                    flags[("lerp", ti)] = True

                    def lora(Wt, wkey, xin, xkey, rows):
                        ps, key = ps1()
                        for h in range(8):
                            PE(lambda e, h=h, ps=ps: e.matmul(ps[0:64, h * 64:h * 64 + T], lhsT=Wt[0:rows, h * 64:(h + 1) * 64],
                                                              rhs=xin[0:rows, 0:T], start=True, stop=True), r=[wkey, xkey], w=[key])
                        return ps, key

                    yield
                    ps, key = lora(W2, "W2", TW, "TW", 32)
                    DVE(lambda e, ps=ps: e.tensor_tensor(out=d("T2"), in0=p3(ps), in1=ppb(W0c), op=ALU.add),
                        r=[key, "PP"], w=["T2"])
                    yield
                    ACT(lambda e: e.activation(out=d("T2"), in_=d("T2"), func=AF.Sigmoid), r=["T2"], w=["T2"])
                    for h in range(8):
                        DVE(lambda e, h=h: e.tensor_tensor_scan(out=CUM[:, h, 1:1 + T], data0=ONESF[:, 0:T], data1=D["T2"][:, h, 0:T],
                                                                initial=0.0, op0=ALU.mult, op1=ALU.add),
                            r=["T2", "ONESF"], w=["CUM"])
                    ACT(lambda e: e.activation(out=d("Winv"), in_=CUM[:, :, 1:1 + T], func=AF.Exp, scale=DEC), r=["CUM"], w=["Winv"])
                    ACT(lambda e: e.activation(out=d("Winc"), in_=CUM[:, :, 1:1 + T], func=AF.Exp, scale=-DEC), r=["CUM"], w=["Winc"])
                    ACT(lambda e: e.activation(out=d("Wexc"), in_=CUM[:, :, 0:T], func=AF.Exp, scale=-DEC), r=["CUM"], w=["Wexc"])
                    flags[("decay", ti)] = True
                    yield
                    ps, key = lora(A2, "A2", ZAl, "ZAl", 32)
                    DVE(lambda e, ps=ps: e.tensor_tensor(out=d("Aa"), in0=p3(ps), in1=ppb(A0c), op=ALU.add),
                        r=[key, "PP"], w=["Aa"])
                    yield
                    ACT(lambda e: e.activation(out=d("Aa"), in_=d("Aa"), func=AF.Sigmoid), r=["Aa"], w=["Aa"])
                    flags[("a", ti)] = True
                    DVE(lambda e: e.tensor_tensor(out=d("KKn"), in0=d("Kz"), in1=ppb(KKc), op=ALU.mult), r=["Kz", "PP"], w=["KKn"])
                    ACT(lambda e: e.activation(out=T1b[:, :, 0:T], in_=d("KKn"), func=AF.Square), r=["KKn"], w=["T1b"])

                    yield
                    ps, key = headsum(T1b, "T1b", ONESB, "ONESB")
                    DVE(lambda e, ps=ps: e.tensor_scalar(out=d("T2"), in0=p3(ps), scalar1=1e-24, scalar2=None, op0=ALU.max),
                        r=[key], w=["T2"])
                    yield
                    rsqrt_inplace("T2")
                    yield
                    DVE(lambda e: e.tensor_tensor(out=d("KKn"), in0=d("KKn"), in1=d("T2"), op=ALU.mult), r=["KKn", "T2"], w=["KKn"])
                    DVE(lambda e: e.scalar_tensor_tensor(out=d("At"), in0=d("KKn"), scalar=-1.0, in1=d("Wexc"), op0=ALU.mult, op1=ALU.mult),
                        r=["KKn", "Wexc"], w=["At"])
                    DVE(lambda e: e.tensor_tensor(out=d("Bt"), in0=d("KKn"), in1=d("Aa"), op=ALU.mult), r=["KKn", "Aa"], w=["Bt"])
                    DVE(lambda e: e.tensor_tensor(out=d("Bt"), in0=d("Bt"), in1=d("Winv"), op=ALU.mult), r=["Bt", "Winv"], w=["Bt"])


                    flags[("prep", ti)] = True
                    yield
                    def amat(lh, rh, mask, dstn):
                        pa, ka = ps1()
                        for h in range(8):
                            PE(lambda e, h=h, pa=pa: e.matmul(pa[0:Ck, h * 64:h * 64 + Ck], lhsT=D[lh][:, h, 0:T], rhs=D[rh][:, h, 0:T],
                                                              start=True, stop=True), r=[lh, rh], w=[ka])
                        DVE(lambda e, pa=pa: e.tensor_tensor(out=v3(Cc[dstn], Ck, Ck), in0=v3(pa, Ck, Ck),
                                                             in1=m3(mask, Ck, Ck), op=ALU.mult), r=[ka, ("CT", mask)], w=[dstn])

                    amat("Bt", "At", "msu", "X0")
                    yield
                    amat("At", "Bt", "msl", "Xt0")
                    yield
                    DVE(lambda e: e.tensor_tensor(out=v3(Cc["R0"], Ck, Ck), in0=v3(Cc["X0"], Ck, Ck), in1=m3("idt", Ck, Ck), op=ALU.add),
                        r=["X0", ("CT", "idt")], w=["R0"])
                    ACT(lambda e: e.copy(out=v3(Rtb[0], Ck, Ck), in_=v3(Cc["R0"], Ck, Ck)), r=["R0"], w=[("Rtb", 0)])
                    nl = 5 if Ck == 64 else 3
                    cur = 0

                    def sq(lt, lk, rt, rk):
                        pa, ka = ps1()
                        for h in range(8):
                            PE(lambda e, h=h, pa=pa: e.matmul(pa[0:Ck, h * 64:h * 64 + Ck], lhsT=lt[0:Ck, h * 64:h * 64 + Ck],
                                                              rhs=rt[0:Ck, h * 64:h * 64 + Ck], start=True, stop=True),
                               r=[lk, rk], w=[ka])
                        return pa, ka

                    for lvl in range(1, nl + 1):
                        nx = 1 - cur
                        if lvl == 1:
                            Xo, Xok, Xto, Xtok = Cc["X0"], "X0", Cc["Xt0"], "Xt0"
                        else:
                            Xo, Xok, Xto, Xtok = Xb[cur], ("Xb", cur), Xtb[cur], ("Xtb", cur)
                        pa, ka = sq(Xo, Xok, Xto, Xtok)
                        ACT(lambda e, pa=pa, nx=nx: e.copy(out=v3(Xtb[nx], Ck, Ck), in_=v3(pa, Ck, Ck)), r=[ka], w=[("Xtb", nx)])
                        yield
                        if lvl < nl:
                            pa, ka = sq(Xto, Xtok, Xo, Xok)
                            ACT(lambda e, pa=pa, nx=nx: e.copy(out=v3(Xb[nx], Ck, Ck), in_=v3(pa, Ck, Ck)), r=[ka], w=[("Xb", nx)])
                            yield
                        R, Rn = f"R{cur}", f"R{nx}"
                        pa, ka = sq(Xtb[nx], ("Xtb", nx), Rtb[cur], ("Rtb", cur))
                        DVE(lambda e, pa=pa, R=R, Rn=Rn: e.tensor_tensor(out=v3(Cc[Rn], Ck, Ck), in0=v3(pa, Ck, Ck),
                                                                         in1=v3(Cc[R], Ck, Ck), op=ALU.add), r=[ka, R], w=[Rn])
                        if lvl < nl:
                            ACT(lambda e, Rn=Rn, nx=nx: e.copy(out=v3(Rtb[nx], Ck, Ck), in_=v3(Cc[Rn], Ck, Ck)), r=[Rn], w=[("Rtb", nx)])
                        yield
                        cur = nx
                    Rfin = f"R{cur}"

                    yield
                def genP2b(ti):
                    r0, T = tiles[ti]
                    b = ti % 2
                    par = ti % 2
                    ht = HT[b]
                    ym = YMIX[b]
                    ymk = ("YMIX", b)
                    is_meta = (ti == 0)
                    Ck = T
                    gc = ti - 1
                    D = Dp[par]
                    Cc = Ccp[par]
                    PE, ACT, DVE, ST = (trw(f_, par) for f_ in (PE0, ACT0, DVE0, ST0))
                    QT = QTs[par]
                    PSP = "m"
                    yield
                    def ppb(col):
                        return PPt[:, col:col + 8].unsqueeze(2).to_broadcast([64, 8, T])

                    def d(n):
                        return D[n][:, :, 0:T]

                    def p3(ps, rows=64):
                        return ps.rearrange("p (h t) -> p h t", t=64)[0:rows, :, 0:T]

                    def headsum(srct, skey, lhs, lkey):
                        ps, key = ps1(PSP)
                        if T == 64:
                            PE(lambda e, ps=ps: e.matmul(ps[0:64, :], lhsT=lhs[:, :], rhs=srct[:].rearrange("p h t -> p (h t)"), start=True, stop=True),
                               r=[skey, lkey], w=[key])
                        else:
                            for h in range(8):
                                PE(lambda e, ps=ps, h=h: e.matmul(ps[0:64, h * 64:h * 64 + T], lhsT=lhs[:, :], rhs=srct[:, h, 0:T], start=True, stop=True),
                                   r=[skey, lkey], w=[key])
                        return ps, key

                    def rsqrt_inplace(n):
                        ACT(lambda e: e.activation(out=d(n), in_=d(n), func=AF.Ln), r=[n], w=[n])
                        ACT(lambda e: e.activation(out=d(n), in_=d(n), func=AF.Exp, scale=-0.5), r=[n], w=[n])

                    def lora(Wt, wkey, xin, xkey, rows):
                        ps, key = ps1()
                        for h in range(8):
                            PE(lambda e, h=h, ps=ps: e.matmul(ps[0:64, h * 64:h * 64 + T], lhsT=Wt[0:rows, h * 64:(h + 1) * 64],
                                                              rhs=xin[0:rows, 0:T], start=True, stop=True), r=[wkey, xkey], w=[key])
                        return ps, key

                    def waitf(name):
                        while not flags.get((name, ti)) and not getattr(S, "dry", False):
                            yield

                    def tr_tm(src, dstn):
                        pt, kt = ps1()
                        for h in range(8):
                            PE(lambda e, h=h, src=src, pt=pt: e.transpose(pt[0:Ck, h * 64:(h + 1) * 64], D[src][:, h, 0:T], IDF[0:64, 0:64]),
                               r=[src, "IDF"], w=[kt])
                        ACT(lambda e, pt=pt, dstn=dstn: e.copy(out=Cc[dstn][0:Ck, :], in_=pt[0:Ck, :]), r=[kt], w=[dstn])

                    def amat(lh, rh, mask, dstn):
                        pa, ka = ps1()
                        for h in range(8):
                            PE(lambda e, h=h, pa=pa: e.matmul(pa[0:Ck, h * 64:h * 64 + Ck], lhsT=D[lh][:, h, 0:T], rhs=D[rh][:, h, 0:T],
                                                              start=True, stop=True), r=[lh, rh], w=[ka])
                        DVE(lambda e, pa=pa: e.tensor_tensor(out=v3(Cc[dstn], Ck, Ck), in0=v3(pa, Ck, Ck),
                                                             in1=m3(mask, Ck, Ck), op=ALU.mult), r=[ka, ("CT", mask)], w=[dstn])

                    yield from waitf("lerp")
                    ps, key = lora(G2, "G2", SGb, "SGb", 96)
                    ACT(lambda e, ps=ps: e.copy(out=d("Gg"), in_=p3(ps)), r=[key], w=["Gg"])
                    yield
                    tr_tm("Vv", "VTMt")
                    yield
                    yield from waitf("decay")
                    DVE(lambda e: e.tensor_tensor(out=d("Rt"), in0=d("Rr"), in1=d("Winc"), op=ALU.mult), r=["Rr", "Winc"], w=["Rt"])
                    yield
                    yield from waitf("a")
                    DVE(lambda e: e.scalar_tensor_tensor(out=d("T3"), in0=d("Aa"), scalar=-1.0, in1=ppb(KAc), op0=ALU.add, op1=ALU.mult),
                        r=["Aa", "PP"], w=["T3"])
                    DVE(lambda e: e.scalar_tensor_tensor(out=d("Kk"), in0=d("T3"), scalar=1.0, in1=d("Kz"), op0=ALU.add, op1=ALU.mult),
                        r=["T3", "Kz"], w=["Kk"])
                    DVE(lambda e: e.tensor_tensor(out=d("Kt"), in0=d("Kk"), in1=d("Winv"), op=ALU.mult), r=["Kk", "Winv"], w=["Kt"])
                    DVE(lambda e: e.tensor_tensor(out=d("T3"), in0=d("Rr"), in1=d("Kk"), op=ALU.mult), r=["Rr", "Kk"], w=["T3"])
                    DVE(lambda e: e.tensor_tensor(out=T3b[:, :, 0:T], in0=d("T3"), in1=ppb(RKc), op=ALU.mult), r=["T3", "PP"], w=["T3b"])
                    yield
                    tr_tm("Kt", "KTMt")
                    yield
                    ps, key = headsum(T3b, "T3b", ONESB, "ONESB")
                    DVE(lambda e, ps=ps: e.tensor_tensor(out=d("BON"), in0=p3(ps), in1=d("Vv"), op=ALU.mult), r=[key, "Vv"], w=["BON"])
                    yield
                    amat("Kt", "Rt", "mui", "ARK")
                    yield
                    yield from waitf("prep")
                    tr_tm("Bt", "BTMt")
                    yield
                    amat("Kt", "At", "msu", "AAK")
                    yield
                    amat("Bt", "Rt", "mui", "ARB")
                    yield
                def genAtt(ti):
                    r0, T = tiles[ti]
                    b = ti % 2
                    par = ti % 2
                    ht = HT[b]
                    ym = YMIX[b]
                    ymk = ("YMIX", b)
                    is_meta = (ti == 0)
                    Ck = T
                    gc = ti - 1
                    D = Dp[par]
                    Cc = Ccp[par]
                    PE, ACT, DVE, ST = (trw(f_, par) for f_ in (PE0, ACT0, DVE0, ST0))
                    QT = QTs[par]
                    PSP = "m"
                    yield
                    def ppb(col):
                        return PPt[:, col:col + 8].unsqueeze(2).to_broadcast([64, 8, T])

                    def d(n):
                        return D[n][:, :, 0:T]

                    def p3(ps, rows=64):
                        return ps.rearrange("p (h t) -> p h t", t=64)[0:rows, :, 0:T]

                    def headsum(srct, skey, lhs, lkey):
                        ps, key = ps1(PSP)
                        if T == 64:
                            PE(lambda e, ps=ps: e.matmul(ps[0:64, :], lhsT=lhs[:, :], rhs=srct[:].rearrange("p h t -> p (h t)"), start=True, stop=True),
                               r=[skey, lkey], w=[key])
                        else:
                            for h in range(8):
                                PE(lambda e, ps=ps, h=h: e.matmul(ps[0:64, h * 64:h * 64 + T], lhsT=lhs[:, :], rhs=srct[:, h, 0:T], start=True, stop=True),
                                   r=[skey, lkey], w=[key])
                        return ps, key

                    def rsqrt_inplace(n):
                        ACT(lambda e: e.activation(out=d(n), in_=d(n), func=AF.Ln), r=[n], w=[n])
                        ACT(lambda e: e.activation(out=d(n), in_=d(n), func=AF.Exp, scale=-0.5), r=[n], w=[n])

                    if is_meta:
                        blocks = [(lambda kv: KTM[:, kv, :], "KTMeta", lambda kv: VM[0:16, kv * 64:(kv + 1) * 64], "VMeta", 16,
                                   CT["almm"][:].rearrange("p (h t) -> p h t", t=16), ("CT", "almm"))]
                    else:
                        DVE(lambda e: e.scalar_tensor_tensor(out=ALMC[:, :], in0=CT["almd"][:, :], scalar=float(gc), in1=CT["alm0"][:, :],
                                                             op0=ALU.mult, op1=ALU.add), r=[("CT", "almd"), ("CT", "alm0")], w=["ALMC"])
                        blocks = []
                        for back, aln in ((2, "al2"), (1, "al1"), (0, "al0")):
                            g2_ = gc - back
                            if g2_ < 0:
                                continue
                            sl_ = g2_ % 4
                            blocks.append((lambda kv, sl_=sl_: KTR[:, kv, sl_, :], ("KTR", sl_),
                                           lambda kv, sl_=sl_: VR[:, sl_, kv * 64:(kv + 1) * 64], ("VR", sl_), 64,
                                           CT[aln][:].rearrange("p (h t) -> p h t", t=64), ("CT", aln)))
                        blocks.append((lambda kv: KTM[:, kv, :], "KTMeta", lambda kv: VM[0:16, kv * 64:(kv + 1) * 64], "VMeta", 16,
                                       ALMC[:].rearrange("p (h t) -> p h t", t=64), "ALMC"))
                    nq = Ck
                    pts = []
                    for bi, (kf, kkey, vf, vkey, nk, bias3, bkey) in enumerate(blocks):
                        pa, ka = ps1()
                        for h in range(8):
                            PE(lambda e, h=h, pa=pa, kf=kf, nk=nk: e.matmul(pa[0:nk, h * 64:h * 64 + nq], lhsT=kf(h // 4), rhs=QT[:, h, 0:T],
                                                                            start=True, stop=True), r=[kkey, "QT"], w=[ka])
                        sbt_ = SBT[bi % 2]
                        DVE(lambda e, pa=pa, nk=nk, bias3=bias3, sbt_=sbt_: e.scalar_tensor_tensor(
                            out=v3(sbt_, nk, nq), in0=v3(pa, nk, nq), scalar=0.125,
                            in1=bias3[0:nk, :, 0:nq], op0=ALU.mult, op1=ALU.add), r=[ka, bkey], w=[("SBT", 0)])
                        ACT(lambda e, nk=nk, sbt_=sbt_, bi=bi: e.activation(out=v3(PT[bi], nk, nq), in_=v3(sbt_, nk, nq), func=AF.Exp),
                            r=[("SBT", 0)], w=[("PT", bi)])
                        pts.append((bi, nk, vf, vkey))
                        yield
                    pd, kd = ps1()
                    po, ko = ps1()
                    nb_ = len(pts)
                    for i_, (bi, nk, vf, vkey) in enumerate(pts):
                        if nq == 64:
                            PE(lambda e, bi=bi, nk=nk, pd=pd, i_=i_: e.matmul(
                                pd[0:64, :], lhsT=ONESB[0:nk, :], rhs=PT[bi][0:nk, :],
                                start=(i_ == 0), stop=(i_ == nb_ - 1)), r=[("PT", bi), "ONESB"], w=[kd])
                        else:
                            for h in range(8):
                                PE(lambda e, bi=bi, nk=nk, pd=pd, i_=i_, h=h: e.matmul(
                                    pd[0:64, h * 64:h * 64 + nq], lhsT=ONESB[0:nk, :], rhs=PT[bi][0:nk, h * 64:h * 64 + nq],
                                    start=(i_ == 0), stop=(i_ == nb_ - 1)), r=[("PT", bi), "ONESB"], w=[kd])
                    for kv in range(2):
                        for i_, (bi, nk, vf, vkey) in enumerate(pts):
                            if nq == 64:
                                PE(lambda e, bi=bi, nk=nk, po=po, i_=i_, kv=kv, vf=vf: e.matmul(
                                    po[0:64, kv * 256:(kv + 1) * 256], lhsT=vf(kv),
                                    rhs=PT[bi][0:nk, kv * 256:(kv + 1) * 256], start=(i_ == 0), stop=(i_ == nb_ - 1)),
                                   r=[("PT", bi), vkey], w=[ko])
                            else:
                                for hh in range(4):
                                    h = kv * 4 + hh
                                    PE(lambda e, bi=bi, nk=nk, po=po, i_=i_, kv=kv, vf=vf, h=h: e.matmul(
                                        po[0:64, h * 64:h * 64 + nq], lhsT=vf(kv),
                                        rhs=PT[bi][0:nk, h * 64:h * 64 + nq], start=(i_ == 0), stop=(i_ == nb_ - 1)),
                                       r=[("PT", bi), vkey], w=[ko])
                    DVE(lambda e, pd=pd: e.tensor_tensor(out=v3(DEN, 64, nq), in0=v3(pd, 64, nq),
                                                         in1=ESK[:, :].unsqueeze(2).to_broadcast([64, 8, nq]), op=ALU.add),
                        r=[kd, "ESK"], w=["DEN"])
                    ACT(lambda e: e.activation(out=v3(DEN, 64, nq), in_=v3(DEN, 64, nq), func=AF.Ln), r=["DEN"], w=["DEN"])
                    ACT(lambda e: e.activation(out=v3(DEN, 64, nq), in_=v3(DEN, 64, nq), func=AF.Exp, scale=-1.0), r=["DEN"], w=["DEN"])
                    DVE(lambda e, po=po: e.tensor_tensor(out=ym[:, 8:16, 0:T], in0=v3(po, 64, nq),
                                                         in1=v3(DEN, 64, nq), op=ALU.mult), r=[ko, "DEN"], w=[ymk])


                    yield
                def genS(ti):
                    r0, T = tiles[ti]
                    b = ti % 2
                    par = ti % 2
                    ht = HT[b]
                    ym = YMIX[b]
                    ymk = ("YMIX", b)
                    is_meta = (ti == 0)
                    Ck = T
                    gc = ti - 1
                    D = Dp[par]
                    Cc = Ccp[par]
                    PE, ACT, DVE, ST = (trw(f_, par) for f_ in (PE0, ACT0, DVE0, ST0))
                    QT = QTs[par]
                    PSP = "s"
                    Rfin = "R1"
                    def ppb(col):
                        return PPt[:, col:col + 8].unsqueeze(2).to_broadcast([64, 8, T])

                    def d(n):
                        return D[n][:, :, 0:T]

                    def p3(ps, rows=64):
                        return ps.rearrange("p (h t) -> p h t", t=64)[0:rows, :, 0:T]

                    def headsum(srct, skey, lhs, lkey):
                        ps, key = ps1(PSP)
                        if T == 64:
                            PE(lambda e, ps=ps: e.matmul(ps[0:64, :], lhsT=lhs[:, :], rhs=srct[:].rearrange("p h t -> p (h t)"), start=True, stop=True),
                               r=[skey, lkey], w=[key])
                        else:
                            for h in range(8):
                                PE(lambda e, ps=ps, h=h: e.matmul(ps[0:64, h * 64:h * 64 + T], lhsT=lhs[:, :], rhs=srct[:, h, 0:T], start=True, stop=True),
                                   r=[skey, lkey], w=[key])
                        return ps, key

                    def rsqrt_inplace(n):
                        ACT(lambda e: e.activation(out=d(n), in_=d(n), func=AF.Ln), r=[n], w=[n])
                        ACT(lambda e: e.activation(out=d(n), in_=d(n), func=AF.Exp, scale=-0.5), r=[n], w=[n])

                    pa, ka = ps1("s")
                    for h in range(8):
                        PE(lambda e, h=h, pa=pa: e.matmul(pa[0:Ck, h * 64:(h + 1) * 64], lhsT=D["At"][:, h, 0:T], rhs=STT[:, h, :],
                                                          start=True, stop=False), r=["At", "STATE"], w=[ka])
                        PE(lambda e, h=h, pa=pa: e.matmul(pa[0:Ck, h * 64:(h + 1) * 64], lhsT=Cc["AAK"][0:Ck, h * 64:h * 64 + Ck],
                                                          rhs=Cc["VTMt"][0:Ck, h * 64:(h + 1) * 64], start=False, stop=True),
                           r=["AAK", "VTMt"], w=[ka])
                    ACT(lambda e, pa=pa: e.copy(out=Cc["RHS0"][0:Ck, :], in_=pa[0:Ck, :]), r=[ka], w=["RHS0"])
                    yield
                    pa, ka = ps1("s")
                    for h in range(8):
                        PE(lambda e, h=h, pa=pa: e.matmul(pa[0:Ck, h * 64:(h + 1) * 64], lhsT=Cc[Rfin][0:Ck, h * 64:h * 64 + Ck],
                                                          rhs=Cc["RHS0"][0:Ck, h * 64:(h + 1) * 64], start=True, stop=True),
                           r=[Rfin, "RHS0"], w=[ka])
                    DVE(lambda e, pa=pa: e.tensor_copy(out=Cc["UU"][0:Ck, :], in_=pa[0:Ck, :]), r=[ka], w=["UU"])
                    yield
                    py, ky = ps1("s")
                    for h in range(8):
                        PE(lambda e, h=h, py=py: e.matmul(py[0:64, h * 64:h * 64 + Ck], lhsT=STT[:, h, :], rhs=D["Rt"][:, h, 0:T],
                                                          start=True, stop=False), r=["STATE", "Rt"], w=[ky])
                        PE(lambda e, h=h, py=py: e.matmul(py[0:64, h * 64:h * 64 + Ck], lhsT=Cc["UU"][0:Ck, h * 64:(h + 1) * 64],
                                                          rhs=Cc["ARB"][0:Ck, h * 64:h * 64 + Ck], start=False, stop=False),
                           r=["UU", "ARB"], w=[ky])
                        PE(lambda e, h=h, py=py: e.matmul(py[0:64, h * 64:h * 64 + Ck], lhsT=Cc["VTMt"][0:Ck, h * 64:(h + 1) * 64],
                                                          rhs=Cc["ARK"][0:Ck, h * 64:h * 64 + Ck], start=False, stop=True),
                           r=["VTMt", "ARK"], w=[ky])
                    ACT(lambda e, py=py: e.copy(out=d("YS"), in_=p3(py)), r=[ky], w=["YS"])
                    ACT(lambda e, py=py: e.copy(out=YSb[:, :, 0:T], in_=p3(py)), r=[ky], w=["YSb"])
                    yield
                    pst, kst = ps1("s")
                    for h in range(8):
                        PE(lambda e, h=h, pst=pst: e.matmul(pst[0:64, h * 64:(h + 1) * 64], lhsT=Cc["KTMt"][0:Ck, h * 64:(h + 1) * 64],
                                                            rhs=Cc["VTMt"][0:Ck, h * 64:(h + 1) * 64], start=True, stop=False),
                           r=["KTMt", "VTMt"], w=[kst])
                        PE(lambda e, h=h, pst=pst: e.matmul(pst[0:64, h * 64:(h + 1) * 64], lhsT=Cc["BTMt"][0:Ck, h * 64:(h + 1) * 64],
                                                            rhs=Cc["UU"][0:Ck, h * 64:(h + 1) * 64], start=False, stop=True),
                           r=["BTMt", "UU"], w=[kst])
                    DVE(lambda e, pst=pst: e.tensor_tensor(out=Cc["RHS0"][:, :], in0=pst[0:64, :], in1=STT[:].rearrange("p h v -> p (h v)"),
                                                           op=ALU.add), r=[kst, "STATE"], w=["RHS0"])
                    DVE(lambda e: e.tensor_tensor(out=STT[:, :, :], in0=Cc["RHS0"][:].rearrange("p (h v) -> p h v", v=64),
                                                  in1=D["Winc"][:, :, T - 1:T].to_broadcast([64, 8, 64]), op=ALU.mult),
                        r=["RHS0", "Winc"], w=["STATE"])


                    yield
                    yield
                    ps, key = headsum(YSb, "YSb", ONESDB, "ONESDB")
                    DVE(lambda e, ps=ps: e.tensor_tensor(out=d("S1"), in0=d("YS"), in1=p3(ps), op=ALU.subtract), r=[key, "YS"], w=["S1"])
                    ACT(lambda e: e.activation(out=S1b[:, :, 0:T], in_=d("S1"), func=AF.Square), r=["S1"], w=["S1b"])
                    yield
                    ps, key = headsum(S1b, "S1b", ONESDB, "ONESDB")
                    DVE(lambda e, ps=ps: e.tensor_scalar(out=d("S2"), in0=p3(ps), scalar1=64e-5, scalar2=None, op0=ALU.add), r=[key], w=["S2"])
                    yield
                    rsqrt_inplace("S2")
                    yield
                    DVE(lambda e: e.tensor_tensor(out=d("S1"), in0=d("S1"), in1=d("S2"), op=ALU.mult), r=["S1", "S2"], w=["S1"])
                    DVE(lambda e: e.tensor_tensor(out=d("S1"), in0=d("S1"), in1=ppb(LNWc), op=ALU.mult), r=["S1", "PP"], w=["S1"])
                    DVE(lambda e: e.tensor_tensor(out=d("S1"), in0=d("S1"), in1=ppb(LNBc), op=ALU.add), r=["S1", "PP"], w=["S1"])
                    DVE(lambda e: e.tensor_tensor(out=d("S1"), in0=d("S1"), in1=d("BON"), op=ALU.add), r=["S1", "BON"], w=["S1"])
                    DVE(lambda e: e.tensor_tensor(out=ym[:, 0:8, 0:T], in0=d("S1"), in1=d("Gg"), op=ALU.mult), r=["S1", "Gg"], w=[ymk])
                    ST(lambda e, r0=r0, T=T, ym=ym: e.dma_start(out=ym_d[:, r0:r0 + T].rearrange("(g p) t -> p g t", p=64), in_=ym[:, :, 0:T]),
                       r=[ymk], w=[("ymd", ti)])
                    yield
                def drive_dyn(gens):
                    live = [[g, 0.0] for g, _ in gens]
                    while live:
                        tpe = S.tfree["tensor"] + SLACK
                        cands = [it for it in live if it[1] + S.lat <= tpe]
                        item = cands[0] if cands else min(live, key=lambda it: it[1])
                        S.step_fin = 0.0
                        try:
                            next(item[0])
                            if S.step_fin > 0.0:
                                item[1] = S.step_fin
                            else:
                                item[1] += 0.3
                        except StopIteration:
                            live.remove(item)

                def count_steps(genf, ti):
                    S.dry = True
                    sv = dict(rot)
                    fl = dict(flags)
                    n = 0
                    for _ in genf(ti):
                        n += 1
                    rot.update(sv)
                    flags.clear()
                    flags.update(fl)
                    S.dry = False
                    return n + 1

                def drive_frac(gens):
                    live = [[g, 0, float(tot)] for g, tot in gens]
                    while live:
                        item = min(live, key=lambda it: (it[1] + BIAS.get(id(it[0]), 0.0)) / it[2])
                        try:
                            next(item[0])
                            item[1] += 1
                        except StopIteration:
                            live.remove(item)

                BIAS = {}

                def drive(gens):
                    live = [[g, n] for g, n in gens]
                    while live:
                        for item in list(live):
                            g, n = item
                            for _ in range(n):
                                try:
                                    next(g)
                                except StopIteration:
                                    live.remove(item)
                                    break

                rot["pipe"] = True
                load_tile(0)
                nt_ = len(tiles)
                drive([(genP1(0), 1)])
                drive([(genP2(0), 1), (genP2b(0), 1), (genAtt(0), 1)] + ([(genP1(1), 1)] if nt_ > 1 else []))
                for ti in range(nt_):
                    if FRAC:
                        gl = [(genS(ti), count_steps(genS, ti) * FW[0])]
                        if ti + 1 < nt_:
                            gl += [(genP2(ti + 1), count_steps(genP2, ti + 1) * FW[1]), (genAtt(ti + 1), count_steps(genAtt, ti + 1) * FW[2])]
                        if ti + 2 < nt_:
                            gl += [(genP1(ti + 2), count_steps(genP1, ti + 2) * FW[3])]
                        drive_frac(gl)
                        continue
                    gl = [(genS(ti), WGT[0])]
                    if ti + 1 < nt_:
                        gl += [(genP2(ti + 1), WGT[1]), (genP2b(ti + 1), WGT[4]), (genAtt(ti + 1), WGT[2])]
                    if ti + 2 < nt_:
                        gl += [(genP1(ti + 2), WGT[3])]
                    (drive_dyn if DYN else drive)(gl)
                rot["pipe"] = False
                S.barrier()

        def outproj_phase():
            with ExitStack() as es:
                WOUT = sbt(es, "WOUT", [128, 8, DM], BF16)
                HT = [sbt(es, f"HTo{b}", [128, 2, DM]) for b in range(2)]
                YT = [sbt(es, f"YTo{b}", [128, 8, 256], BF16) for b in range(2)]
                for c in range(8):
                    LDC(lambda e, c=c: e.dma_start(out=WOUT[:, c, :], in_=wout_d[c * 128:(c + 1) * 128, :]), w=[("WOUT", c)])
                tiles = [(0, NM)] + [(NM + 256 * i, 256) for i in range(NR // 256)]

                def load_tile(ti):
                    r0, T = tiles[ti]
                    P = min(T, 128)
                    nsub = (T + 127) // 128
                    b = ti % 2
                    LD(lambda e: e.dma_start(out=HT[b][:P, 0:nsub, :], in_=h1_d[r0:r0 + T, :].rearrange("(s p) d -> p s d", p=P)),
                       w=[("HTo", b)])
                    (LD2 if OQ >= 1 else LD)(lambda e: e.dma_start(out=YT[b][:, :, 0:T], in_=ym_d[:, r0:r0 + T].rearrange("(c p) t -> p c t", p=128)),
                       w=[("YTo", b)])

                load_tile(0)
                for ti, (r0, T) in enumerate(tiles):
                    P = min(T, 128)
                    nsub = (T + 127) // 128
                    b = ti % 2
                    if ti + 1 < len(tiles):
                        load_tile(ti + 1)
                    for s in range(nsub):
                        for hf in range(2):
                            po, ko = ps1()
                            for c in range(8):
                                PE(lambda e, c=c, s=s, hf=hf, po=po: e.matmul(po[0:P, :], lhsT=YT[b][:, c, s * 128:s * 128 + P],
                                                                              rhs=WOUT[:, c, hf * 512:(hf + 1) * 512],
                                                                              start=(c == 0), stop=(c == 7)),
                                   r=[("YTo", b), ("WOUT", c)], w=[ko])
                            DVE(lambda e, s=s, hf=hf, po=po: e.tensor_tensor(out=HT[b][0:P, s, hf * 512:(hf + 1) * 512], in0=po[0:P, :],
                                                                             in1=HT[b][0:P, s, hf * 512:(hf + 1) * 512], op=ALU.add),
                                r=[ko, ("HTo", b)], w=[("HTo", b)])
                    (ST2 if OQ >= 2 else ST)(lambda e, r0=r0, T=T, P=P, nsub=nsub, b=b: e.dma_start(
                        out=h2_d[r0:r0 + T, :].rearrange("(s p) d -> p s d", p=P), in_=HT[b][:P, 0:nsub, :]), r=[("HTo", b)], w=[("h2", ti)])
                S.barrier()

        tilesA = [(0, NM)] + [(NM + 256 * i, 256) for i in range(NR // 256)]
        tilesC = [(NM + 256 * i, 256) for i in range(NR // 256)]

        def srcA(r0, T):
            return meta_d[0:NM, :] if r0 == 0 else x_d[r0 - NM:r0 - NM + T, :]

        if "A" in phases:
            ffn_phase("A", tilesA, srcA, lambda r0, T: h1_d[r0:r0 + T, :], 0, 0, False)
        if "B" in phases:
            mix_phase()
        if "O" in phases:
            outproj_phase()
        if "C" in phases:
            ffn_phase("C", tilesC, lambda r0, T: h2_d[r0:r0 + T, :], lambda r0, T: out_d[r0 - NM:r0 - NM + T, :], 1, 2, True)
        if dbg:
            dh1 = nc.dram_tensor("dbg_h1", [LT, DM], F32, kind="ExternalOutput").ap()
            dh2 = nc.dram_tensor("dbg_h2", [LT, DM], F32, kind="ExternalOutput").ap()
            dym = nc.dram_tensor("dbg_ym", [DM, LT], BF16, kind="ExternalOutput").ap()
            ST(lambda e: e.dma_start(out=dh1, in_=h1_d), w=["dbg1"])
            ST(lambda e: e.dma_start(out=dh2, in_=h2_d), w=["dbg2"])
            ST(lambda e: e.dma_start(out=dym, in_=ym_d), w=["dbg3"])
        S.barrier(["sync"])
    return nc, S


def host_inputs(inp, b, NR=4096):
    f = lambda a: np.ascontiguousarray(np.asarray(a, dtype=np.float32))
    m = {}
    m["x"] = f(inp["x"][b][:NR])
    m["meta"] = f(inp["meta_tokens"])
    m["gvec"] = f(np.stack([inp["ffn1_norm"][0], inp["mix_norm"][0], inp["ffn2_norm"][0], inp["final_norm"]], 0))
    m["wg1"] = f(inp["ffn1_w_gate"][0]); m["wu1"] = f(inp["ffn1_w_up"][0]); m["wd1"] = f(inp["ffn1_w_down"][0])
    m["wg2"] = f(inp["ffn2_w_gate"][0]); m["wu2"] = f(inp["ffn2_w_up"][0]); m["wd2"] = f(inp["ffn2_w_down"][0])
    m["w_in"] = f(inp["w_in"][0]); m["w_out"] = f(inp["w_out"][0])
    m["w2"] = f(inp["rwkv_w2"][0]); m["a2"] = f(inp["rwkv_a2"][0]); m["g2"] = f(inp["rwkv_g2"][0])
    mu = np.asarray(inp["rwkv_mu"][0], np.float32)
    hp = lambda v: np.asarray(v, np.float32).reshape(-1, 64).T
    b_attn = np.asarray(inp["b_attn"][0], np.float32)
    cols = [hp(mu[0:512]), hp(mu[512:1024]), hp(mu[1024:1536]), hp(inp["rwkv_w0"][0]), hp(inp["rwkv_a0"][0]),
            hp(inp["rwkv_k_k"][0]), hp(inp["rwkv_k_a"][0]), hp(np.asarray(inp["rwkv_r_k"][0]).reshape(-1)),
            hp(inp["rwkv_ln_w"][0]), hp(inp["rwkv_ln_b"][0]), hp(b_attn[0:512]), hp(b_attn[512:640])]
    m["pp"] = f(np.concatenate(cols, axis=1))
    pl = np.zeros((96, 3), np.float32)
    pl[:32, 0] = mu[1536:1568]; pl[:32, 1] = mu[1568:1600]; pl[:96, 2] = mu[1600:1696]
    m["pl"] = pl
    m["bv"] = f(b_attn[640:768].reshape(1, 128))
    m["sinks"] = f(np.asarray(inp["attn_sinks"][0]).reshape(1, 8))
    return m


_CACHE = {}


def kernel(**inputs):
    B = inputs["x"].shape[0]
    NR = inputs["x"].shape[1]
    if NR not in _CACHE:
        _CACHE[NR] = build_nc(NR)[0]
    nc = _CACHE[NR]
    consts = {"c_" + k: v for k, v in make_consts().items()}
    in_maps = []
    for b in range(B):
        m = host_inputs(inputs, b, NR)
        m.update(consts)
        in_maps.append(m)
    res = run_bass_kernel_spmd(nc, in_maps, core_ids=list(range(B)))
    out = np.stack([np.asarray(r["out"], dtype=np.float32) for r in res.results], axis=0)
    return out
```
